# Optimizing a Trainium2 kernel written in Bass

```python
import math
import jax, jax.numpy as jnp
from jax import lax
import numpy as np

D_MODEL = 1024
BATCH = 8
SEQ = 2048
DEPTH = 2

GRID_W = 64
Q_BLOCK = 128
HEAD_DIM = 64
ROPE_THETA = 10000.0
EPS = 1e-6
NEG_INF = -1e30

A_HEADS = 8
A_KV_HEADS = 2

B_HEADS = 8
B_BRANCHES = ((128, 1), (512, 4), (2048, 16))
NUM_BUCKETS = 32
REL_MAX_DISTANCE = 1024

C_HEADS = 16
C_Q_RANK = 256
C_KV_RANK = 128
C_NOPE_DIM = 64
C_ROPE_DIM = 32
C_V_DIM = 64

D_FF = 4 * D_MODEL

A_Q_W = A_HEADS * HEAD_DIM
A_KV_W = A_KV_HEADS * HEAD_DIM
B_W = B_HEADS * HEAD_DIM
AB_IN_W = A_Q_W + 2 * A_KV_W + 3 * B_W
AB_OUT_IN = A_Q_W + B_W
C_DOWN_W = C_Q_RANK + C_KV_RANK + C_ROPE_DIM
C_QK_DIM = C_NOPE_DIM + C_ROPE_DIM
C_OUT_IN = C_HEADS * C_V_DIM
N_EVEN = (DEPTH + 1) // 2
N_ODD = DEPTH // 2

kernel_name = "hybrid_gqa_dilated_mla_encoder"


def rms_norm(x, g):
    xf = x.astype(jnp.float32)
    y = xf * lax.rsqrt(jnp.mean(xf * xf, axis=-1, keepdims=True) + EPS)
    return (y * g.astype(jnp.float32)).astype(x.dtype)


def rope_angles(pos, dim):
    inv_freq = ROPE_THETA ** (-jnp.arange(0, dim, 2, dtype=jnp.float32) / dim)
    return pos.astype(jnp.float32)[:, None] * inv_freq[None, :]


def apply_rope(x, cos, sin):
    xf = x.astype(jnp.float32)
    half = x.shape[-1] // 2
    x1, x2 = xf[..., :half], xf[..., half:]
    out = jnp.concatenate([x1 * cos - x2 * sin, x2 * cos + x1 * sin], axis=-1)
    return out.astype(x.dtype)


def unblock(o):
    nb, b, q = o.shape[:3]
    return jnp.moveaxis(o, 0, 1).reshape((b, nb * q) + o.shape[3:])


def t5_bucket(rel):
    nb = NUM_BUCKETS // 2
    max_exact = nb // 2
    base = jnp.where(rel > 0, nb, 0)
    n = jnp.abs(rel)
    nf = jnp.maximum(n, 1).astype(jnp.float32)
    large = max_exact + (jnp.log(nf / max_exact) / math.log(REL_MAX_DISTANCE / max_exact)
                         * (nb - max_exact)).astype(jnp.int32)
    large = jnp.minimum(large, nb - 1)
    return base + jnp.where(n < max_exact, n, large)


def gqa_attention(q, k, v):
    b, s = q.shape[:2]
    rep = A_HEADS // A_KV_HEADS
    q = q.reshape(b, s, A_KV_HEADS, rep, HEAD_DIM)
    scale = HEAD_DIM ** -0.5

    def block(i):
        qb = lax.dynamic_slice_in_dim(q, i * Q_BLOCK, Q_BLOCK, axis=1)
        logits = jnp.einsum('bqgrd,bkgd->bgrqk', qb, k).astype(jnp.float32) * scale
        p = jax.nn.softmax(logits, axis=-1).astype(v.dtype)
        return jnp.einsum('bgrqk,bkgd->bqgrd', p, v)

    o = unblock(lax.map(block, jnp.arange(s // Q_BLOCK)))
    return o.reshape(b, s, A_HEADS * HEAD_DIM)


def dilated_attention(q, k, v, rel_bias):
    b, s = q.shape[:2]
    scale = HEAD_DIM ** -0.5
    pad = max((w // (2 * d)) * d for w, d in B_BRANCHES)
    kp = jnp.pad(k, ((0, 0), (pad, pad), (0, 0), (0, 0)))
    vp = jnp.pad(v, ((0, 0), (pad, pad), (0, 0), (0, 0)))
    offsets, biases = [], []
    for w, d in B_BRANCHES:
        n_side = w // (2 * d)
        off = jnp.arange(-n_side, n_side + 1, dtype=jnp.int32) * d
        offsets.append(off)
        biases.append(rel_bias[t5_bucket(off)].astype(jnp.float32).T)

    def block(i):
        t = i * Q_BLOCK + jnp.arange(Q_BLOCK, dtype=jnp.int32)
        qb = lax.dynamic_slice_in_dim(q, i * Q_BLOCK, Q_BLOCK, axis=1)
        outs, lses = [], []
        for off, bias in zip(offsets, biases):
            pos = t[:, None] + off[None, :]
            valid = (pos >= 0) & (pos < s)
            kg = jnp.take(kp, pos + pad, axis=1)
            vg = jnp.take(vp, pos + pad, axis=1)
            logits = jnp.einsum('bqhd,bqkhd->bhqk', qb, kg).astype(jnp.float32) * scale
            logits = jnp.where(valid[None, None], logits + bias[None, :, None, :], NEG_INF)
            m = jnp.max(logits, axis=-1, keepdims=True)
            p = jnp.exp(logits - m)
            l = jnp.sum(p, axis=-1, keepdims=True)
            o = jnp.einsum('bhqk,bqkhd->bqhd', (p / l).astype(vg.dtype), vg)
            outs.append(o.astype(jnp.float32))
            lses.append((m + jnp.log(l))[..., 0])
        wts = jax.nn.softmax(jnp.stack(lses, axis=0), axis=0)
        wts = jnp.transpose(wts, (0, 1, 3, 2))[..., None]
        return jnp.sum(jnp.stack(outs, axis=0) * wts, axis=0).astype(q.dtype)

    o = unblock(lax.map(block, jnp.arange(s // Q_BLOCK)))
    return o.reshape(b, s, B_HEADS * HEAD_DIM)


def mla_attention(xn, w_down, q_norm_g, kv_norm_g, w_uq, w_ukv, cos, sin):
    b, s = xn.shape[:2]
    h = xn @ w_down
    c_q, c_kv, k_r = jnp.split(h, [C_Q_RANK, C_Q_RANK + C_KV_RANK], axis=-1)
    q = (rms_norm(c_q, q_norm_g) @ w_uq).reshape(b, s, C_HEADS, C_QK_DIM)
    q_n, q_r = q[..., :C_NOPE_DIM], q[..., C_NOPE_DIM:]
    q_r = apply_rope(q_r, cos[:, None, :], sin[:, None, :])
    k_r = apply_rope(k_r, cos, sin)
    kv = (rms_norm(c_kv, kv_norm_g) @ w_ukv).reshape(b, s, C_HEADS, C_NOPE_DIM + C_V_DIM)
    k_n, v = kv[..., :C_NOPE_DIM], kv[..., C_NOPE_DIM:]
    scale = C_QK_DIM ** -0.5

    def block(i):
        qnb = lax.dynamic_slice_in_dim(q_n, i * Q_BLOCK, Q_BLOCK, axis=1)
        qrb = lax.dynamic_slice_in_dim(q_r, i * Q_BLOCK, Q_BLOCK, axis=1)
        logits = (jnp.einsum('bqhd,bkhd->bhqk', qnb, k_n)
                  + jnp.einsum('bqhr,bkr->bhqk', qrb, k_r)).astype(jnp.float32) * scale
        p = jax.nn.softmax(logits, axis=-1).astype(v.dtype)
        return jnp.einsum('bhqk,bkhd->bqhd', p, v)

    o = unblock(lax.map(block, jnp.arange(s // Q_BLOCK)))
    return o.reshape(b, s, C_OUT_IN)


def setup_inputs(seed: int = 0) -> dict:
    key = jax.random.key(seed)
    ks = jax.random.split(key, 20)
    f32 = jnp.float32

    def w(k, shape, fan_in):
        return jax.random.normal(k, shape, f32) * (fan_in ** -0.5)

    def g(k, shape):
        return 1.0 + 0.02 * jax.random.normal(k, shape, f32)

    return {
        "x": jax.random.normal(ks[0], (BATCH, SEQ, D_MODEL), f32),
        "norm_mix_g": g(ks[1], (DEPTH, D_MODEL)),
        "norm_mlp_g": g(ks[2], (DEPTH, D_MODEL)),
        "ab_w_in": w(ks[3], (N_EVEN, D_MODEL, AB_IN_W), D_MODEL),
        "a_q_norm_g": g(ks[4], (N_EVEN, HEAD_DIM)),
        "a_k_norm_g": g(ks[5], (N_EVEN, HEAD_DIM)),
        "ab_w_out": w(ks[6], (N_EVEN, AB_OUT_IN, D_MODEL), AB_OUT_IN),
        "rel_bias": 0.2 * jax.random.normal(ks[7], (NUM_BUCKETS, B_HEADS), f32),
        "c_w_down": w(ks[8], (N_ODD, D_MODEL, C_DOWN_W), D_MODEL),
        "c_q_norm_g": g(ks[9], (N_ODD, C_Q_RANK)),
        "c_kv_norm_g": g(ks[10], (N_ODD, C_KV_RANK)),
        "c_w_uq": w(ks[11], (N_ODD, C_Q_RANK, C_HEADS * C_QK_DIM), C_Q_RANK),
        "c_w_ukv": w(ks[12], (N_ODD, C_KV_RANK, C_HEADS * (C_NOPE_DIM + C_V_DIM)), C_KV_RANK),
        "c_w_out": w(ks[13], (N_ODD, C_OUT_IN, D_MODEL), C_OUT_IN),
        "mlp_w1": w(ks[14], (DEPTH, D_MODEL, D_FF), D_MODEL),
        "mlp_w2": w(ks[15], (DEPTH, D_FF, D_MODEL), D_FF),
        "final_norm_g": g(ks[16], (D_MODEL,)),
    }


def reference(x, norm_mix_g, norm_mlp_g, ab_w_in, a_q_norm_g, a_k_norm_g, ab_w_out,
              rel_bias, c_w_down, c_q_norm_g, c_kv_norm_g, c_w_uq, c_w_ukv, c_w_out,
              mlp_w1, mlp_w2, final_norm_g):
    b, s, _ = x.shape
    rows = s // GRID_W
    row = jnp.repeat(jnp.arange(rows, dtype=jnp.int32), GRID_W)
    col = jnp.tile(jnp.arange(GRID_W, dtype=jnp.int32), rows)
    ang_ax = jnp.concatenate([rope_angles(row, HEAD_DIM // 2),
                              rope_angles(col, HEAD_DIM // 2)], axis=-1)
    cos_ax, sin_ax = jnp.cos(ang_ax)[:, None, :], jnp.sin(ang_ax)[:, None, :]
    ang_c = rope_angles(jnp.arange(s, dtype=jnp.int32), C_ROPE_DIM)
    cos_c, sin_c = jnp.cos(ang_c), jnp.sin(ang_c)
    split_ab = np.cumsum([A_Q_W, A_KV_W, A_KV_W, B_W, B_W]).tolist()

    h = x
    for layer in range(DEPTH):
        j = layer // 2
        hn = rms_norm(h, norm_mix_g[layer])
        if layer % 2 == 0:
            proj = hn @ ab_w_in[j]
            qa, ka, va, qb, kb, vb = jnp.split(proj, split_ab, axis=-1)
            qa = rms_norm(qa.reshape(b, s, A_HEADS, HEAD_DIM), a_q_norm_g[j])
            ka = rms_norm(ka.reshape(b, s, A_KV_HEADS, HEAD_DIM), a_k_norm_g[j])
            qa = apply_rope(qa, cos_ax, sin_ax)
            ka = apply_rope(ka, cos_ax, sin_ax)
            va = va.reshape(b, s, A_KV_HEADS, HEAD_DIM)
            o_a = gqa_attention(qa, ka, va)
            o_b = dilated_attention(qb.reshape(b, s, B_HEADS, HEAD_DIM),
                                    kb.reshape(b, s, B_HEADS, HEAD_DIM),
                                    vb.reshape(b, s, B_HEADS, HEAD_DIM), rel_bias)
            mix = jnp.concatenate([o_a, o_b], axis=-1) @ ab_w_out[j]
        else:
            o_c = mla_attention(hn, c_w_down[j], c_q_norm_g[j], c_kv_norm_g[j],
                                c_w_uq[j], c_w_ukv[j], cos_c, sin_c)
            mix = o_c @ c_w_out[j]
        h = h + mix
        hn = rms_norm(h, norm_mlp_g[layer])
        h = h + jnp.square(jax.nn.relu(hn @ mlp_w1[layer])) @ mlp_w2[layer]
    return rms_norm(h, final_norm_g)
```

```python
import math
import numpy as np
from contextlib import ExitStack
import concourse.bass as bass
import concourse.mybir as mybir
from concourse.bass_utils import run_bass_kernel_spmd

F32 = mybir.dt.float32
BF16 = mybir.dt.bfloat16
ALU = mybir.AluOpType
AF = mybir.ActivationFunctionType

S = 2048
D = 1024
NB = 4
NEG = -30000.0
EPS = 1e-6

ENGS = ("pe", "act", "dve", "pool", "sp")


class Res:
    __slots__ = ("name", "psum", "lw", "readers", "dma_tl")

    def __init__(self, name, psum=False):
        self.name = name
        self.psum = psum
        self.lw = None
        self.readers = []
        self.dma_tl = None


class _Op:
    __slots__ = ("eng", "meth", "args", "kw", "waits", "signal", "idx", "dma_tl")


class Prog:
    def __init__(self, nc):
        self.nc = nc
        self.ops = {e: [] for e in ENGS}
        self.seen = {e: {} for e in ENGS}
        self.snap = {}
        self.dma_count = {}
        self.dma_tls = []
        self.last_real = {}
        self.nwaits = 0

    def _deps(self, eng, own_tl, reads, writes):
        deps = []
        for r in reads:
            if r.lw is not None:
                if not (r.lw[0] == own_tl and eng == "pe"):
                    deps.append(r.lw)
            if r.psum:
                for ev in r.readers:
                    if ev[0] != own_tl:
                        deps.append(ev)
        same_ok = own_tl in ("act", "dve", "pool")
        for w in writes:
            if w.lw is not None and (w.lw[0] != own_tl or same_ok):
                deps.append(w.lw)
            for ev in w.readers:
                if ev[0] != own_tl or same_ok:
                    deps.append(ev)
        return deps

    def _resolve(self, eng, deps):
        seen = self.seen[eng]
        best = {}
        for tl, i in deps:
            if best.get(tl, 0) < i:
                best[tl] = i
        waits = [(tl, i) for tl, i in best.items() if seen.get(tl, 0) < i]
        for tl, i in waits:
            if seen.get(tl, 0) < i:
                seen[tl] = i
            sn = self.snap.get((tl, i))
            if sn:
                for t2, i2 in sn.items():
                    if seen.get(t2, 0) < i2:
                        seen[t2] = i2
        for tl, i in waits:
            if tl in self.ops:
                self.ops[tl][i - 1].signal = True
        return waits

    def op(self, eng, meth, *args, R=(), W=(), **kw):
        o = _Op()
        o.eng, o.meth, o.args, o.kw = eng, meth, args, kw
        o.signal = False
        o.dma_tl = None
        lst = self.ops[eng]
        o.idx = len(lst) + 1
        o.waits = self._resolve(eng, self._deps(eng, eng, R, W))
        lst.append(o)
        self.last_real[eng] = o.idx
        ev = (eng, o.idx)
        self.snap[ev] = dict(self.seen[eng])
        for r in R:
            r.readers.append(ev)
        for w in W:
            w.lw = ev
            w.readers = []
        return o

    def dma(self, queue, out, in_, R=(), W=(), **kw):
        o = _Op()
        o.eng, o.meth, o.args, o.kw = queue, "dma_start", (), dict(out=out, in_=in_, **kw)
        o.signal = False
        base = W[0] if W else R[0]
        if base.dma_tl is None:
            base.dma_tl = "dma:" + base.name
        tl = base.dma_tl
        if tl not in self.dma_count:
            self.dma_count[tl] = 0
            self.dma_tls.append(tl)
        o.dma_tl = tl
        lst = self.ops[queue]
        o.idx = len(lst) + 1
        o.waits = self._resolve(queue, self._deps(queue, tl, R, W))
        lst.append(o)
        self.dma_count[tl] += 1
        ev = (tl, self.dma_count[tl])
        self.snap[ev] = dict(self.seen[queue])
        for r in R:
            r.readers.append(ev)
        for w in W:
            w.lw = ev
            w.readers = []
        return o

    def _noop(self, eng, deps):
        o = _Op()
        o.eng, o.meth, o.args, o.kw = eng, None, (), {}
        o.signal = False
        o.dma_tl = None
        lst = self.ops[eng]
        o.idx = len(lst) + 1
        o.waits = self._resolve(eng, deps)
        lst.append(o)

    def final_wait(self, eng, resources):
        deps = []
        for r in resources:
            if r.lw is not None:
                deps.append(r.lw)
            deps.extend(r.readers)
        self._noop(eng, deps)

    def barrier(self):
        targets = [(e, i) for e, i in self.last_real.items() if i]
        targets += [(tl, c) for tl, c in self.dma_count.items() if c]
        for e in ENGS:
            deps = [t for t in targets if not (t[0] == e and e in ("pe", "sp"))]
            self._noop(e, deps)

    def emit(self, stack):
        nc = self.nc
        sems = {}
        for e in ENGS:
            if any(o.signal for o in self.ops[e]):
                sems[e] = stack.enter_context(nc.semaphore("s_" + e))
        for tl in self.dma_tls:
            sems[tl] = stack.enter_context(nc.semaphore("s_" + tl.replace(":", "_")))
        sigval = {}
        for e in ENGS:
            c = 0
            for o in self.ops[e]:
                if o.signal:
                    c += 1
                    sigval[(e, o.idx)] = c
        block = stack.enter_context(nc.Block())

        def run(ekey, eng):
            for o in self.ops[ekey]:
                for tl, i in o.waits:
                    if tl in self.ops:
                        eng.wait_ge(sems[tl], sigval[(tl, i)])
                    else:
                        eng.wait_ge(sems[tl], 16 * i)
                    self.nwaits += 1
                if o.meth is None:
                    continue
                ins = getattr(eng, o.meth)(*o.args, **o.kw)
                if o.dma_tl is not None:
                    ins.then_inc(sems[o.dma_tl], 16)
                elif o.signal:
                    ins.then_inc(sems[ekey], 1)

        @block.tensor
        def _(eng):
            run("pe", eng)

        @block.scalar
        def _(eng):
            run("act", eng)

        @block.vector
        def _(eng):
            run("dve", eng)

        @block.gpsimd
        def _(eng):
            run("pool", eng)

        @block.sync
        def _(eng):
            run("sp", eng)


class Arena:
    def __init__(self, tile, nbytes):
        self.tile = tile
        self.size = nbytes
        self.top = 0

    def mark(self):
        return self.top

    def release(self, m):
        self.top = m

    def alloc(self, nbytes, dt=BF16):
        nbytes = (nbytes + 63) // 64 * 64
        off = self.top
        assert off + nbytes <= self.size, ("arena overflow", off, nbytes, self.size)
        self.top += nbytes
        v = self.tile[:, off // 2:(off + nbytes) // 2]
        if dt == F32:
            v = v.bitcast(F32)
        return v


def bcast_mid(ap2, n):
    a = ap2.ap
    return bass.AP(ap2.tensor, ap2.offset, [list(a[0]), [0, n], list(a[1])])


def _t5_bucket(rel):
    nb = 16
    max_exact = 8
    base = np.where(rel > 0, nb, 0)
    n = np.abs(rel)
    nf = np.maximum(n, 1).astype(np.float32)
    large = max_exact + (np.log(nf / np.float32(max_exact)) / np.float32(math.log(1024 / max_exact))
                         * np.float32(nb - max_exact)).astype(np.int32)
    large = np.minimum(large, nb - 1)
    return base + np.where(n < max_exact, n, large)


def _rope_tables():
    def inv_freq(dim):
        return (10000.0 ** (-np.arange(0, dim, 2, dtype=np.float32) / np.float32(dim))).astype(np.float32)

    t = np.arange(S)
    row = (t // 64).astype(np.float32)
    col = (t % 64).astype(np.float32)
    f16 = inv_freq(32)
    ang_ax = np.concatenate([row[:, None] * f16[None, :], col[:, None] * f16[None, :]], axis=-1)
    cosA = np.cos(ang_ax).astype(np.float32)
    sinA = np.sin(ang_ax).astype(np.float32)
    ropeA = np.zeros((2, 128, S), np.float32)
    for p in range(128):
        dd = p % 64
        ropeA[0, p] = cosA[:, dd % 32]
        ropeA[1, p] = (-1.0 if dd < 32 else 1.0) * sinA[:, dd % 32]
    ang_c = t.astype(np.float32)[:, None] * inv_freq(32)[None, :]
    cosC = np.cos(ang_c).astype(np.float32)
    sinC = np.sin(ang_c).astype(np.float32)
    ropeC = np.zeros((2, 128, S), np.float32)
    for p in range(64, 96):
        dd = p - 64
        ropeC[0, p] = cosC[:, dd % 16]
        ropeC[1, p] = (-1.0 if dd < 16 else 1.0) * sinC[:, dd % 16]
    return ropeA, ropeC


B_DIL = (1, 4, 16)


def _pack_shared(inp):
    f = lambda a: np.ascontiguousarray(np.asarray(a, dtype=np.float32))
    sh = {}
    ropeA, ropeC = _rope_tables()
    sh["ropeA"] = ropeA
    sh["ropeC"] = ropeC
    sh["ident"] = np.eye(128, dtype=np.float32)
    nm = f(inp["norm_mix_g"])
    nf = f(inp["norm_mlp_g"])
    gq = f(inp["a_q_norm_g"])[0]
    gk = f(inp["a_k_norm_g"])[0]
    perm64 = np.concatenate([np.arange(32, 64), np.arange(0, 32)])
    cols = []
    for v in (nm[0], nm[1], nf[0], nf[1]):
        cols.append(v.reshape(8, 128).T)
    pidx = np.arange(128) % 64
    cols.append(np.stack([gq[pidx], gq[perm64][pidx], gk[pidx], gk[perm64][pidx]], axis=1))
    cols.append(f(inp["c_q_norm_g"])[0].reshape(2, 128).T)
    cols.append(f(inp["c_kv_norm_g"])[0].reshape(1, 128).T)
    cols.append(f(inp["final_norm_g"]).reshape(8, 128).T)
    sh["gvec"] = f(np.concatenate(cols, axis=1))
    Win = f(inp["ab_w_in"])[0]
    Wk = Win.reshape(8, 128, 2304).transpose(1, 0, 2)
    wa = np.zeros((2, 128, 8, 832), np.float32)
    for g in range(2):
        q = Wk[:, :, g * 256:(g + 1) * 256]
        qp = q.reshape(128, 8, 4, 64)[:, :, :, perm64].reshape(128, 8, 256)
        k = Wk[:, :, 512 + g * 64:512 + (g + 1) * 64]
        kp = k[:, :, perm64]
        v = Wk[:, :, 640 + g * 64:640 + (g + 1) * 64]
        wa[g] = np.concatenate([q, qp, k, k, kp, kp, v], axis=2)
    sh["wa"] = wa
    wb = np.zeros((4, 128, 8, 384), np.float32)
    for hp in range(4):
        wb[hp] = np.concatenate([Wk[:, :, 768 + hp * 128:768 + (hp + 1) * 128],
                                 Wk[:, :, 1280 + hp * 128:1280 + (hp + 1) * 128],
                                 Wk[:, :, 1792 + hp * 128:1792 + (hp + 1) * 128]], axis=2)
    sh["wb"] = wb
    sh["wo0"] = f(f(inp["ab_w_out"])[0].reshape(8, 128, 1024).transpose(1, 0, 2))
    sh["wo1"] = f(f(inp["c_w_out"])[0].reshape(8, 128, 1024).transpose(1, 0, 2))
    w1 = f(inp["mlp_w1"])
    w2 = f(inp["mlp_w2"])
    sh["w1"] = f(w1.reshape(2, 8, 128, 4, 1024).transpose(0, 3, 2, 1, 4))
    sh["w2"] = f(w2.reshape(2, 4, 8, 128, 1024).transpose(0, 1, 3, 2, 4))
    rb = f(inp["rel_bias"])
    k_i = np.arange(128)[:, None]
    v_i = np.arange(384)[None, :]
    xrel = k_i - v_i + 128
    valid = np.abs(xrel) <= 64
    tb = np.full((8, 128, 3, 384), NEG, np.float32)
    for bi, d in enumerate(B_DIL):
        bucket = _t5_bucket(np.clip(xrel, -64, 64) * d)
        for h in range(8):
            tb[h, :, bi, :] = np.where(valid, rb[bucket, h], np.float32(NEG))
    sh["tb"] = tb
    Wd = f(inp["c_w_down"])[0]
    perm32 = np.concatenate([np.arange(16, 32), np.arange(0, 16)])
    kr = Wd[:, 384:416]
    Wd2 = np.concatenate([Wd, kr[:, perm32]], axis=1)
    sh["wdn"] = f(Wd2.reshape(8, 128, 448).transpose(1, 0, 2))
    Wq = f(inp["c_w_uq"])[0].reshape(2, 128, 16, 96).transpose(1, 0, 2, 3)
    wuq = np.concatenate([Wq, Wq[:, :, :, 64:96][:, :, :, perm32]], axis=3)
    sh["wuq"] = f(wuq)
    Wkv = f(inp["c_w_ukv"])[0].reshape(128, 16, 128)
    sh["wukv"] = f(np.concatenate([Wkv[:, :, 0:64].reshape(128, 1024), Wkv[:, :, 64:128].reshape(128, 1024)], axis=1))
    return sh


SHARED_SHAPES = {
    "ropeA": [2, 128, S], "ropeC": [2, 128, S], "ident": [128, 128], "gvec": [128, 47],
    "wa": [2, 128, 8, 832], "wb": [4, 128, 8, 384], "wo0": [128, 8, 1024], "wo1": [128, 8, 1024],
    "w1": [2, 4, 128, 8, 1024], "w2": [2, 4, 128, 8, 1024], "tb": [8, 128, 3, 384],
    "wdn": [128, 8, 448], "wuq": [128, 2, 16, 128], "wukv": [128, 2048],
}


def build(stage="full"):
    nc = bass.Bass("TRN2", target_bir_lowering=False)
    dr = {}
    dr["x"] = nc.dram_tensor("x", [S, D], F32, kind="ExternalInput").ap()
    for k, shp in SHARED_SHAPES.items():
        dr[k] = nc.dram_tensor(k, shp, F32, kind="ExternalInput").ap()
    if stage == "full":
        out_d = nc.dram_tensor("out", [S, D], F32, kind="ExternalOutput").ap()
    else:
        out_d = nc.dram_tensor("out", [128, 8, S], F32, kind="ExternalOutput").ap()

    st = ExitStack()
    P = Prog(nc)

    def sbt(name, shape, dt=F32):
        return st.enter_context(nc.sbuf_tensor("sb_" + name, list(shape), dt))

    hT = sbt("hT", [128, 8, S], F32)
    rH = [[Res("h%d_%d" % (c, t)) for t in range(NB)] for c in range(8)]
    identf = sbt("identf", [128, 128], F32); rIdent = Res("identf")
    gvec = sbt("gvec", [128, 47], F32); rG = Res("gvec")
    cm = sbt("cm", [128, 4, 128], BF16); rCM = Res("cm")
    epst = sbt("epst", [128, 1], F32); rEps = Res("eps")
    onesf = sbt("onesf", [128, 512], F32); rOnes = Res("onesf")
    arena_bytes = (nc.sbuf_bytes_remaining - 256) // 64 * 64
    print("arena bytes", arena_bytes)
    arena_t = sbt("arena", [128, arena_bytes // 2], BF16)
    AR = Arena(arena_t, arena_bytes)
    banks_t = st.enter_context(nc.psum_tensor("banks", [128, 8, 512], F32))
    BK = [banks_t[:, i, :] for i in range(8)]
    rBK = [Res("bank%d" % i, psum=True) for i in range(8)]

    P.dma("sp", identf[:], dr["ident"], W=[rIdent])
    P.dma("sp", gvec[:], dr["gvec"], W=[rG])
    P.op("dve", "memset", epst[:], EPS, W=[rEps])
    P.op("dve", "memset", onesf[:], -1.0, W=[rOnes])
    P.op("pool", "memset", cm[:, 0, :], 1.0 / 1024, W=[rCM])
    P.op("pool", "memset", cm[:, 1, :], 0.0, W=[rCM])
    P.op("pool", "memset", cm[0:64, 1, 0:64], 1.0 / 64, W=[rCM])
    P.op("pool", "memset", cm[64:128, 1, 64:128], 1.0 / 64, W=[rCM])
    P.op("pool", "memset", cm[:, 2, :], 1.0 / 256, W=[rCM])
    P.op("pool", "memset", cm[:, 3, :], 1.0 / 128, W=[rCM])

    blk = lambda t: slice(t * 512, (t + 1) * 512)
    evac_rr = [0]

    def evac(out, in_, R, W):
        evac_rr[0] ^= 1
        if evac_rr[0]:
            P.op("act", "activation", out=out, in_=in_, func=AF.Copy, R=R, W=W)
        else:
            P.op("dve", "tensor_copy", out=out, in_=in_, R=R, W=W)

    def phase_load():
        m = AR.mark()
        xs = [AR.alloc(4096, F32) for _ in range(3)]
        rX = [Res("xs%d" % i) for i in range(3)]
        for i in range(16):
            b = i % 3
            P.dma("sp", xs[b], dr["x"][i * 128:(i + 1) * 128, :], W=[rX[b]])
            t = i // 4
            for half in range(2):
                bk = (2 * i + half) % 8
                for cc in range(4):
                    c = half * 4 + cc
                    P.op("pe", "transpose", BK[bk][:, cc * 128:(cc + 1) * 128], xs[b][:, c * 128:(c + 1) * 128],
                         identf[:], R=[rX[b], rIdent], W=[rBK[bk]])
                evac(hT[:, half * 4:half * 4 + 4, i * 128:(i + 1) * 128],
                     BK[bk].rearrange("p (a b) -> p a b", a=4),
                     R=[rBK[bk]], W=[rH[half * 4 + cc][t] for cc in range(4)])
        P.barrier()
        AR.release(m)

    def rms_block(t, srcs, rsrcs, gcol, cmi, dsts, rdsts, scratch):
        sq, lnv, rstd, rSq, rLn, rRs = scratch
        n = len(srcs)
        bk = 4 + (t % 2)
        for c in range(n):
            eng = "pool" if c % 2 == 0 else "dve"
            P.op(eng, "tensor_tensor", out=sq[:, c, :], in0=srcs[c], in1=srcs[c], op=ALU.mult, R=[rsrcs[c]], W=[rSq[c]])
        for c in range(n):
            P.op("pe", "matmul", BK[bk], cm[:, cmi, :], sq[:, c, :], start=(c == 0), stop=(c == n - 1),
                 R=[rCM, rSq[c]], W=[rBK[bk]])
        P.op("act", "activation", out=lnv, in_=BK[bk], func=AF.Ln, bias=epst[:], scale=1.0,
             R=[rBK[bk], rEps], W=[rLn])
        P.op("act", "activation", out=rstd, in_=lnv, func=AF.Exp, scale=-0.5, R=[rLn], W=[rRs])
        for c in range(n):
            eng = "dve"
            P.op(eng, "scalar_tensor_tensor", out=dsts[c], in0=srcs[c],
                 scalar=gvec[:, gcol + c:gcol + c + 1], in1=rstd, op0=ALU.mult, op1=ALU.mult,
                 R=[rsrcs[c], rG, rRs], W=[rdsts[c]])

    def rmsnorm_fm(src, rSrc, nchunks, gcol, cmi, dst, rDst, scratch):
        for t in range(NB):
            rms_block(t, [src[:, c, blk(t)] for c in range(nchunks)], [rSrc[c][t] for c in range(nchunks)], gcol, cmi,
                      [dst[:, c, blk(t)] for c in range(nchunks)], [rDst[c][t] for c in range(nchunks)], scratch)

    nsc = [0]

    def norm_scratch():
        nsc[0] += 1
        sq = AR.alloc(8 * 512 * 2, BF16).rearrange("p (c n) -> p c n", c=8)
        lnv = AR.alloc(2048, F32)
        rstd = AR.alloc(2048, F32)
        k = nsc[0]
        return sq, lnv, rstd, [Res("n%d_sq%d" % (k, c)) for c in range(8)], Res("n%d_ln" % k), Res("n%d_rs" % k)

    def run_attention(steps, scale, PT2, rPT2, spairs, bg=None, bg_every=4):
        n = len(steps)
        npair = len(spairs)
        la = npair - 1

        def qk(s_):
            sa = spairs[s_ % npair]
            for l, ln in enumerate(steps[s_][0]):
                P.op("pe", "matmul", BK[sa + l], ln[0], ln[1], start=True, stop=True,
                     R=[ln[2], ln[3]], W=[rBK[sa + l]])

        def ex(s_):
            sa = spairs[s_ % npair]
            k = s_ % len(PT2)
            P.op("act", "activation", out=PT2[k].rearrange("p (a n) -> p a n", a=2), in_=banks_t[:, sa:sa + 2, :],
                 func=AF.Exp, scale=scale, R=[rBK[sa], rBK[sa + 1]], W=[rPT2[k]])

        def pv(s_):
            k = s_ % len(PT2)
            for l, ln in enumerate(steps[s_][0]):
                P.op("pe", "matmul", BK[ln[6]], ln[4], PT2[k][:, l * 512:(l + 1) * 512], start=ln[7], stop=ln[8],
                     R=[ln[5], rPT2[k]], W=[rBK[ln[6]]])
            if steps[s_][1] is not None:
                steps[s_][1]()

        for s_ in range(min(la, n)):
            qk(s_)
        for s_ in range(n):
            ex(s_)
            if s_ + la < n:
                qk(s_ + la)
            pv(s_)
            if bg is not None and s_ % bg_every == bg_every - 1:
                next(bg, None)
        if bg is not None:
            for _ in bg:
                pass

    def normalize(ob, osel, dst, rDst, rec, rRec):
        orow = slice(0, 64) if osel == 0 else slice(64, 128)
        lrow = slice(64, 128) if osel == 0 else slice(0, 64)
        lc, rc = rec
        P.op("dve", "tensor_copy", out=lc[lrow, :], in_=BK[ob][lrow, :], R=[rBK[ob]], W=[rRec[0]])
        P.op("pool", "tensor_tensor", out=rc[lrow, :], in0=lc[lrow, :], in1=onesf[lrow, :], op=ALU.pow,
             R=[rOnes, rRec[0]], W=[rRec[1]])
        P.op("dve", "tensor_tensor", out=dst[orow, :], in0=BK[ob][orow, :], in1=rc[lrow, :],
             op=ALU.mult, R=[rBK[ob], rRec[1]], W=[rDst])

    def out_proj(wo_name, OT, rOT):
        m = AR.mark()
        wo = AR.alloc(8 * 1024 * 2, BF16).rearrange("p (c n) -> p c n", c=8)
        rWo = Res("wo_" + wo_name)
        for q in range(4):
            P.dma("pool", wo[:, 2 * q:2 * q + 2, :], dr[wo_name][:, 2 * q:2 * q + 2, :], W=[rWo])
        k = 0
        for t in range(NB):
            for dc in range(8):
                bk = k % 4
                k += 1
                for fc in range(8):
                    P.op("pe", "matmul", BK[bk], wo[:, fc, dc * 128:(dc + 1) * 128], OT[:, fc, blk(t)],
                         start=(fc == 0), stop=(fc == 7), R=[rWo, rOT[fc][t]], W=[rBK[bk]])
                P.op("dve", "tensor_tensor", out=hT[:, dc, blk(t)], in0=BK[bk], in1=hT[:, dc, blk(t)], op=ALU.add,
                     R=[rBK[bk], rH[dc][t]], W=[rH[dc][t]])
        P.barrier()
        AR.release(m)

    def mlp(layer, gcol):
        m = AR.mark()
        hn = AR.alloc(8 * S * 2, BF16).rearrange("p (c n) -> p c n", c=8)
        rHn = [[Res("mhn%d_%d" % (c, t)) for t in range(NB)] for c in range(8)]
        w1b = [AR.alloc(8 * 1024 * 2, BF16).rearrange("p (c n) -> p c n", c=8) for _ in range(2)]
        w2b = [AR.alloc(8 * 1024 * 2, BF16).rearrange("p (c n) -> p c n", c=8) for _ in range(2)]
        rW1 = [Res("w1b%d" % i) for i in range(2)]
        rW2 = [Res("w2b%d" % i) for i in range(2)]
        rbuf = [AR.alloc(2048, F32) for _ in range(3)]
        rR = [Res("mr%d" % i) for i in range(3)]
        uset = [AR.alloc(8 * 1024, BF16).rearrange("p (c n) -> p c n", c=8) for _ in range(2)]
        rU = [[Res("mu%d_%d" % (i, c)) for c in range(8)] for i in range(2)]

        def load_w(g):
            b = g % 2
            for q in range(2):
                P.dma("pool", w1b[b][:, 4 * q:4 * q + 4, :], dr["w1"][layer, g, :, 4 * q:4 * q + 4, :], W=[rW1[b]])
            for q in range(2):
                P.dma("pool", w2b[b][:, 4 * q:4 * q + 4, :], dr["w2"][layer, g, :, 4 * q:4 * q + 4, :], W=[rW2[b]])

        load_w(0)
        load_w(1)
        scratch = norm_scratch()
        rmsnorm_fm(hT, rH, 8, gcol, 0, hn, rHn, scratch)
        its = [(g, t) for g in range(4) for t in range(NB)]
        rcnt = [0]
        pcnt = [0]

        def up(k):
            g, t = its[k]
            b = g % 2
            us, rUs = uset[k % 2], rU[k % 2]
            for fcl in range(8):
                bk = 6 + (fcl % 2)
                for kc in range(8):
                    P.op("pe", "matmul", BK[bk], w1b[b][:, kc, fcl * 128:(fcl + 1) * 128], hn[:, kc, blk(t)],
                         start=(kc == 0), stop=(kc == 7), R=[rW1[b], rHn[kc][t]], W=[rBK[bk]])
                ri = rcnt[0] % 3
                rcnt[0] += 1
                P.op("act", "activation", out=rbuf[ri], in_=BK[bk], func=AF.Relu, R=[rBK[bk]], W=[rR[ri]])
                eng = "pool" if fcl % 2 == 0 else "dve"
                P.op(eng, "tensor_tensor", out=us[:, fcl, :], in0=rbuf[ri], in1=rbuf[ri], op=ALU.mult, R=[rR[ri]], W=[rUs[fcl]])

        def down(k):
            g, t = its[k]
            b = g % 2
            us, rUs = uset[k % 2], rU[k % 2]
            for dcl in ((0, 1, 2), (3, 4, 5), (6, 7)):
                banks = (0, 1, 2) if pcnt[0] % 2 == 0 else (3, 4, 5)
                pcnt[0] += 1
                for di, dc in enumerate(dcl):
                    bk = banks[di]
                    for fcl in range(8):
                        P.op("pe", "matmul", BK[bk], w2b[b][:, fcl, dc * 128:(dc + 1) * 128], us[:, fcl, :],
                             start=(fcl == 0), stop=(fcl == 7), R=[rW2[b], rUs[fcl]], W=[rBK[bk]])
                    P.op("dve", "tensor_tensor", out=hT[:, dc, blk(t)], in0=BK[bk], in1=hT[:, dc, blk(t)], op=ALU.add,
                         R=[rBK[bk], rH[dc][t]], W=[rH[dc][t]])

        up(0)
        for k in range(16):
            if k + 1 < 16:
                up(k + 1)
            down(k)
            g, t = its[k]
            if t == NB - 1 and g + 2 < 4:
                load_w(g + 2)
        P.barrier()
        AR.release(m)

    def layer0_mix():
        m0 = AR.mark()
        OT = AR.alloc(8 * S * 2, BF16).rearrange("p (c n) -> p c n", c=8)
        rOT = [[Res("ot%d_%d" % (c, t)) for t in range(NB)] for c in range(8)]
        hn = AR.alloc(8 * S * 2, BF16).rearrange("p (c n) -> p c n", c=8)
        rHn = [[Res("hn%d_%d" % (c, t)) for t in range(NB)] for c in range(8)]
        PT2 = [AR.alloc(2048, BF16) for _ in range(3)]
        rPT2 = [Res("pt2_%d" % i) for i in range(3)]
        PT = [PT2[i // 2][:, (i % 2) * 512:(i % 2 + 1) * 512] for i in range(4)]
        rPT = [rPT2[i // 2] for i in range(4)]
        rec = [(AR.alloc(2048, F32), AR.alloc(2048, F32)) for _ in range(2)]
        rRec = [(Res("recl%d" % i), Res("recr%d" % i)) for i in range(2)]
        m1 = AR.mark()
        scratch = norm_scratch()
        rmsnorm_fm(hT, rH, 8, 0, 0, hn, rHn, scratch)
        P.barrier()
        AR.release(m1)

        ropeA = AR.alloc(2 * S * 4, F32).rearrange("p (a n) -> p a n", a=2)
        rRope = Res("ropeA")
        P.dma("sp", ropeA[:, 0, :], dr["ropeA"][0], W=[rRope])
        P.dma("sp", ropeA[:, 1, :], dr["ropeA"][1], W=[rRope])
        wA1 = AR.alloc(8 * 832 * 2, BF16).rearrange("p (c n) -> p c n", c=8)
        rWA1 = Res("wA")
        wA = [wA1, wA1]
        rWA = [rWA1, rWA1]
        QA = AR.alloc(2 * S * 2, BF16).rearrange("p (c n) -> p c n", c=2)
        KA = AR.alloc(S * 2, BF16)
        VA = AR.alloc(16 * 192 * 2, BF16).rearrange("p (i n) -> p i n", i=16)
        qf = [AR.alloc(2048, F32) for _ in range(2)]
        sqb = AR.alloc(1024, BF16)
        lnv = AR.alloc(2048, F32)
        rstd = AR.alloc(2048, F32)
        t1 = AR.alloc(2048, F32)
        t2 = AR.alloc(2048, F32)
        rQf = [Res("qf0"), Res("qf1")]
        rSq, rLn, rRs, rT1, rT2 = Res("a_sq"), Res("a_ln"), Res("a_rs"), Res("a_t1"), Res("a_t2")
        rQA = [Res("QA0"), Res("QA1")]
        rKA = Res("KA")
        rVA = Res("VA")
        P.op("pool", "memset", VA[:, :, 0:64], 1.0, W=[rVA])
        P.op("pool", "memset", VA[:, :, 128:192], 1.0, W=[rVA])

        def proj_norm_rope(w, rW, c0, cp0, gc, dstT, rDst):
            for t in range(NB):
                for kc in range(8):
                    P.op("pe", "matmul", BK[4], w[:, kc, c0:c0 + 128], hn[:, kc, blk(t)], start=(kc == 0), stop=(kc == 7),
                         R=[rW, rHn[kc][t]], W=[rBK[4]])
                for kc in range(8):
                    P.op("pe", "matmul", BK[5], w[:, kc, cp0:cp0 + 128], hn[:, kc, blk(t)], start=(kc == 0), stop=(kc == 7),
                         R=[rW, rHn[kc][t]], W=[rBK[5]])
                P.op("dve", "tensor_copy", out=qf[0], in_=BK[4], R=[rBK[4]], W=[rQf[0]])
                P.op("act", "activation", out=qf[1], in_=BK[5], func=AF.Copy, scale=gvec[:, gc + 1:gc + 2],
                     R=[rBK[5], rG], W=[rQf[1]])
                P.op("pool", "tensor_tensor", out=sqb, in0=qf[0], in1=qf[0], op=ALU.mult, R=[rQf[0]], W=[rSq])
                P.op("pe", "matmul", BK[4], cm[:, 1, :], sqb, start=True, stop=True, R=[rCM, rSq], W=[rBK[4]])
                P.op("act", "activation", out=lnv, in_=BK[4], func=AF.Ln, bias=epst[:], scale=1.0,
                     R=[rBK[4], rEps], W=[rLn])
                P.op("act", "activation", out=rstd, in_=lnv, func=AF.Exp, scale=-0.5, R=[rLn], W=[rRs])
                P.op("dve", "scalar_tensor_tensor", out=t1, in0=qf[0], scalar=gvec[:, gc:gc + 1], in1=ropeA[:, 0, blk(t)],
                     op0=ALU.mult, op1=ALU.mult, R=[rQf[0], rG, rRope], W=[rT1])
                P.op("dve", "tensor_tensor", out=t2, in0=qf[1], in1=ropeA[:, 1, blk(t)], op=ALU.mult, R=[rQf[1], rRope], W=[rT2])
                P.op("pool", "tensor_tensor", out=t1, in0=t1, in1=t2, op=ALU.add, R=[rT1, rT2], W=[rT1])
                P.op("dve", "tensor_tensor", out=dstT[:, blk(t)], in0=t1, in1=rstd, op=ALU.mult, R=[rT1, rRs], W=[rDst])

        for g in range(2):
            w = wA[g]
            for q in range(2):
                P.dma("pool", w[:, 4 * q:4 * q + 4, :], dr["wa"][g, :, 4 * q:4 * q + 4, :], W=[rWA[g]])
            for j in range(2):
                proj_norm_rope(w, rWA[g], j * 128, 256 + j * 128, 32, QA[:, j, :], rQA[j])
            proj_norm_rope(w, rWA[g], 512, 640, 34, KA, rKA)
            for half in range(2):
                bk = 4 + half
                for ii in range(8):
                    i = half * 8 + ii
                    for kc in range(8):
                        P.op("pe", "matmul", BK[bk][:, ii * 64:(ii + 1) * 64], hn[:, kc, i * 128:(i + 1) * 128],
                             w[:, kc, 768:832], start=(kc == 0), stop=(kc == 7), skip_group_check=True,
                             R=[rWA[g], rHn[kc][i // 4]], W=[rBK[bk]])
                evac(VA[:, half * 8:half * 8 + 8, 64:128], BK[bk].rearrange("p (i n) -> p i n", i=8),
                     R=[rBK[bk]], W=[rVA])
            steps = []
            for j in range(2):
                for t in range(NB):
                    obs = (4 + 2 * (t % 2), 5 + 2 * (t % 2))
                    for i in range(16):
                        lanes = []
                        for e in range(2):
                            rows = slice(e * 64, (e + 1) * 64)
                            v = VA[:, i, 64:192] if e == 0 else VA[:, i, 0:128]
                            lanes.append((KA[rows, i * 128:(i + 1) * 128], QA[rows, j, blk(t)], rKA, rQA[j], v, rVA,
                                          obs[e], i == 0, i == 15))
                        post = None
                        if i == 15:
                            def post(j=j, t=t, obs=obs):
                                for e in range(2):
                                    normalize(obs[e], e, OT[:, 2 * g + j, blk(t)], rOT[2 * g + j][t], rec[e], rRec[e])
                        steps.append((lanes, post))
            run_attention(steps, 0.125, PT2, rPT2, (0, 2))
        P.barrier()
        AR.release(m1)

        wB = AR.alloc(8 * 384 * 2, BF16).rearrange("p (c n) -> p c n", c=8)
        rWB = Res("wB")
        QB = AR.alloc(S * 2, BF16)
        KB = AR.alloc(S * 2, BF16)
        VB = AR.alloc(3 * 16 * 192 * 2, BF16).rearrange("p (b i n) -> p b i n", b=3, i=16)
        rQB, rKB, rVB = Res("QB"), Res("KB"), Res("VB")
        TBs = [AR.alloc(3 * 384 * 4, F32).rearrange("p (b n) -> p b n", b=3) for _ in range(2)]
        rTB = [Res("tb0"), Res("tb1")]
        tmp = [AR.alloc(2048, F32) for _ in range(3)]
        rTmp = [Res("btmp%d" % i) for i in range(3)]
        PTb = [AR.alloc(1024, BF16) for _ in range(4)]
        rPTb = [Res("bpt%d" % i) for i in range(4)]
        lnl = AR.alloc(2048, F32)
        rLnl = Res("b_lnl")
        P.op("pool", "memset", VB[:, :, :, 64:128], 1.0, W=[rVB])
        sb_rr = [0]
        tmp_rr = [0]
        pt_rr = [0]
        ob_rr = [0]
        for hp in range(4):
            for q in range(2):
                P.dma("pool", wB[:, 4 * q:4 * q + 4, :], dr["wb"][hp, :, 4 * q:4 * q + 4, :], W=[rWB])
            for e in range(2):
                P.dma("sp", TBs[e], dr["tb"][2 * hp + e], W=[rTB[e]])
            for (c0, dstT, rDst) in ((0, QB, rQB), (128, KB, rKB)):
                for t in range(NB):
                    bk = 5 + (t % 2)
                    for kc in range(8):
                        P.op("pe", "matmul", BK[bk], wB[:, kc, c0:c0 + 128], hn[:, kc, blk(t)], start=(kc == 0), stop=(kc == 7),
                             R=[rWB, rHn[kc][t]], W=[rBK[bk]])
                    evac(dstT[:, blk(t)], BK[bk], R=[rBK[bk]], W=[rDst])
            for bi, d in enumerate(B_DIL):
                nkt = (S // d) // 128
                for grp in range(4):
                    bk = 5 + (grp % 2)
                    for q in range(4):
                        ti = grp * 4 + q
                        r_, kt = ti // nkt, ti % nkt
                        s0 = kt * 128 * d + r_
                        tok = slice(s0, s0 + 127 * d + 1, d)
                        for kc in range(8):
                            P.op("pe", "matmul", BK[bk][:, q * 128:(q + 1) * 128], hn[:, kc, tok], wB[:, kc, 256:384],
                                 start=(kc == 0), stop=(kc == 7), skip_group_check=True,
                                 R=[rWB] + [rHn[kc][tt] for tt in range(NB)], W=[rBK[bk]])
                    dst = VB[:, bi, grp * 4:grp * 4 + 4, :].rearrange("p i (a n) -> p i a n", a=3)[:, :, 0:3:2, :]
                    evac(dst, BK[bk].rearrange("p (i a n) -> p i a n", i=4, a=2), R=[rBK[bk]], W=[rVB])
            flat = []
            for e in range(2):
                for c in range(NB):
                    ob = 6 + (ob_rr[0] % 2)
                    ob_rr[0] += 1
                    jobs = []
                    for di, delta in enumerate((128, 0, -128)):
                        regs = []
                        for qtl in range(4):
                            kt = 4 * c + qtl + delta // 128
                            if 0 <= kt < 16:
                                regs.append((qtl * 128, 128, kt, slice(kt * 128, kt * 128 + 128),
                                             slice(c * 512 + qtl * 128, c * 512 + qtl * 128 + 128),
                                             slice(qtl * 128, qtl * 128 + 128)))
                        if regs:
                            jobs.append((0, regs, slice(128 - delta, 256 - delta), 128))
                    for di, delta in enumerate((128, 0, -128)):
                        kt = c + delta // 128
                        if 0 <= kt < 4:
                            regs = []
                            for r_ in range(4):
                                ks = kt * 512 + r_
                                qs = c * 512 + r_
                                regs.append((r_ * 128, 128, r_ * 4 + kt, slice(ks, ks + 127 * 4 + 1, 4),
                                             slice(qs, qs + 127 * 4 + 1, 4), slice(r_, r_ + 127 * 4 + 1, 4)))
                            jobs.append((1, regs, slice(128 - delta, 256 - delta), 128))
                    regs = []
                    for r_ in range(16):
                        qs = c * 512 + r_
                        regs.append((r_ * 32, 32, r_, slice(r_, r_ + 127 * 16 + 1, 16),
                                     slice(qs, qs + 31 * 16 + 1, 16), slice(r_, r_ + 31 * 16 + 1, 16)))
                    jobs.append((2, regs, slice(128 + c * 32, 128 + c * 32 + 32), 32))
                    for ji, jb in enumerate(jobs):
                        flat.append((jb, ob, e, ji == 0, ji == len(jobs) - 1, c))
            nj = len(flat)
            LA = 3

            def b_qk(j):
                (bi, regs, mcols, w), ob, e, isfirst, islast, c = flat[j]
                rows = slice(e * 64, (e + 1) * 64)
                sb_ = j % 5
                for (col0, w_, ti, ksl, qsl, osl) in regs:
                    P.op("pe", "matmul", BK[sb_][:, col0:col0 + w_], KB[rows, ksl], QB[rows, qsl],
                         start=True, stop=True, skip_group_check=True, R=[rKB, rQB], W=[rBK[sb_]])

            def b_mid(j):
                (bi, regs, mcols, w), ob, e, isfirst, islast, c = flat[j]
                sb_ = j % 5
                c_lo = regs[0][0]
                c_hi = regs[-1][0] + w
                nreg = len(regs)
                tm, rTm = tmp[j % 3], rTmp[j % 3]
                P.op("dve", "scalar_tensor_tensor",
                     out=tm[:, c_lo:c_hi].rearrange("p (a n) -> p a n", a=nreg),
                     in0=BK[sb_][:, c_lo:c_hi].rearrange("p (a n) -> p a n", a=nreg),
                     scalar=0.125, in1=bcast_mid(TBs[e][:, bi, mcols], nreg),
                     op0=ALU.mult, op1=ALU.add, R=[rBK[sb_], rTB[e]], W=[rTm])
                P.op("act", "activation", out=PTb[j % 4][:, c_lo:c_hi], in_=tm[:, c_lo:c_hi], func=AF.Exp,
                     R=[rTm], W=[rPTb[j % 4]])

            def b_pv(j):
                (bi, regs, mcols, w), ob, e, isfirst, islast, c = flat[j]
                vsl = slice(0, 128) if e == 0 else slice(64, 192)
                orow = slice(0, 64) if e == 0 else slice(64, 128)
                lrow = slice(64, 128) if e == 0 else slice(0, 64)
                pt = PTb[j % 4]
                for ri, (col0, w_, ti, ksl, qsl, osl) in enumerate(regs):
                    P.op("pe", "matmul", BK[ob][:, osl], VB[:, bi, ti, vsl], pt[:, col0:col0 + w_],
                         start=(isfirst and ri == 0), stop=(islast and ri == len(regs) - 1), skip_group_check=True,
                         R=[rVB, rPTb[j % 4]], W=[rBK[ob]])
                if islast:
                    rr = c % 2
                    P.op("act", "activation", out=lnl[lrow, :], in_=BK[ob][lrow, :], func=AF.Ln, R=[rBK[ob]], W=[rLnl])
                    P.op("act", "activation", out=rec[rr][1][lrow, :], in_=lnl[lrow, :], func=AF.Exp, scale=-1.0,
                         R=[rLnl], W=[rRec[rr][1]])
                    P.op("dve", "tensor_tensor", out=OT[orow, 4 + hp, blk(c)], in0=BK[ob][orow, :], in1=rec[rr][1][lrow, :],
                         op=ALU.mult, R=[rBK[ob], rRec[rr][1]], W=[rOT[4 + hp][c]])

            for j in range(min(LA, nj)):
                b_qk(j)
            for j in range(nj):
                b_mid(j)
                if j + LA < nj:
                    b_qk(j + LA)
                b_pv(j)
        P.barrier()
        AR.release(m1)
        out_proj("wo0", OT, rOT)
        AR.release(m0)

    def layer1_mix():
        m0 = AR.mark()
        PT2 = [AR.alloc(2048, BF16) for _ in range(3)]
        rPT2 = [Res("c_pt2_%d" % i) for i in range(3)]
        rec = [(AR.alloc(2048, F32), AR.alloc(2048, F32)) for _ in range(2)]
        rRec = [(Res("c_recl%d" % i), Res("c_recr%d" % i)) for i in range(2)]
        cqn = AR.alloc(2 * S * 2, BF16).rearrange("p (c n) -> p c n", c=2)
        ckvn = AR.alloc(S * 2, BF16).rearrange("p (c n) -> p c n", c=1)
        krT = AR.alloc(S * 2, BF16)
        rCqn = [[Res("cqn%d_%d" % (c, t)) for t in range(NB)] for c in range(2)]
        rCkvn = [[Res("ckvn_%d" % t) for t in range(NB)]]
        rKr = Res("krT")
        ropeC = AR.alloc(2 * S * 4, F32).rearrange("p (a n) -> p a n", a=2)
        rRope = Res("ropeC")
        P.dma("sp", ropeC[:, 0, :], dr["ropeC"][0], W=[rRope])
        P.dma("sp", ropeC[:, 1, :], dr["ropeC"][1], W=[rRope])
        wuq = AR.alloc(2 * 16 * 128 * 2, BF16).rearrange("p (c h n) -> p c h n", c=2, h=16)
        wukv = AR.alloc(2048 * 2, BF16)
        rWuq, rWukv = Res("wuq"), Res("wukv")
        for c in range(2):
            P.dma("pool", wuq[:, c], dr["wuq"][:, c], W=[rWuq])
        P.dma("pool", wukv, dr["wukv"], W=[rWukv])
        t1 = AR.alloc(2048, F32)
        t2 = AR.alloc(2048, F32)
        rT1, rT2 = Res("c_t1"), Res("c_t2")
        m1 = AR.mark()
        hn = AR.alloc(8 * S * 2, BF16).rearrange("p (c n) -> p c n", c=8)
        rHn = [[Res("c_hn%d_%d" % (c, t)) for t in range(NB)] for c in range(8)]
        wdn = AR.alloc(8 * 448 * 2, BF16).rearrange("p (c n) -> p c n", c=8)
        rWdn = Res("wdn")
        for q in range(2):
            P.dma("pool", wdn[:, 4 * q:4 * q + 4, :], dr["wdn"][:, 4 * q:4 * q + 4, :], W=[rWdn])
        cblk = [AR.alloc(3 * 2048, F32).rearrange("p (c n) -> p c n", c=3) for _ in range(2)]
        rCb = [[Res("cb%d_%d" % (i, c)) for c in range(3)] for i in range(2)]
        scratch = norm_scratch()
        rmsnorm_fm(hT, rH, 8, 8, 0, hn, rHn, scratch)
        rr_ = slice(64, 96)
        for t in range(NB):
            cb = cblk[t % 2]
            rC = rCb[t % 2]
            for ci in range(3):
                c0 = ci * 128
                bk = 6 + (ci % 2)
                for kc in range(8):
                    P.op("pe", "matmul", BK[bk], wdn[:, kc, c0:c0 + 128], hn[:, kc, blk(t)], start=(kc == 0), stop=(kc == 7),
                         R=[rWdn, rHn[kc][t]], W=[rBK[bk]])
                evac(cb[:, ci, :], BK[bk], R=[rBK[bk]], W=[rC[ci]])
            for (c0, bk) in ((384, 2), (416, 3)):
                for kc in range(8):
                    P.op("pe", "matmul", BK[bk][64:96, :], wdn[:, kc, c0:c0 + 32], hn[:, kc, blk(t)],
                         start=(kc == 0), stop=(kc == 7), R=[rWdn, rHn[kc][t]], W=[rBK[bk]])
            P.op("dve", "tensor_tensor", out=t1[rr_, :], in0=BK[2][rr_, :], in1=ropeC[rr_, 0, blk(t)], op=ALU.mult,
                 R=[rBK[2], rRope], W=[rT1])
            P.op("dve", "tensor_tensor", out=t2[rr_, :], in0=BK[3][rr_, :], in1=ropeC[rr_, 1, blk(t)], op=ALU.mult,
                 R=[rBK[3], rRope], W=[rT2])
            P.op("pool", "tensor_tensor", out=krT[rr_, blk(t)], in0=t1[rr_, :], in1=t2[rr_, :], op=ALU.add,
                 R=[rT1, rT2], W=[rKr])
            rms_block(t, [cb[:, 0, :], cb[:, 1, :]], [rC[0], rC[1]], 36, 2,
                      [cqn[:, 0, blk(t)], cqn[:, 1, blk(t)]], [rCqn[0][t], rCqn[1][t]], scratch)
            rms_block(t, [cb[:, 2, :]], [rC[2]], 38, 3, [ckvn[:, 0, blk(t)]], [rCkvn[0][t]], scratch)
        P.barrier()
        AR.release(m1)
        OT = AR.alloc(8 * S * 2, BF16).rearrange("p (c n) -> p c n", c=8)
        rOT = [[Res("c_ot%d_%d" % (c, t)) for t in range(NB)] for c in range(8)]
        m2 = AR.mark()
        sets = []
        for si in range(2):
            QT = AR.alloc(2 * S * 2, BF16).rearrange("p (h n) -> p h n", h=2)
            KT = AR.alloc(2 * S * 2, BF16).rearrange("p (h n) -> p h n", h=2)
            VC = AR.alloc(16 * 192 * 2, BF16).rearrange("p (i n) -> p i n", i=16)
            rQT = [Res("QT%d_%d" % (si, i)) for i in range(2)]
            rKT = [Res("KT%d_%d" % (si, i)) for i in range(2)]
            rVC = Res("VC%d" % si)
            P.op("pool", "memset", VC[:, :, 64:128], 1.0, W=[rVC])
            sets.append((QT, KT, VC, rQT, rKT, rVC))
        t1s = [t1, t1]
        t2s = [t2, t2]
        rT1s = [rT1, rT1]
        rT2s = [rT2, rT2]
        cscale = 96.0 ** -0.5
        ucnt = [0]

        def proj_group(grp, si):
            QT, KT, VC, rQT, rKT, rVC = sets[si]
            for hl in range(2):
                h = 2 * grp + hl
                for t in range(NB):
                    u = ucnt[0] % 2
                    ucnt[0] += 1
                    for kc in range(2):
                        P.op("pe", "matmul", BK[4][0:96, :], wuq[:, kc, h, 0:96], cqn[:, kc, blk(t)], start=(kc == 0), stop=(kc == 1),
                             R=[rWuq, rCqn[kc][t]], W=[rBK[4]])
                    for kc in range(2):
                        P.op("pe", "matmul", BK[5][64:96, :], wuq[:, kc, h, 96:128], cqn[:, kc, blk(t)], start=(kc == 0), stop=(kc == 1),
                             R=[rWuq, rCqn[kc][t]], W=[rBK[5]])
                    P.op("pe", "matmul", BK[5][0:64, :], wukv[:, h * 64:(h + 1) * 64], ckvn[:, 0, blk(t)], start=True, stop=True,
                         R=[rWukv, rCkvn[0][t]], W=[rBK[5]])
                    P.op("dve", "tensor_copy", out=QT[0:64, hl, blk(t)], in_=BK[4][0:64, :], R=[rBK[4]], W=[rQT[hl]])
                    P.op("dve", "tensor_tensor", out=t1s[u][rr_, :], in0=BK[4][rr_, :], in1=ropeC[rr_, 0, blk(t)], op=ALU.mult,
                         R=[rBK[4], rRope], W=[rT1s[u]])
                    P.op("dve", "tensor_tensor", out=t2s[u][rr_, :], in0=BK[5][rr_, :], in1=ropeC[rr_, 1, blk(t)], op=ALU.mult,
                         R=[rBK[5], rRope], W=[rT2s[u]])
                    P.op("dve", "tensor_copy", out=KT[0:64, hl, blk(t)], in_=BK[5][0:64, :], R=[rBK[5]], W=[rKT[hl]])
                    P.op("pool", "tensor_tensor", out=QT[rr_, hl, blk(t)], in0=t1s[u][rr_, :], in1=t2s[u][rr_, :], op=ALU.add,
                         R=[rT1s[u], rT2s[u]], W=[rQT[hl]])
                    yield
                P.op("dve", "tensor_copy", out=KT[rr_, hl, :], in_=krT[rr_, :], R=[rKr], W=[rKT[hl]])
                yield
            for i4 in range(4):
                for q in range(4):
                    i = i4 * 4 + q
                    P.op("pe", "matmul", BK[4][:, q * 128:(q + 1) * 128], ckvn[:, 0, i * 128:(i + 1) * 128],
                         wukv[:, 1024 + grp * 128:1024 + (grp + 1) * 128], start=True, stop=True,
                         skip_group_check=True, R=[rWukv, rCkvn[0][i // 4]], W=[rBK[4]])
                dst = VC[:, i4 * 4:i4 * 4 + 4, :].rearrange("p i (b n) -> p i b n", b=3)[:, :, 0:3:2, :]
                P.op("dve", "tensor_copy", out=dst, in_=BK[4].rearrange("p (i b n) -> p i b n", i=4, b=2),
                     R=[rBK[4]], W=[rVC])
                yield

        def attn_steps(grp, si):
            QT, KT, VC, rQT, rKT, rVC = sets[si]
            steps = []
            for hl in range(2):
                h = 2 * grp + hl
                e = hl
                for t in range(NB):
                    ob = 6 + (t % 2)
                    for i2 in range(8):
                        lanes = []
                        for l in range(2):
                            i = 2 * i2 + l
                            v = VC[:, i, 0:128] if e == 0 else VC[:, i, 64:192]
                            lanes.append((KT[0:96, hl, i * 128:(i + 1) * 128], QT[0:96, hl, blk(t)], rKT[hl], rQT[hl], v, rVC,
                                          ob, i == 0, i == 15))
                        post = None
                        if i2 == 7:
                            def post(h=h, e=e, t=t, ob=ob):
                                normalize(ob, e, OT[:, h // 2, blk(t)], rOT[h // 2][t], rec[t % 2], rRec[t % 2])
                        steps.append((lanes, post))
            return steps

        import os
        MODE = os.environ.get("MLA_MODE", "seq")
        for _ in proj_group(0, 0):
            pass
        for grp in range(8):
            bg = proj_group(grp + 1, (grp + 1) % 2) if grp + 1 < 8 else None
            if MODE == "bg":
                run_attention(attn_steps(grp, grp % 2), cscale, PT2, rPT2, (0, 2), bg=bg, bg_every=4)
            elif MODE == "seq":
                run_attention(attn_steps(grp, grp % 2), cscale, PT2, rPT2, (0, 2))
                if bg is not None:
                    for _ in bg:
                        pass
            elif MODE == "projonly":
                if bg is not None:
                    for _ in bg:
                        pass
        P.barrier()
        AR.release(m2)
        out_proj("wo1", OT, rOT)
        AR.release(m0)

    def final_store():
        m = AR.mark()
        yT = AR.alloc(8 * S * 4, F32).rearrange("p (c n) -> p c n", c=8)
        rY = [[Res("y%d_%d" % (c, t)) for t in range(NB)] for c in range(8)]
        scratch = norm_scratch()
        rmsnorm_fm(hT, rH, 8, 39, 0, yT, rY, scratch)
        ys = [AR.alloc(4096, F32) for _ in range(3)]
        rYs = [Res("ys%d" % i) for i in range(3)]
        rOut = [Res("out%d" % i) for i in range(3)]
        for i in range(16):
            b = i % 3
            t = i // 4
            for half in range(2):
                bk = (2 * i + half) % 8
                for cc in range(4):
                    c = half * 4 + cc
                    P.op("pe", "transpose", BK[bk][:, cc * 128:(cc + 1) * 128], yT[:, c, i * 128:(i + 1) * 128],
                         identf[:], R=[rY[c][t], rIdent], W=[rBK[bk]])
                evac(ys[b][:, half * 512:(half + 1) * 512], BK[bk], R=[rBK[bk]], W=[rYs[b]])
            P.dma("sp", out_d[i * 128:(i + 1) * 128, :], ys[b], R=[rYs[b]], W=[rOut[b]])
        P.final_wait("sp", rOut)
        AR.release(m)

    def dump_h():
        rOut = Res("out")
        for c in range(8):
            P.dma("sp", out_d[:, c, :], hT[:, c, :], R=[rH[c][t] for t in range(NB)], W=[rOut])
        P.final_wait("sp", [rOut])

    phase_load()
    stages = ["x", "l0mix", "l0mlp", "l1mix", "l1mlp", "full"]
    si = stages.index(stage)
    if si >= 1:
        layer0_mix()
    if si >= 2:
        mlp(0, 16)
    if si >= 3:
        layer1_mix()
    if si >= 4:
        mlp(1, 24)
    if stage == "full":
        final_store()
    else:
        dump_h()
    P.emit(st)
    st.close()
    print('ops', {e: len(v) for e, v in P.ops.items()}, 'waits', P.nwaits)
    return nc


_CACHE = {}


def run(inputs, stage="full", ncores=8):
    if stage not in _CACHE:
        _CACHE[stage] = build(stage)
    nc = _CACHE[stage]
    x = np.asarray(inputs["x"], dtype=np.float32)
    sh = _pack_shared(inputs)
    in_maps = []
    for b in range(ncores):
        m = {"x": np.ascontiguousarray(x[b])}
        m.update(sh)
        in_maps.append(m)
    res = run_bass_kernel_spmd(nc, in_maps, core_ids=list(range(ncores)))
    return [r["out"] for r in res.results]


def kernel(**inputs):
    outs = run(inputs, "full")
    return np.stack(outs, axis=0).astype(np.float32)
```

```python
import math
import numpy as np
from contextlib import ExitStack
import concourse.bass as bass
import concourse.mybir as mybir
from concourse.bass_utils import run_bass_kernel_spmd

F32 = mybir.dt.float32
BF16 = mybir.dt.bfloat16
ALU = mybir.AluOpType
AF = mybir.ActivationFunctionType

S = 2048
D = 1024
NB = 4
NEG = -30000.0
EPS = 1e-6

ENGS = ("pe", "act", "dve", "pool", "sp")


class Res:
    __slots__ = ("name", "psum", "lw", "readers", "dma_tl")

    def __init__(self, name, psum=False):
        self.name = name
        self.psum = psum
        self.lw = None
        self.readers = []
        self.dma_tl = None


class _Op:
    __slots__ = ("eng", "meth", "args", "kw", "waits", "signal", "idx", "dma_tl")


class Prog:
    def __init__(self, nc):
        self.nc = nc
        self.ops = {e: [] for e in ENGS}
        self.seen = {e: {} for e in ENGS}
        self.snap = {}
        self.dma_count = {}
        self.dma_tls = []
        self.last_real = {}
        self.nwaits = 0

    def _deps(self, eng, own_tl, reads, writes):
        deps = []
        for r in reads:
            if r.lw is not None:
                if not (r.lw[0] == own_tl and eng == "pe"):
                    deps.append(r.lw)
            if r.psum:
                for ev in r.readers:
                    if ev[0] != own_tl:
                        deps.append(ev)
        same_ok = own_tl in ("act", "dve", "pool")
        for w in writes:
            if w.lw is not None and (w.lw[0] != own_tl or same_ok):
                deps.append(w.lw)
            for ev in w.readers:
                if ev[0] != own_tl or same_ok:
                    deps.append(ev)
        return deps

    def _resolve(self, eng, deps):
        seen = self.seen[eng]
        best = {}
        for tl, i in deps:
            if best.get(tl, 0) < i:
                best[tl] = i
        waits = [(tl, i) for tl, i in best.items() if seen.get(tl, 0) < i]
        for tl, i in waits:
            if seen.get(tl, 0) < i:
                seen[tl] = i
            sn = self.snap.get((tl, i))
            if sn:
                for t2, i2 in sn.items():
                    if seen.get(t2, 0) < i2:
                        seen[t2] = i2
        for tl, i in waits:
            if tl in self.ops:
                self.ops[tl][i - 1].signal = True
        return waits

    def op(self, eng, meth, *args, R=(), W=(), **kw):
        o = _Op()
        o.eng, o.meth, o.args, o.kw = eng, meth, args, kw
        o.signal = False
        o.dma_tl = None
        lst = self.ops[eng]
        o.idx = len(lst) + 1
        o.waits = self._resolve(eng, self._deps(eng, eng, R, W))
        lst.append(o)
        self.last_real[eng] = o.idx
        ev = (eng, o.idx)
        self.snap[ev] = dict(self.seen[eng])
        for r in R:
            r.readers.append(ev)
        for w in W:
            w.lw = ev
            w.readers = []
        return o

    def dma(self, queue, out, in_, R=(), W=(), **kw):
        o = _Op()
        o.eng, o.meth, o.args, o.kw = queue, "dma_start", (), dict(out=out, in_=in_, **kw)
        o.signal = False
        base = W[0] if W else R[0]
        if base.dma_tl is None:
            base.dma_tl = "dma:" + base.name
        tl = base.dma_tl
        if tl not in self.dma_count:
            self.dma_count[tl] = 0
            self.dma_tls.append(tl)
        o.dma_tl = tl
        lst = self.ops[queue]
        o.idx = len(lst) + 1
        o.waits = self._resolve(queue, self._deps(queue, tl, R, W))
        lst.append(o)
        self.dma_count[tl] += 1
        ev = (tl, self.dma_count[tl])
        self.snap[ev] = dict(self.seen[queue])
        for r in R:
            r.readers.append(ev)
        for w in W:
            w.lw = ev
            w.readers = []
        return o

    def _noop(self, eng, deps):
        o = _Op()
        o.eng, o.meth, o.args, o.kw = eng, None, (), {}
        o.signal = False
        o.dma_tl = None
        lst = self.ops[eng]
        o.idx = len(lst) + 1
        o.waits = self._resolve(eng, deps)
        lst.append(o)

    def final_wait(self, eng, resources):
        deps = []
        for r in resources:
            if r.lw is not None:
                deps.append(r.lw)
            deps.extend(r.readers)
        self._noop(eng, deps)

    def barrier(self):
        targets = [(e, i) for e, i in self.last_real.items() if i]
        targets += [(tl, c) for tl, c in self.dma_count.items() if c]
        for e in ENGS:
            deps = [t for t in targets if not (t[0] == e and e in ("pe", "sp"))]
            self._noop(e, deps)

    def emit(self, stack):
        nc = self.nc
        sems = {}
        for e in ENGS:
            if any(o.signal for o in self.ops[e]):
                sems[e] = stack.enter_context(nc.semaphore("s_" + e))
        for tl in self.dma_tls:
            sems[tl] = stack.enter_context(nc.semaphore("s_" + tl.replace(":", "_")))
        sigval = {}
        for e in ENGS:
            c = 0
            for o in self.ops[e]:
                if o.signal:
                    c += 1
                    sigval[(e, o.idx)] = c
        block = stack.enter_context(nc.Block())

        def run(ekey, eng):
            for o in self.ops[ekey]:
                for tl, i in o.waits:
                    if tl in self.ops:
                        eng.wait_ge(sems[tl], sigval[(tl, i)])
                    else:
                        eng.wait_ge(sems[tl], 16 * i)
                    self.nwaits += 1
                if o.meth is None:
                    continue
                ins = getattr(eng, o.meth)(*o.args, **o.kw)
                if o.dma_tl is not None:
                    ins.then_inc(sems[o.dma_tl], 16)
                elif o.signal:
                    ins.then_inc(sems[ekey], 1)

        @block.tensor
        def _(eng):
            run("pe", eng)

        @block.scalar
        def _(eng):
            run("act", eng)

        @block.vector
        def _(eng):
            run("dve", eng)

        @block.gpsimd
        def _(eng):
            run("pool", eng)

        @block.sync
        def _(eng):
            run("sp", eng)


class Arena:
    def __init__(self, tile, nbytes):
        self.tile = tile
        self.size = nbytes
        self.top = 0

    def mark(self):
        return self.top

    def release(self, m):
        self.top = m

    def alloc(self, nbytes, dt=BF16):
        nbytes = (nbytes + 63) // 64 * 64
        off = self.top
        assert off + nbytes <= self.size, ("arena overflow", off, nbytes, self.size)
        self.top += nbytes
        v = self.tile[:, off // 2:(off + nbytes) // 2]
        if dt == F32:
            v = v.bitcast(F32)
        return v


def bcast_mid(ap2, n):
    a = ap2.ap
    return bass.AP(ap2.tensor, ap2.offset, [list(a[0]), [0, n], list(a[1])])


def _t5_bucket(rel):
    nb = 16
    max_exact = 8
    base = np.where(rel > 0, nb, 0)
    n = np.abs(rel)
    nf = np.maximum(n, 1).astype(np.float32)
    large = max_exact + (np.log(nf / np.float32(max_exact)) / np.float32(math.log(1024 / max_exact))
                         * np.float32(nb - max_exact)).astype(np.int32)
    large = np.minimum(large, nb - 1)
    return base + np.where(n < max_exact, n, large)


def _rope_tables():
    def inv_freq(dim):
        return (10000.0 ** (-np.arange(0, dim, 2, dtype=np.float32) / np.float32(dim))).astype(np.float32)

    t = np.arange(S)
    row = (t // 64).astype(np.float32)
    col = (t % 64).astype(np.float32)
    f16 = inv_freq(32)
    ang_ax = np.concatenate([row[:, None] * f16[None, :], col[:, None] * f16[None, :]], axis=-1)
    cosA = np.cos(ang_ax).astype(np.float32)
    sinA = np.sin(ang_ax).astype(np.float32)
    ropeA = np.zeros((2, 128, S), np.float32)
    for p in range(128):
        dd = p % 64
        ropeA[0, p] = cosA[:, dd % 32]
        ropeA[1, p] = (-1.0 if dd < 32 else 1.0) * sinA[:, dd % 32]
    ang_c = t.astype(np.float32)[:, None] * inv_freq(32)[None, :]
    cosC = np.cos(ang_c).astype(np.float32)
    sinC = np.sin(ang_c).astype(np.float32)
    ropeC = np.zeros((2, 128, S), np.float32)
    for p in range(64, 96):
        dd = p - 64
        ropeC[0, p] = cosC[:, dd % 16]
        ropeC[1, p] = (-1.0 if dd < 16 else 1.0) * sinC[:, dd % 16]
    return ropeA, ropeC


B_DIL = (1, 4, 16)


def _pack_shared(inp):
    f = lambda a: np.ascontiguousarray(np.asarray(a, dtype=np.float32))
    sh = {}
    ropeA, ropeC = _rope_tables()
    sh["ropeA"] = ropeA
    sh["ropeC"] = ropeC
    sh["ident"] = np.eye(128, dtype=np.float32)
    nm = f(inp["norm_mix_g"])
    nf = f(inp["norm_mlp_g"])
    gq = f(inp["a_q_norm_g"])[0]
    gk = f(inp["a_k_norm_g"])[0]
    perm64 = np.concatenate([np.arange(32, 64), np.arange(0, 32)])
    cols = []
    for v in (nm[0], nm[1], nf[0], nf[1]):
        cols.append(v.reshape(8, 128).T)
    pidx = np.arange(128) % 64
    cols.append(np.stack([gq[pidx], gq[perm64][pidx], gk[pidx], gk[perm64][pidx]], axis=1))
    cols.append(f(inp["c_q_norm_g"])[0].reshape(2, 128).T)
    cols.append(f(inp["c_kv_norm_g"])[0].reshape(1, 128).T)
    cols.append(f(inp["final_norm_g"]).reshape(8, 128).T)
    sh["gvec"] = f(np.concatenate(cols, axis=1))
    Win = f(inp["ab_w_in"])[0]
    Wk = Win.reshape(8, 128, 2304).transpose(1, 0, 2)
    wa = np.zeros((2, 128, 8, 832), np.float32)
    for g in range(2):
        q = Wk[:, :, g * 256:(g + 1) * 256]
        qp = q.reshape(128, 8, 4, 64)[:, :, :, perm64].reshape(128, 8, 256)
        k = Wk[:, :, 512 + g * 64:512 + (g + 1) * 64]
        kp = k[:, :, perm64]
        v = Wk[:, :, 640 + g * 64:640 + (g + 1) * 64]
        wa[g] = np.concatenate([q, qp, k, k, kp, kp, v], axis=2)
    sh["wa"] = wa
    wb = np.zeros((4, 128, 8, 384), np.float32)
    for hp in range(4):
        wb[hp] = np.concatenate([Wk[:, :, 768 + hp * 128:768 + (hp + 1) * 128],
                                 Wk[:, :, 1280 + hp * 128:1280 + (hp + 1) * 128],
                                 Wk[:, :, 1792 + hp * 128:1792 + (hp + 1) * 128]], axis=2)
    sh["wb"] = wb
    sh["wo0"] = f(f(inp["ab_w_out"])[0].reshape(8, 128, 1024).transpose(1, 0, 2))
    sh["wo1"] = f(f(inp["c_w_out"])[0].reshape(8, 128, 1024).transpose(1, 0, 2))
    w1 = f(inp["mlp_w1"])
    w2 = f(inp["mlp_w2"])
    sh["w1"] = f(w1.reshape(2, 8, 128, 4, 1024).transpose(0, 3, 2, 1, 4))
    sh["w2"] = f(w2.reshape(2, 4, 8, 128, 1024).transpose(0, 1, 3, 2, 4))
    rb = f(inp["rel_bias"])
    k_i = np.arange(128)[:, None]
    v_i = np.arange(384)[None, :]
    xrel = k_i - v_i + 128
    valid = np.abs(xrel) <= 64
    tb = np.full((8, 128, 3, 384), NEG, np.float32)
    for bi, d in enumerate(B_DIL):
        bucket = _t5_bucket(np.clip(xrel, -64, 64) * d)
        for h in range(8):
            tb[h, :, bi, :] = np.where(valid, rb[bucket, h], np.float32(NEG))
    sh["tb"] = tb
    Wd = f(inp["c_w_down"])[0]
    perm32 = np.concatenate([np.arange(16, 32), np.arange(0, 16)])
    kr = Wd[:, 384:416]
    Wd2 = np.concatenate([Wd, kr[:, perm32]], axis=1)
    sh["wdn"] = f(Wd2.reshape(8, 128, 448).transpose(1, 0, 2))
    Wq = f(inp["c_w_uq"])[0].reshape(2, 128, 16, 96).transpose(1, 0, 2, 3)
    wuq = np.concatenate([Wq, Wq[:, :, :, 64:96][:, :, :, perm32]], axis=3)
    sh["wuq"] = f(wuq)
    Wkv = f(inp["c_w_ukv"])[0].reshape(128, 16, 128)
    sh["wukv"] = f(np.concatenate([Wkv[:, :, 0:64].reshape(128, 1024), Wkv[:, :, 64:128].reshape(128, 1024)], axis=1))
    return sh


SHARED_SHAPES = {
    "ropeA": [2, 128, S], "ropeC": [2, 128, S], "ident": [128, 128], "gvec": [128, 47],
    "wa": [2, 128, 8, 832], "wb": [4, 128, 8, 384], "wo0": [128, 8, 1024], "wo1": [128, 8, 1024],
    "w1": [2, 4, 128, 8, 1024], "w2": [2, 4, 128, 8, 1024], "tb": [8, 128, 3, 384],
    "wdn": [128, 8, 448], "wuq": [128, 2, 16, 128], "wukv": [128, 2048],
}


def build(stage="full"):
    nc = bass.Bass("TRN2", target_bir_lowering=False)
    dr = {}
    dr["x"] = nc.dram_tensor("x", [S, D], F32, kind="ExternalInput").ap()
    for k, shp in SHARED_SHAPES.items():
        dr[k] = nc.dram_tensor(k, shp, F32, kind="ExternalInput").ap()
    if stage == "full":
        out_d = nc.dram_tensor("out", [S, D], F32, kind="ExternalOutput").ap()
    else:
        out_d = nc.dram_tensor("out", [128, 8, S], F32, kind="ExternalOutput").ap()

    st = ExitStack()
    P = Prog(nc)

    def sbt(name, shape, dt=F32):
        return st.enter_context(nc.sbuf_tensor("sb_" + name, list(shape), dt))

    hT = sbt("hT", [128, 8, S], F32)
    rH = [[Res("h%d_%d" % (c, t)) for t in range(NB)] for c in range(8)]
    identf = sbt("identf", [128, 128], F32); rIdent = Res("identf")
    gvec = sbt("gvec", [128, 47], F32); rG = Res("gvec")
    cm = sbt("cm", [128, 4, 128], BF16); rCM = Res("cm")
    epst = sbt("epst", [128, 1], F32); rEps = Res("eps")
    onesf = sbt("onesf", [128, 512], F32); rOnes = Res("onesf")
    arena_bytes = (nc.sbuf_bytes_remaining - 256) // 64 * 64
    print("arena bytes", arena_bytes)
    arena_t = sbt("arena", [128, arena_bytes // 2], BF16)
    AR = Arena(arena_t, arena_bytes)
    banks_t = st.enter_context(nc.psum_tensor("banks", [128, 8, 512], F32))
    BK = [banks_t[:, i, :] for i in range(8)]
    rBK = [Res("bank%d" % i, psum=True) for i in range(8)]

    P.dma("sp", identf[:], dr["ident"], W=[rIdent])
    P.dma("sp", gvec[:], dr["gvec"], W=[rG])
    P.op("dve", "memset", epst[:], EPS, W=[rEps])
    P.op("dve", "memset", onesf[:], -1.0, W=[rOnes])
    P.op("pool", "memset", cm[:, 0, :], 1.0 / 1024, W=[rCM])
    P.op("pool", "memset", cm[:, 1, :], 0.0, W=[rCM])
    P.op("pool", "memset", cm[0:64, 1, 0:64], 1.0 / 64, W=[rCM])
    P.op("pool", "memset", cm[64:128, 1, 64:128], 1.0 / 64, W=[rCM])
    P.op("pool", "memset", cm[:, 2, :], 1.0 / 256, W=[rCM])
    P.op("pool", "memset", cm[:, 3, :], 1.0 / 128, W=[rCM])

    blk = lambda t: slice(t * 512, (t + 1) * 512)
    evac_rr = [0]

    def evac(out, in_, R, W):
        evac_rr[0] ^= 1
        if evac_rr[0]:
            P.op("act", "activation", out=out, in_=in_, func=AF.Copy, R=R, W=W)
        else:
            P.op("dve", "tensor_copy", out=out, in_=in_, R=R, W=W)

    def phase_load():
        m = AR.mark()
        xs = [AR.alloc(4096, F32) for _ in range(3)]
        rX = [Res("xs%d" % i) for i in range(3)]
        for i in range(16):
            b = i % 3
            P.dma("sp", xs[b], dr["x"][i * 128:(i + 1) * 128, :], W=[rX[b]])
            t = i // 4
            for half in range(2):
                bk = (2 * i + half) % 8
                for cc in range(4):
                    c = half * 4 + cc
                    P.op("pe", "transpose", BK[bk][:, cc * 128:(cc + 1) * 128], xs[b][:, c * 128:(c + 1) * 128],
                         identf[:], R=[rX[b], rIdent], W=[rBK[bk]])
                evac(hT[:, half * 4:half * 4 + 4, i * 128:(i + 1) * 128],
                     BK[bk].rearrange("p (a b) -> p a b", a=4),
                     R=[rBK[bk]], W=[rH[half * 4 + cc][t] for cc in range(4)])
        P.barrier()
        AR.release(m)

    def rms_block(t, srcs, rsrcs, gcol, cmi, dsts, rdsts, scratch):
        sq, lnv, rstd, rSq, rLn, rRs = scratch
        n = len(srcs)
        bk = 4 + (t % 2)
        for c in range(n):
            eng = "pool" if c % 2 == 0 else "dve"
            P.op(eng, "tensor_tensor", out=sq[:, c, :], in0=srcs[c], in1=srcs[c], op=ALU.mult, R=[rsrcs[c]], W=[rSq[c]])
        for c in range(n):
            P.op("pe", "matmul", BK[bk], cm[:, cmi, :], sq[:, c, :], start=(c == 0), stop=(c == n - 1),
                 R=[rCM, rSq[c]], W=[rBK[bk]])
        P.op("act", "activation", out=lnv, in_=BK[bk], func=AF.Ln, bias=epst[:], scale=1.0,
             R=[rBK[bk], rEps], W=[rLn])
        P.op("act", "activation", out=rstd, in_=lnv, func=AF.Exp, scale=-0.5, R=[rLn], W=[rRs])
        for c in range(n):
            eng = "dve"
            P.op(eng, "scalar_tensor_tensor", out=dsts[c], in0=srcs[c],
                 scalar=gvec[:, gcol + c:gcol + c + 1], in1=rstd, op0=ALU.mult, op1=ALU.mult,
                 R=[rsrcs[c], rG, rRs], W=[rdsts[c]])

    def rmsnorm_fm(src, rSrc, nchunks, gcol, cmi, dst, rDst, scratch):
        for t in range(NB):
            rms_block(t, [src[:, c, blk(t)] for c in range(nchunks)], [rSrc[c][t] for c in range(nchunks)], gcol, cmi,
                      [dst[:, c, blk(t)] for c in range(nchunks)], [rDst[c][t] for c in range(nchunks)], scratch)

    nsc = [0]

    def norm_scratch():
        nsc[0] += 1
        sq = AR.alloc(8 * 512 * 2, BF16).rearrange("p (c n) -> p c n", c=8)
        lnv = AR.alloc(2048, F32)
        rstd = AR.alloc(2048, F32)
        k = nsc[0]
        return sq, lnv, rstd, [Res("n%d_sq%d" % (k, c)) for c in range(8)], Res("n%d_ln" % k), Res("n%d_rs" % k)

    def run_attention(steps, scale, PT2, rPT2, spairs, bg=None, bg_every=4):
        n = len(steps)
        npair = len(spairs)
        la = npair - 1

        def qk(s_):
            sa = spairs[s_ % npair]
            for l, ln in enumerate(steps[s_][0]):
                P.op("pe", "matmul", BK[sa + l], ln[0], ln[1], start=True, stop=True,
                     R=[ln[2], ln[3]], W=[rBK[sa + l]])

        def ex(s_):
            sa = spairs[s_ % npair]
            k = s_ % len(PT2)
            P.op("act", "activation", out=PT2[k].rearrange("p (a n) -> p a n", a=2), in_=banks_t[:, sa:sa + 2, :],
                 func=AF.Exp, scale=scale, R=[rBK[sa], rBK[sa + 1]], W=[rPT2[k]])

        def pv(s_):
            k = s_ % len(PT2)
            for l, ln in enumerate(steps[s_][0]):
                P.op("pe", "matmul", BK[ln[6]], ln[4], PT2[k][:, l * 512:(l + 1) * 512], start=ln[7], stop=ln[8],
                     R=[ln[5], rPT2[k]], W=[rBK[ln[6]]])
            if steps[s_][1] is not None:
                steps[s_][1]()

        for s_ in range(min(la, n)):
            qk(s_)
        for s_ in range(n):
            ex(s_)
            if s_ + la < n:
                qk(s_ + la)
            pv(s_)
            if bg is not None and s_ % bg_every == bg_every - 1:
                next(bg, None)
        if bg is not None:
            for _ in bg:
                pass

    def normalize(ob, osel, dst, rDst, rec, rRec):
        orow = slice(0, 64) if osel == 0 else slice(64, 128)
        lrow = slice(64, 128) if osel == 0 else slice(0, 64)
        lc, rc = rec
        P.op("dve", "reciprocal", out=rc[lrow, :], in_=BK[ob][lrow, :], R=[rBK[ob]], W=[rRec[1]])
        P.op("dve", "tensor_tensor", out=dst[orow, :], in0=BK[ob][orow, :], in1=rc[lrow, :],
             op=ALU.mult, R=[rBK[ob], rRec[1]], W=[rDst])

    def out_proj(wo_name, OT, rOT):
        m = AR.mark()
        wo = AR.alloc(8 * 1024 * 2, BF16).rearrange("p (c n) -> p c n", c=8)
        rWo = Res("wo_" + wo_name)
        for q in range(4):
            P.dma("pool", wo[:, 2 * q:2 * q + 2, :], dr[wo_name][:, 2 * q:2 * q + 2, :], W=[rWo])
        k = 0
        for t in range(NB):
            for dc in range(8):
                bk = k % 4
                k += 1
                for fc in range(8):
                    P.op("pe", "matmul", BK[bk], wo[:, fc, dc * 128:(dc + 1) * 128], OT[:, fc, blk(t)],
                         start=(fc == 0), stop=(fc == 7), R=[rWo, rOT[fc][t]], W=[rBK[bk]])
                P.op("dve", "tensor_tensor", out=hT[:, dc, blk(t)], in0=BK[bk], in1=hT[:, dc, blk(t)], op=ALU.add,
                     R=[rBK[bk], rH[dc][t]], W=[rH[dc][t]])
        P.barrier()
        AR.release(m)

    def mlp(layer, gcol):
        m = AR.mark()
        hn = AR.alloc(8 * S * 2, BF16).rearrange("p (c n) -> p c n", c=8)
        rHn = [[Res("mhn%d_%d" % (c, t)) for t in range(NB)] for c in range(8)]
        w1b = [AR.alloc(8 * 1024 * 2, BF16).rearrange("p (c n) -> p c n", c=8) for _ in range(2)]
        w2b = [AR.alloc(8 * 1024 * 2, BF16).rearrange("p (c n) -> p c n", c=8) for _ in range(2)]
        rW1 = [Res("w1b%d" % i) for i in range(2)]
        rW2 = [Res("w2b%d" % i) for i in range(2)]
        rbuf = [AR.alloc(2048, F32) for _ in range(3)]
        rR = [Res("mr%d" % i) for i in range(3)]
        uset = [AR.alloc(8 * 1024, BF16).rearrange("p (c n) -> p c n", c=8) for _ in range(2)]
        rU = [[Res("mu%d_%d" % (i, c)) for c in range(8)] for i in range(2)]

        def load_w(g):
            b = g % 2
            for q in range(2):
                P.dma("pool", w1b[b][:, 4 * q:4 * q + 4, :], dr["w1"][layer, g, :, 4 * q:4 * q + 4, :], W=[rW1[b]])
            for q in range(2):
                P.dma("pool", w2b[b][:, 4 * q:4 * q + 4, :], dr["w2"][layer, g, :, 4 * q:4 * q + 4, :], W=[rW2[b]])

        load_w(0)
        load_w(1)
        scratch = norm_scratch()
        rmsnorm_fm(hT, rH, 8, gcol, 0, hn, rHn, scratch)
        its = [(g, t) for g in range(4) for t in range(NB)]
        rcnt = [0]
        pcnt = [0]

        def up(k):
            g, t = its[k]
            b = g % 2
            us, rUs = uset[k % 2], rU[k % 2]
            for fcl in range(8):
                bk = 6 + (fcl % 2)
                for kc in range(8):
                    P.op("pe", "matmul", BK[bk], w1b[b][:, kc, fcl * 128:(fcl + 1) * 128], hn[:, kc, blk(t)],
                         start=(kc == 0), stop=(kc == 7), R=[rW1[b], rHn[kc][t]], W=[rBK[bk]])
                ri = rcnt[0] % 3
                rcnt[0] += 1
                P.op("act", "activation", out=rbuf[ri], in_=BK[bk], func=AF.Relu, R=[rBK[bk]], W=[rR[ri]])
                eng = "pool" if fcl % 2 == 0 else "dve"
                P.op(eng, "tensor_tensor", out=us[:, fcl, :], in0=rbuf[ri], in1=rbuf[ri], op=ALU.mult, R=[rR[ri]], W=[rUs[fcl]])

        def down(k):
            g, t = its[k]
            b = g % 2
            us, rUs = uset[k % 2], rU[k % 2]
            for dcl in ((0, 1, 2), (3, 4, 5), (6, 7)):
                banks = (0, 1, 2) if pcnt[0] % 2 == 0 else (3, 4, 5)
                pcnt[0] += 1
                for di, dc in enumerate(dcl):
                    bk = banks[di]
                    for fcl in range(8):
                        P.op("pe", "matmul", BK[bk], w2b[b][:, fcl, dc * 128:(dc + 1) * 128], us[:, fcl, :],
                             start=(fcl == 0), stop=(fcl == 7), R=[rW2[b], rUs[fcl]], W=[rBK[bk]])
                    P.op("dve", "tensor_tensor", out=hT[:, dc, blk(t)], in0=BK[bk], in1=hT[:, dc, blk(t)], op=ALU.add,
                         R=[rBK[bk], rH[dc][t]], W=[rH[dc][t]])

        up(0)
        for k in range(16):
            if k + 1 < 16:
                up(k + 1)
            down(k)
            g, t = its[k]
            if t == NB - 1 and g + 2 < 4:
                load_w(g + 2)
        P.barrier()
        AR.release(m)

    def layer0_mix():
        m0 = AR.mark()
        OT = AR.alloc(8 * S * 2, BF16).rearrange("p (c n) -> p c n", c=8)
        rOT = [[Res("ot%d_%d" % (c, t)) for t in range(NB)] for c in range(8)]
        hn = AR.alloc(8 * S * 2, BF16).rearrange("p (c n) -> p c n", c=8)
        rHn = [[Res("hn%d_%d" % (c, t)) for t in range(NB)] for c in range(8)]
        PT2 = [AR.alloc(2048, BF16) for _ in range(3)]
        rPT2 = [Res("pt2_%d" % i) for i in range(3)]
        PT = [PT2[i // 2][:, (i % 2) * 512:(i % 2 + 1) * 512] for i in range(4)]
        rPT = [rPT2[i // 2] for i in range(4)]
        rec = [(AR.alloc(2048, F32), AR.alloc(2048, F32)) for _ in range(2)]
        rRec = [(Res("recl%d" % i), Res("recr%d" % i)) for i in range(2)]
        m1 = AR.mark()
        scratch = norm_scratch()
        rmsnorm_fm(hT, rH, 8, 0, 0, hn, rHn, scratch)
        P.barrier()
        AR.release(m1)

        ropeA = AR.alloc(2 * S * 4, F32).rearrange("p (a n) -> p a n", a=2)
        rRope = Res("ropeA")
        P.dma("sp", ropeA[:, 0, :], dr["ropeA"][0], W=[rRope])
        P.dma("sp", ropeA[:, 1, :], dr["ropeA"][1], W=[rRope])
        wA1 = AR.alloc(8 * 832 * 2, BF16).rearrange("p (c n) -> p c n", c=8)
        rWA1 = Res("wA")
        wA = [wA1, wA1]
        rWA = [rWA1, rWA1]
        QA = AR.alloc(2 * S * 2, BF16).rearrange("p (c n) -> p c n", c=2)
        KA = AR.alloc(S * 2, BF16)
        VA = AR.alloc(16 * 192 * 2, BF16).rearrange("p (i n) -> p i n", i=16)
        qf = [AR.alloc(2048, F32) for _ in range(2)]
        sqb = AR.alloc(1024, BF16)
        lnv = AR.alloc(2048, F32)
        rstd = AR.alloc(2048, F32)
        t1 = AR.alloc(2048, F32)
        t2 = AR.alloc(2048, F32)
        rQf = [Res("qf0"), Res("qf1")]
        rSq, rLn, rRs, rT1, rT2 = Res("a_sq"), Res("a_ln"), Res("a_rs"), Res("a_t1"), Res("a_t2")
        rQA = [Res("QA0"), Res("QA1")]
        rKA = Res("KA")
        rVA = Res("VA")
        P.op("pool", "memset", VA[:, :, 0:64], 1.0, W=[rVA])
        P.op("pool", "memset", VA[:, :, 128:192], 1.0, W=[rVA])

        def proj_norm_rope(w, rW, c0, cp0, gc, dstT, rDst):
            for t in range(NB):
                for kc in range(8):
                    P.op("pe", "matmul", BK[4], w[:, kc, c0:c0 + 128], hn[:, kc, blk(t)], start=(kc == 0), stop=(kc == 7),
                         R=[rW, rHn[kc][t]], W=[rBK[4]])
                for kc in range(8):
                    P.op("pe", "matmul", BK[5], w[:, kc, cp0:cp0 + 128], hn[:, kc, blk(t)], start=(kc == 0), stop=(kc == 7),
                         R=[rW, rHn[kc][t]], W=[rBK[5]])
                P.op("dve", "tensor_copy", out=qf[0], in_=BK[4], R=[rBK[4]], W=[rQf[0]])
                P.op("act", "activation", out=qf[1], in_=BK[5], func=AF.Copy, scale=gvec[:, gc + 1:gc + 2],
                     R=[rBK[5], rG], W=[rQf[1]])
                P.op("pool", "tensor_tensor", out=sqb, in0=qf[0], in1=qf[0], op=ALU.mult, R=[rQf[0]], W=[rSq])
                P.op("pe", "matmul", BK[4], cm[:, 1, :], sqb, start=True, stop=True, R=[rCM, rSq], W=[rBK[4]])
                P.op("act", "activation", out=lnv, in_=BK[4], func=AF.Ln, bias=epst[:], scale=1.0,
                     R=[rBK[4], rEps], W=[rLn])
                P.op("act", "activation", out=rstd, in_=lnv, func=AF.Exp, scale=-0.5, R=[rLn], W=[rRs])
                P.op("dve", "scalar_tensor_tensor", out=t1, in0=qf[0], scalar=gvec[:, gc:gc + 1], in1=ropeA[:, 0, blk(t)],
                     op0=ALU.mult, op1=ALU.mult, R=[rQf[0], rG, rRope], W=[rT1])
                P.op("dve", "tensor_tensor", out=t2, in0=qf[1], in1=ropeA[:, 1, blk(t)], op=ALU.mult, R=[rQf[1], rRope], W=[rT2])
                P.op("pool", "tensor_tensor", out=t1, in0=t1, in1=t2, op=ALU.add, R=[rT1, rT2], W=[rT1])
                P.op("dve", "tensor_tensor", out=dstT[:, blk(t)], in0=t1, in1=rstd, op=ALU.mult, R=[rT1, rRs], W=[rDst])

        for g in range(2):
            w = wA[g]
            for q in range(2):
                P.dma("pool", w[:, 4 * q:4 * q + 4, :], dr["wa"][g, :, 4 * q:4 * q + 4, :], W=[rWA[g]])
            for j in range(2):
                proj_norm_rope(w, rWA[g], j * 128, 256 + j * 128, 32, QA[:, j, :], rQA[j])
            proj_norm_rope(w, rWA[g], 512, 640, 34, KA, rKA)
            for half in range(2):
                bk = 4 + half
                for ii in range(8):
                    i = half * 8 + ii
                    for kc in range(8):
                        P.op("pe", "matmul", BK[bk][:, ii * 64:(ii + 1) * 64], hn[:, kc, i * 128:(i + 1) * 128],
                             w[:, kc, 768:832], start=(kc == 0), stop=(kc == 7), skip_group_check=True,
                             R=[rWA[g], rHn[kc][i // 4]], W=[rBK[bk]])
                evac(VA[:, half * 8:half * 8 + 8, 64:128], BK[bk].rearrange("p (i n) -> p i n", i=8),
                     R=[rBK[bk]], W=[rVA])
            steps = []
            for j in range(2):
                for t in range(NB):
                    obs = (4 + 2 * (t % 2), 5 + 2 * (t % 2))
                    for i in range(16):
                        lanes = []
                        for e in range(2):
                            rows = slice(e * 64, (e + 1) * 64)
                            v = VA[:, i, 64:192] if e == 0 else VA[:, i, 0:128]
                            lanes.append((KA[rows, i * 128:(i + 1) * 128], QA[rows, j, blk(t)], rKA, rQA[j], v, rVA,
                                          obs[e], i == 0, i == 15))
                        post = None
                        if i == 15:
                            def post(j=j, t=t, obs=obs):
                                for e in range(2):
                                    normalize(obs[e], e, OT[:, 2 * g + j, blk(t)], rOT[2 * g + j][t], rec[e], rRec[e])
                        steps.append((lanes, post))
            run_attention(steps, 0.125, PT2, rPT2, (0, 2))
        P.barrier()
        AR.release(m1)

        wB = AR.alloc(8 * 384 * 2, BF16).rearrange("p (c n) -> p c n", c=8)
        rWB = Res("wB")
        QB = AR.alloc(S * 2, BF16)
        KB = AR.alloc(S * 2, BF16)
        VB = AR.alloc(3 * 16 * 192 * 2, BF16).rearrange("p (b i n) -> p b i n", b=3, i=16)
        rQB, rKB, rVB = Res("QB"), Res("KB"), Res("VB")
        TBs = [AR.alloc(3 * 384 * 4, F32).rearrange("p (b n) -> p b n", b=3) for _ in range(2)]
        rTB = [Res("tb0"), Res("tb1")]
        tmp = [AR.alloc(2048, F32) for _ in range(3)]
        rTmp = [Res("btmp%d" % i) for i in range(3)]
        PTb = [AR.alloc(1024, BF16) for _ in range(4)]
        rPTb = [Res("bpt%d" % i) for i in range(4)]
        lnl = AR.alloc(2048, F32)
        rLnl = Res("b_lnl")
        P.op("pool", "memset", VB[:, :, :, 64:128], 1.0, W=[rVB])
        sb_rr = [0]
        tmp_rr = [0]
        pt_rr = [0]
        ob_rr = [0]
        for hp in range(4):
            for q in range(2):
                P.dma("pool", wB[:, 4 * q:4 * q + 4, :], dr["wb"][hp, :, 4 * q:4 * q + 4, :], W=[rWB])
            for e in range(2):
                P.dma("sp", TBs[e], dr["tb"][2 * hp + e], W=[rTB[e]])
            for (c0, dstT, rDst) in ((0, QB, rQB), (128, KB, rKB)):
                for t in range(NB):
                    bk = 5 + (t % 2)
                    for kc in range(8):
                        P.op("pe", "matmul", BK[bk], wB[:, kc, c0:c0 + 128], hn[:, kc, blk(t)], start=(kc == 0), stop=(kc == 7),
                             R=[rWB, rHn[kc][t]], W=[rBK[bk]])
                    evac(dstT[:, blk(t)], BK[bk], R=[rBK[bk]], W=[rDst])
            for bi, d in enumerate(B_DIL):
                nkt = (S // d) // 128
                for grp in range(4):
                    bk = 5 + (grp % 2)
                    for q in range(4):
                        ti = grp * 4 + q
                        r_, kt = ti // nkt, ti % nkt
                        s0 = kt * 128 * d + r_
                        tok = slice(s0, s0 + 127 * d + 1, d)
                        for kc in range(8):
                            P.op("pe", "matmul", BK[bk][:, q * 128:(q + 1) * 128], hn[:, kc, tok], wB[:, kc, 256:384],
                                 start=(kc == 0), stop=(kc == 7), skip_group_check=True,
                                 R=[rWB] + [rHn[kc][tt] for tt in range(NB)], W=[rBK[bk]])
                    dst = VB[:, bi, grp * 4:grp * 4 + 4, :].rearrange("p i (a n) -> p i a n", a=3)[:, :, 0:3:2, :]
                    evac(dst, BK[bk].rearrange("p (i a n) -> p i a n", i=4, a=2), R=[rBK[bk]], W=[rVB])
            flat = []
            for e in range(2):
                for c in range(NB):
                    ob = 6 + (ob_rr[0] % 2)
                    ob_rr[0] += 1
                    jobs = []
                    for di, delta in enumerate((128, 0, -128)):
                        regs = []
                        for qtl in range(4):
                            kt = 4 * c + qtl + delta // 128
                            if 0 <= kt < 16:
                                regs.append((qtl * 128, 128, kt, slice(kt * 128, kt * 128 + 128),
                                             slice(c * 512 + qtl * 128, c * 512 + qtl * 128 + 128),
                                             slice(qtl * 128, qtl * 128 + 128)))
                        if regs:
                            jobs.append((0, regs, slice(128 - delta, 256 - delta), 128))
                    for di, delta in enumerate((128, 0, -128)):
                        kt = c + delta // 128
                        if 0 <= kt < 4:
                            regs = []
                            for r_ in range(4):
                                ks = kt * 512 + r_
                                qs = c * 512 + r_
                                regs.append((r_ * 128, 128, r_ * 4 + kt, slice(ks, ks + 127 * 4 + 1, 4),
                                             slice(qs, qs + 127 * 4 + 1, 4), slice(r_, r_ + 127 * 4 + 1, 4)))
                            jobs.append((1, regs, slice(128 - delta, 256 - delta), 128))
                    regs = []
                    for r_ in range(16):
                        qs = c * 512 + r_
                        regs.append((r_ * 32, 32, r_, slice(r_, r_ + 127 * 16 + 1, 16),
                                     slice(qs, qs + 31 * 16 + 1, 16), slice(r_, r_ + 31 * 16 + 1, 16)))
                    jobs.append((2, regs, slice(128 + c * 32, 128 + c * 32 + 32), 32))
                    for ji, jb in enumerate(jobs):
                        flat.append((jb, ob, e, ji == 0, ji == len(jobs) - 1, c))
            nj = len(flat)
            LA = 3

            def b_qk(j):
                (bi, regs, mcols, w), ob, e, isfirst, islast, c = flat[j]
                rows = slice(e * 64, (e + 1) * 64)
                sb_ = j % 5
                for (col0, w_, ti, ksl, qsl, osl) in regs:
                    P.op("pe", "matmul", BK[sb_][:, col0:col0 + w_], KB[rows, ksl], QB[rows, qsl],
                         start=True, stop=True, skip_group_check=True, R=[rKB, rQB], W=[rBK[sb_]])

            def b_mid(j):
                (bi, regs, mcols, w), ob, e, isfirst, islast, c = flat[j]
                sb_ = j % 5
                c_lo = regs[0][0]
                c_hi = regs[-1][0] + w
                nreg = len(regs)
                tm, rTm = tmp[j % 3], rTmp[j % 3]
                P.op("dve", "scalar_tensor_tensor",
                     out=tm[:, c_lo:c_hi].rearrange("p (a n) -> p a n", a=nreg),
                     in0=BK[sb_][:, c_lo:c_hi].rearrange("p (a n) -> p a n", a=nreg),
                     scalar=0.125, in1=bcast_mid(TBs[e][:, bi, mcols], nreg),
                     op0=ALU.mult, op1=ALU.add, R=[rBK[sb_], rTB[e]], W=[rTm])
                P.op("act", "activation", out=PTb[j % 4][:, c_lo:c_hi], in_=tm[:, c_lo:c_hi], func=AF.Exp,
                     R=[rTm], W=[rPTb[j % 4]])

            def b_pv(j):
                (bi, regs, mcols, w), ob, e, isfirst, islast, c = flat[j]
                vsl = slice(0, 128) if e == 0 else slice(64, 192)
                orow = slice(0, 64) if e == 0 else slice(64, 128)
                lrow = slice(64, 128) if e == 0 else slice(0, 64)
                pt = PTb[j % 4]
                for ri, (col0, w_, ti, ksl, qsl, osl) in enumerate(regs):
                    P.op("pe", "matmul", BK[ob][:, osl], VB[:, bi, ti, vsl], pt[:, col0:col0 + w_],
                         start=(isfirst and ri == 0), stop=(islast and ri == len(regs) - 1), skip_group_check=True,
                         R=[rVB, rPTb[j % 4]], W=[rBK[ob]])
                if islast:
                    rr = c % 2
                    P.op("act", "activation", out=lnl[lrow, :], in_=BK[ob][lrow, :], func=AF.Ln, R=[rBK[ob]], W=[rLnl])
                    P.op("act", "activation", out=rec[rr][1][lrow, :], in_=lnl[lrow, :], func=AF.Exp, scale=-1.0,
                         R=[rLnl], W=[rRec[rr][1]])
                    P.op("dve", "tensor_tensor", out=OT[orow, 4 + hp, blk(c)], in0=BK[ob][orow, :], in1=rec[rr][1][lrow, :],
                         op=ALU.mult, R=[rBK[ob], rRec[rr][1]], W=[rOT[4 + hp][c]])

            for j in range(min(LA, nj)):
                b_qk(j)
            for j in range(nj):
                b_mid(j)
                if j + LA < nj:
                    b_qk(j + LA)
                b_pv(j)
        P.barrier()
        AR.release(m1)
        out_proj("wo0", OT, rOT)
        AR.release(m0)

    def layer1_mix():
        m0 = AR.mark()
        PT2 = [AR.alloc(2048, BF16) for _ in range(3)]
        rPT2 = [Res("c_pt2_%d" % i) for i in range(3)]
        rec = [(AR.alloc(2048, F32), AR.alloc(2048, F32)) for _ in range(2)]
        rRec = [(Res("c_recl%d" % i), Res("c_recr%d" % i)) for i in range(2)]
        cqn = AR.alloc(2 * S * 2, BF16).rearrange("p (c n) -> p c n", c=2)
        ckvn = AR.alloc(S * 2, BF16).rearrange("p (c n) -> p c n", c=1)
        krT = AR.alloc(S * 2, BF16)
        rCqn = [[Res("cqn%d_%d" % (c, t)) for t in range(NB)] for c in range(2)]
        rCkvn = [[Res("ckvn_%d" % t) for t in range(NB)]]
        rKr = Res("krT")
        ropeC = AR.alloc(2 * S * 4, F32).rearrange("p (a n) -> p a n", a=2)
        rRope = Res("ropeC")
        P.dma("sp", ropeC[:, 0, :], dr["ropeC"][0], W=[rRope])
        P.dma("sp", ropeC[:, 1, :], dr["ropeC"][1], W=[rRope])
        wuq = AR.alloc(2 * 16 * 128 * 2, BF16).rearrange("p (c h n) -> p c h n", c=2, h=16)
        wukv = AR.alloc(2048 * 2, BF16)
        rWuq, rWukv = Res("wuq"), Res("wukv")
        for c in range(2):
            P.dma("pool", wuq[:, c], dr["wuq"][:, c], W=[rWuq])
        P.dma("pool", wukv, dr["wukv"], W=[rWukv])
        t1 = AR.alloc(2048, F32)
        t2 = AR.alloc(2048, F32)
        rT1, rT2 = Res("c_t1"), Res("c_t2")
        m1 = AR.mark()
        hn = AR.alloc(8 * S * 2, BF16).rearrange("p (c n) -> p c n", c=8)
        rHn = [[Res("c_hn%d_%d" % (c, t)) for t in range(NB)] for c in range(8)]
        wdn = AR.alloc(8 * 448 * 2, BF16).rearrange("p (c n) -> p c n", c=8)
        rWdn = Res("wdn")
        for q in range(2):
            P.dma("pool", wdn[:, 4 * q:4 * q + 4, :], dr["wdn"][:, 4 * q:4 * q + 4, :], W=[rWdn])
        cblk = [AR.alloc(3 * 2048, F32).rearrange("p (c n) -> p c n", c=3) for _ in range(2)]
        rCb = [[Res("cb%d_%d" % (i, c)) for c in range(3)] for i in range(2)]
        scratch = norm_scratch()
        rmsnorm_fm(hT, rH, 8, 8, 0, hn, rHn, scratch)
        rr_ = slice(64, 96)
        for t in range(NB):
            cb = cblk[t % 2]
            rC = rCb[t % 2]
            for ci in range(3):
                c0 = ci * 128
                bk = 6 + (ci % 2)
                for kc in range(8):
                    P.op("pe", "matmul", BK[bk], wdn[:, kc, c0:c0 + 128], hn[:, kc, blk(t)], start=(kc == 0), stop=(kc == 7),
                         R=[rWdn, rHn[kc][t]], W=[rBK[bk]])
                evac(cb[:, ci, :], BK[bk], R=[rBK[bk]], W=[rC[ci]])
            for (c0, bk) in ((384, 2), (416, 3)):
                for kc in range(8):
                    P.op("pe", "matmul", BK[bk][64:96, :], wdn[:, kc, c0:c0 + 32], hn[:, kc, blk(t)],
                         start=(kc == 0), stop=(kc == 7), R=[rWdn, rHn[kc][t]], W=[rBK[bk]])
            P.op("dve", "tensor_tensor", out=t1[rr_, :], in0=BK[2][rr_, :], in1=ropeC[rr_, 0, blk(t)], op=ALU.mult,
                 R=[rBK[2], rRope], W=[rT1])
            P.op("dve", "tensor_tensor", out=t2[rr_, :], in0=BK[3][rr_, :], in1=ropeC[rr_, 1, blk(t)], op=ALU.mult,
                 R=[rBK[3], rRope], W=[rT2])
            P.op("pool", "tensor_tensor", out=krT[rr_, blk(t)], in0=t1[rr_, :], in1=t2[rr_, :], op=ALU.add,
                 R=[rT1, rT2], W=[rKr])
            rms_block(t, [cb[:, 0, :], cb[:, 1, :]], [rC[0], rC[1]], 36, 2,
                      [cqn[:, 0, blk(t)], cqn[:, 1, blk(t)]], [rCqn[0][t], rCqn[1][t]], scratch)
            rms_block(t, [cb[:, 2, :]], [rC[2]], 38, 3, [ckvn[:, 0, blk(t)]], [rCkvn[0][t]], scratch)
        P.barrier()
        AR.release(m1)
        OT = AR.alloc(8 * S * 2, BF16).rearrange("p (c n) -> p c n", c=8)
        rOT = [[Res("c_ot%d_%d" % (c, t)) for t in range(NB)] for c in range(8)]
        m2 = AR.mark()
        sets = []
        for si in range(2):
            QT = AR.alloc(2 * S * 2, BF16).rearrange("p (h n) -> p h n", h=2)
            KT = AR.alloc(2 * S * 2, BF16).rearrange("p (h n) -> p h n", h=2)
            VC = AR.alloc(16 * 192 * 2, BF16).rearrange("p (i n) -> p i n", i=16)
            rQT = [Res("QT%d_%d" % (si, i)) for i in range(2)]
            rKT = [Res("KT%d_%d" % (si, i)) for i in range(2)]
            rVC = Res("VC%d" % si)
            P.op("pool", "memset", VC[:, :, 64:128], 1.0, W=[rVC])
            sets.append((QT, KT, VC, rQT, rKT, rVC))
        t1s = [t1, t1]
        t2s = [t2, t2]
        rT1s = [rT1, rT1]
        rT2s = [rT2, rT2]
        cscale = 96.0 ** -0.5
        ucnt = [0]

        def proj_group(grp, si):
            QT, KT, VC, rQT, rKT, rVC = sets[si]
            for hl in range(2):
                h = 2 * grp + hl
                for t in range(NB):
                    u = ucnt[0] % 2
                    ucnt[0] += 1
                    for kc in range(2):
                        P.op("pe", "matmul", BK[4][0:96, :], wuq[:, kc, h, 0:96], cqn[:, kc, blk(t)], start=(kc == 0), stop=(kc == 1),
                             R=[rWuq, rCqn[kc][t]], W=[rBK[4]])
                    for kc in range(2):
                        P.op("pe", "matmul", BK[5][64:96, :], wuq[:, kc, h, 96:128], cqn[:, kc, blk(t)], start=(kc == 0), stop=(kc == 1),
                             R=[rWuq, rCqn[kc][t]], W=[rBK[5]])
                    P.op("pe", "matmul", BK[5][0:64, :], wukv[:, h * 64:(h + 1) * 64], ckvn[:, 0, blk(t)], start=True, stop=True,
                         R=[rWukv, rCkvn[0][t]], W=[rBK[5]])
                    P.op("dve", "tensor_copy", out=QT[0:64, hl, blk(t)], in_=BK[4][0:64, :], R=[rBK[4]], W=[rQT[hl]])
                    P.op("dve", "tensor_tensor", out=t1s[u][rr_, :], in0=BK[4][rr_, :], in1=ropeC[rr_, 0, blk(t)], op=ALU.mult,
                         R=[rBK[4], rRope], W=[rT1s[u]])
                    P.op("dve", "tensor_tensor", out=t2s[u][rr_, :], in0=BK[5][rr_, :], in1=ropeC[rr_, 1, blk(t)], op=ALU.mult,
                         R=[rBK[5], rRope], W=[rT2s[u]])
                    P.op("dve", "tensor_copy", out=KT[0:64, hl, blk(t)], in_=BK[5][0:64, :], R=[rBK[5]], W=[rKT[hl]])
                    P.op("pool", "tensor_tensor", out=QT[rr_, hl, blk(t)], in0=t1s[u][rr_, :], in1=t2s[u][rr_, :], op=ALU.add,
                         R=[rT1s[u], rT2s[u]], W=[rQT[hl]])
                    yield
                P.op("dve", "tensor_copy", out=KT[rr_, hl, :], in_=krT[rr_, :], R=[rKr], W=[rKT[hl]])
                yield
            for i4 in range(4):
                for q in range(4):
                    i = i4 * 4 + q
                    P.op("pe", "matmul", BK[4][:, q * 128:(q + 1) * 128], ckvn[:, 0, i * 128:(i + 1) * 128],
                         wukv[:, 1024 + grp * 128:1024 + (grp + 1) * 128], start=True, stop=True,
                         skip_group_check=True, R=[rWukv, rCkvn[0][i // 4]], W=[rBK[4]])
                dst = VC[:, i4 * 4:i4 * 4 + 4, :].rearrange("p i (b n) -> p i b n", b=3)[:, :, 0:3:2, :]
                P.op("dve", "tensor_copy", out=dst, in_=BK[4].rearrange("p (i b n) -> p i b n", i=4, b=2),
                     R=[rBK[4]], W=[rVC])
                yield

        def attn_steps(grp, si):
            QT, KT, VC, rQT, rKT, rVC = sets[si]
            steps = []
            for hl in range(2):
                h = 2 * grp + hl
                e = hl
                for t in range(NB):
                    ob = 6 + (t % 2)
                    for i2 in range(8):
                        lanes = []
                        for l in range(2):
                            i = 2 * i2 + l
                            v = VC[:, i, 0:128] if e == 0 else VC[:, i, 64:192]
                            lanes.append((KT[0:96, hl, i * 128:(i + 1) * 128], QT[0:96, hl, blk(t)], rKT[hl], rQT[hl], v, rVC,
                                          ob, i == 0, i == 15))
                        post = None
                        if i2 == 7:
                            def post(h=h, e=e, t=t, ob=ob):
                                normalize(ob, e, OT[:, h // 2, blk(t)], rOT[h // 2][t], rec[t % 2], rRec[t % 2])
                        steps.append((lanes, post))
            return steps

        import os
        MODE = os.environ.get("MLA_MODE", "seq")
        for _ in proj_group(0, 0):
            pass
        for grp in range(8):
            bg = proj_group(grp + 1, (grp + 1) % 2) if grp + 1 < 8 else None
            if MODE == "bg":
                run_attention(attn_steps(grp, grp % 2), cscale, PT2, rPT2, (0, 2), bg=bg, bg_every=4)
            elif MODE == "seq":
                run_attention(attn_steps(grp, grp % 2), cscale, PT2, rPT2, (0, 2))
                if bg is not None:
                    for _ in bg:
                        pass
            elif MODE == "projonly":
                if bg is not None:
                    for _ in bg:
                        pass
        P.barrier()
        AR.release(m2)
        out_proj("wo1", OT, rOT)
        AR.release(m0)

    def final_store():
        m = AR.mark()
        yT = AR.alloc(8 * S * 4, F32).rearrange("p (c n) -> p c n", c=8)
        rY = [[Res("y%d_%d" % (c, t)) for t in range(NB)] for c in range(8)]
        scratch = norm_scratch()
        rmsnorm_fm(hT, rH, 8, 39, 0, yT, rY, scratch)
        ys = [AR.alloc(4096, F32) for _ in range(3)]
        rYs = [Res("ys%d" % i) for i in range(3)]
        rOut = [Res("out%d" % i) for i in range(3)]
        for i in range(16):
            b = i % 3
            t = i // 4
            for half in range(2):
                bk = (2 * i + half) % 8
                for cc in range(4):
                    c = half * 4 + cc
                    P.op("pe", "transpose", BK[bk][:, cc * 128:(cc + 1) * 128], yT[:, c, i * 128:(i + 1) * 128],
                         identf[:], R=[rY[c][t], rIdent], W=[rBK[bk]])
                evac(ys[b][:, half * 512:(half + 1) * 512], BK[bk], R=[rBK[bk]], W=[rYs[b]])
            P.dma("sp", out_d[i * 128:(i + 1) * 128, :], ys[b], R=[rYs[b]], W=[rOut[b]])
        P.final_wait("sp", rOut)
        AR.release(m)

    def dump_h():
        rOut = Res("out")
        for c in range(8):
            P.dma("sp", out_d[:, c, :], hT[:, c, :], R=[rH[c][t] for t in range(NB)], W=[rOut])
        P.final_wait("sp", [rOut])

    phase_load()
    stages = ["x", "l0mix", "l0mlp", "l1mix", "l1mlp", "full"]
    si = stages.index(stage)
    if si >= 1:
        layer0_mix()
    if si >= 2:
        mlp(0, 16)
    if si >= 3:
        layer1_mix()
    if si >= 4:
        mlp(1, 24)
    if stage == "full":
        final_store()
    else:
        dump_h()
    P.emit(st)
    st.close()
    print('ops', {e: len(v) for e, v in P.ops.items()}, 'waits', P.nwaits)
    return nc


_CACHE = {}


def run(inputs, stage="full", ncores=8):
    if stage not in _CACHE:
        _CACHE[stage] = build(stage)
    nc = _CACHE[stage]
    x = np.asarray(inputs["x"], dtype=np.float32)
    sh = _pack_shared(inputs)
    in_maps = []
    for b in range(ncores):
        m = {"x": np.ascontiguousarray(x[b])}
        m.update(sh)
        in_maps.append(m)
    res = run_bass_kernel_spmd(nc, in_maps, core_ids=list(range(ncores)))
    return [r["out"] for r in res.results]


def kernel(**inputs):
    outs = run(inputs, "full")
    return np.stack(outs, axis=0).astype(np.float32)
```

```python
import math
import numpy as np
from contextlib import ExitStack
import concourse.bass as bass
import concourse.mybir as mybir
from concourse.bass_utils import run_bass_kernel_spmd

F32 = mybir.dt.float32
BF16 = mybir.dt.bfloat16
ALU = mybir.AluOpType
AF = mybir.ActivationFunctionType

S = 2048
D = 1024
NB = 4
NEG = -30000.0
EPS = 1e-6

ENGS = ("pe", "act", "dve", "pool", "sp")


class Res:
    __slots__ = ("name", "psum", "lw", "readers", "dma_tl")

    def __init__(self, name, psum=False):
        self.name = name
        self.psum = psum
        self.lw = None
        self.readers = []
        self.dma_tl = None


class _Op:
    __slots__ = ("eng", "meth", "args", "kw", "waits", "signal", "idx", "dma_tl")


class Prog:
    def __init__(self, nc):
        self.nc = nc
        self.ops = {e: [] for e in ENGS}
        self.seen = {e: {} for e in ENGS}
        self.snap = {}
        self.dma_count = {}
        self.dma_tls = []
        self.last_real = {}
        self.nwaits = 0

    def _deps(self, eng, own_tl, reads, writes):
        deps = []
        for r in reads:
            if r.lw is not None:
                if not (r.lw[0] == own_tl and eng == "pe"):
                    deps.append(r.lw)
            if r.psum:
                for ev in r.readers:
                    if ev[0] != own_tl:
                        deps.append(ev)
        same_ok = own_tl in ("act", "dve", "pool")
        for w in writes:
            if w.lw is not None and (w.lw[0] != own_tl or same_ok):
                deps.append(w.lw)
            for ev in w.readers:
                if ev[0] != own_tl or same_ok:
                    deps.append(ev)
        return deps

    def _resolve(self, eng, deps):
        seen = self.seen[eng]
        best = {}
        for tl, i in deps:
            if best.get(tl, 0) < i:
                best[tl] = i
        waits = [(tl, i) for tl, i in best.items() if seen.get(tl, 0) < i]
        for tl, i in waits:
            if seen.get(tl, 0) < i:
                seen[tl] = i
            sn = self.snap.get((tl, i))
            if sn:
                for t2, i2 in sn.items():
                    if seen.get(t2, 0) < i2:
                        seen[t2] = i2
        for tl, i in waits:
            if tl in self.ops:
                self.ops[tl][i - 1].signal = True
        return waits

    def op(self, eng, meth, *args, R=(), W=(), **kw):
        o = _Op()
        o.eng, o.meth, o.args, o.kw = eng, meth, args, kw
        o.signal = False
        o.dma_tl = None
        lst = self.ops[eng]
        o.idx = len(lst) + 1
        o.waits = self._resolve(eng, self._deps(eng, eng, R, W))
        lst.append(o)
        self.last_real[eng] = o.idx
        ev = (eng, o.idx)
        self.snap[ev] = dict(self.seen[eng])
        for r in R:
            r.readers.append(ev)
        for w in W:
            w.lw = ev
            w.readers = []
        return o

    def dma(self, queue, out, in_, R=(), W=(), **kw):
        o = _Op()
        o.eng, o.meth, o.args, o.kw = queue, "dma_start", (), dict(out=out, in_=in_, **kw)
        o.signal = False
        base = W[0] if W else R[0]
        if base.dma_tl is None:
            base.dma_tl = "dma:" + base.name
        tl = base.dma_tl
        if tl not in self.dma_count:
            self.dma_count[tl] = 0
            self.dma_tls.append(tl)
        o.dma_tl = tl
        lst = self.ops[queue]
        o.idx = len(lst) + 1
        o.waits = self._resolve(queue, self._deps(queue, tl, R, W))
        lst.append(o)
        self.dma_count[tl] += 1
        ev = (tl, self.dma_count[tl])
        self.snap[ev] = dict(self.seen[queue])
        for r in R:
            r.readers.append(ev)
        for w in W:
            w.lw = ev
            w.readers = []
        return o

    def _noop(self, eng, deps):
        o = _Op()
        o.eng, o.meth, o.args, o.kw = eng, None, (), {}
        o.signal = False
        o.dma_tl = None
        lst = self.ops[eng]
        o.idx = len(lst) + 1
        o.waits = self._resolve(eng, deps)
        lst.append(o)

    def final_wait(self, eng, resources):
        deps = []
        for r in resources:
            if r.lw is not None:
                deps.append(r.lw)
            deps.extend(r.readers)
        self._noop(eng, deps)

    def barrier(self):
        targets = [(e, i) for e, i in self.last_real.items() if i]
        targets += [(tl, c) for tl, c in self.dma_count.items() if c]
        for e in ENGS:
            deps = [t for t in targets if not (t[0] == e and e in ("pe", "sp"))]
            self._noop(e, deps)

    def emit(self, stack):
        nc = self.nc
        sems = {}
        for e in ENGS:
            if any(o.signal for o in self.ops[e]):
                sems[e] = stack.enter_context(nc.semaphore("s_" + e))
        for tl in self.dma_tls:
            sems[tl] = stack.enter_context(nc.semaphore("s_" + tl.replace(":", "_")))
        sigval = {}
        for e in ENGS:
            c = 0
            for o in self.ops[e]:
                if o.signal:
                    c += 1
                    sigval[(e, o.idx)] = c
        block = stack.enter_context(nc.Block())

        def run(ekey, eng):
            for o in self.ops[ekey]:
                for tl, i in o.waits:
                    if tl in self.ops:
                        eng.wait_ge(sems[tl], sigval[(tl, i)])
                    else:
                        eng.wait_ge(sems[tl], 16 * i)
                    self.nwaits += 1
                if o.meth is None:
                    continue
                ins = getattr(eng, o.meth)(*o.args, **o.kw)
                if o.dma_tl is not None:
                    ins.then_inc(sems[o.dma_tl], 16)
                elif o.signal:
                    ins.then_inc(sems[ekey], 1)

        @block.tensor
        def _(eng):
            run("pe", eng)

        @block.scalar
        def _(eng):
            run("act", eng)

        @block.vector
        def _(eng):
            run("dve", eng)

        @block.gpsimd
        def _(eng):
            run("pool", eng)

        @block.sync
        def _(eng):
            run("sp", eng)


class Arena:
    def __init__(self, tile, nbytes):
        self.tile = tile
        self.size = nbytes
        self.top = 0

    def mark(self):
        return self.top

    def release(self, m):
        self.top = m

    def alloc(self, nbytes, dt=BF16):
        nbytes = (nbytes + 63) // 64 * 64
        off = self.top
        assert off + nbytes <= self.size, ("arena overflow", off, nbytes, self.size)
        self.top += nbytes
        v = self.tile[:, off // 2:(off + nbytes) // 2]
        if dt == F32:
            v = v.bitcast(F32)
        return v


def bcast_mid(ap2, n):
    a = ap2.ap
    return bass.AP(ap2.tensor, ap2.offset, [list(a[0]), [0, n], list(a[1])])


def _t5_bucket(rel):
    nb = 16
    max_exact = 8
    base = np.where(rel > 0, nb, 0)
    n = np.abs(rel)
    nf = np.maximum(n, 1).astype(np.float32)
    large = max_exact + (np.log(nf / np.float32(max_exact)) / np.float32(math.log(1024 / max_exact))
                         * np.float32(nb - max_exact)).astype(np.int32)
    large = np.minimum(large, nb - 1)
    return base + np.where(n < max_exact, n, large)


def _rope_tables():
    def inv_freq(dim):
        return (10000.0 ** (-np.arange(0, dim, 2, dtype=np.float32) / np.float32(dim))).astype(np.float32)

    t = np.arange(S)
    row = (t // 64).astype(np.float32)
    col = (t % 64).astype(np.float32)
    f16 = inv_freq(32)
    ang_ax = np.concatenate([row[:, None] * f16[None, :], col[:, None] * f16[None, :]], axis=-1)
    cosA = np.cos(ang_ax).astype(np.float32)
    sinA = np.sin(ang_ax).astype(np.float32)
    ropeA = np.zeros((2, 128, S), np.float32)
    for p in range(128):
        dd = p % 64
        ropeA[0, p] = cosA[:, dd % 32]
        ropeA[1, p] = (-1.0 if dd < 32 else 1.0) * sinA[:, dd % 32]
    ang_c = t.astype(np.float32)[:, None] * inv_freq(32)[None, :]
    cosC = np.cos(ang_c).astype(np.float32)
    sinC = np.sin(ang_c).astype(np.float32)
    ropeC = np.zeros((2, 128, S), np.float32)
    for p in range(64, 96):
        dd = p - 64
        ropeC[0, p] = cosC[:, dd % 16]
        ropeC[1, p] = (-1.0 if dd < 16 else 1.0) * sinC[:, dd % 16]
    return ropeA, ropeC


B_DIL = (1, 4, 16)


def _pack_shared(inp):
    f = lambda a: np.ascontiguousarray(np.asarray(a, dtype=np.float32))
    sh = {}
    ropeA, ropeC = _rope_tables()
    sh["ropeA"] = ropeA
    sh["ropeC"] = ropeC
    sh["ident"] = np.eye(128, dtype=np.float32)
    nm = f(inp["norm_mix_g"])
    nf = f(inp["norm_mlp_g"])
    gq = f(inp["a_q_norm_g"])[0]
    gk = f(inp["a_k_norm_g"])[0]
    perm64 = np.concatenate([np.arange(32, 64), np.arange(0, 32)])
    cols = []
    for v in (nm[0], nm[1], nf[0], nf[1]):
        cols.append(v.reshape(8, 128).T)
    pidx = np.arange(128) % 64
    cols.append(np.stack([gq[pidx], gq[perm64][pidx], gk[pidx], gk[perm64][pidx]], axis=1))
    cols.append(f(inp["c_q_norm_g"])[0].reshape(2, 128).T)
    cols.append(f(inp["c_kv_norm_g"])[0].reshape(1, 128).T)
    cols.append(f(inp["final_norm_g"]).reshape(8, 128).T)
    sh["gvec"] = f(np.concatenate(cols, axis=1))
    Win = f(inp["ab_w_in"])[0]
    Wk = Win.reshape(8, 128, 2304).transpose(1, 0, 2)
    wa = np.zeros((2, 128, 8, 832), np.float32)
    for g in range(2):
        q = Wk[:, :, g * 256:(g + 1) * 256]
        qp = q.reshape(128, 8, 4, 64)[:, :, :, perm64].reshape(128, 8, 256)
        k = Wk[:, :, 512 + g * 64:512 + (g + 1) * 64]
        kp = k[:, :, perm64]
        v = Wk[:, :, 640 + g * 64:640 + (g + 1) * 64]
        wa[g] = np.concatenate([q, qp, k, k, kp, kp, v], axis=2)
    sh["wa"] = wa
    wb = np.zeros((4, 128, 8, 384), np.float32)
    for hp in range(4):
        wb[hp] = np.concatenate([Wk[:, :, 768 + hp * 128:768 + (hp + 1) * 128],
                                 Wk[:, :, 1280 + hp * 128:1280 + (hp + 1) * 128],
                                 Wk[:, :, 1792 + hp * 128:1792 + (hp + 1) * 128]], axis=2)
    sh["wb"] = wb
    sh["wo0"] = f(f(inp["ab_w_out"])[0].reshape(8, 128, 1024).transpose(1, 0, 2))
    sh["wo1"] = f(f(inp["c_w_out"])[0].reshape(8, 128, 1024).transpose(1, 0, 2))
    w1 = f(inp["mlp_w1"])
    w2 = f(inp["mlp_w2"])
    sh["w1"] = f(w1.reshape(2, 8, 128, 4, 1024).transpose(0, 3, 2, 1, 4))
    sh["w2"] = f(w2.reshape(2, 4, 8, 128, 1024).transpose(0, 1, 3, 2, 4))
    rb = f(inp["rel_bias"])
    k_i = np.arange(128)[:, None]
    v_i = np.arange(384)[None, :]
    xrel = k_i - v_i + 128
    valid = np.abs(xrel) <= 64
    tb = np.full((8, 128, 3, 384), NEG, np.float32)
    for bi, d in enumerate(B_DIL):
        bucket = _t5_bucket(np.clip(xrel, -64, 64) * d)
        for h in range(8):
            tb[h, :, bi, :] = np.where(valid, rb[bucket, h], np.float32(NEG))
    sh["tb"] = tb
    Wd = f(inp["c_w_down"])[0]
    perm32 = np.concatenate([np.arange(16, 32), np.arange(0, 16)])
    kr = Wd[:, 384:416]
    Wd2 = np.concatenate([Wd, kr[:, perm32]], axis=1)
    sh["wdn"] = f(Wd2.reshape(8, 128, 448).transpose(1, 0, 2))
    Wq = f(inp["c_w_uq"])[0].reshape(2, 128, 16, 96).transpose(1, 0, 2, 3)
    wuq = np.concatenate([Wq, Wq[:, :, :, 64:96][:, :, :, perm32]], axis=3)
    sh["wuq"] = f(wuq)
    Wkv = f(inp["c_w_ukv"])[0].reshape(128, 16, 128)
    sh["wukv"] = f(np.concatenate([Wkv[:, :, 0:64].reshape(128, 1024), Wkv[:, :, 64:128].reshape(128, 1024)], axis=1))
    return sh


SHARED_SHAPES = {
    "ropeA": [2, 128, S], "ropeC": [2, 128, S], "ident": [128, 128], "gvec": [128, 47],
    "wa": [2, 128, 8, 832], "wb": [4, 128, 8, 384], "wo0": [128, 8, 1024], "wo1": [128, 8, 1024],
    "w1": [2, 4, 128, 8, 1024], "w2": [2, 4, 128, 8, 1024], "tb": [8, 128, 3, 384],
    "wdn": [128, 8, 448], "wuq": [128, 2, 16, 128], "wukv": [128, 2048],
}


def build(stage="full"):
    nc = bass.Bass("TRN2", target_bir_lowering=False)
    dr = {}
    dr["x"] = nc.dram_tensor("x", [S, D], F32, kind="ExternalInput").ap()
    for k, shp in SHARED_SHAPES.items():
        dr[k] = nc.dram_tensor(k, shp, F32, kind="ExternalInput").ap()
    if stage == "full":
        out_d = nc.dram_tensor("out", [S, D], F32, kind="ExternalOutput").ap()
    else:
        out_d = nc.dram_tensor("out", [128, 8, S], F32, kind="ExternalOutput").ap()

    st = ExitStack()
    P = Prog(nc)

    def sbt(name, shape, dt=F32):
        return st.enter_context(nc.sbuf_tensor("sb_" + name, list(shape), dt))

    hT = sbt("hT", [128, 8, S], F32)
    rH = [[Res("h%d_%d" % (c, t)) for t in range(NB)] for c in range(8)]
    identf = sbt("identf", [128, 128], F32); rIdent = Res("identf")
    gvec = sbt("gvec", [128, 47], F32); rG = Res("gvec")
    cm = sbt("cm", [128, 4, 128], BF16); rCM = Res("cm")
    epst = sbt("epst", [128, 1], F32); rEps = Res("eps")
    onesf = sbt("onesf", [128, 512], F32); rOnes = Res("onesf")
    arena_bytes = (nc.sbuf_bytes_remaining - 256) // 64 * 64
    print("arena bytes", arena_bytes)
    arena_t = sbt("arena", [128, arena_bytes // 2], BF16)
    AR = Arena(arena_t, arena_bytes)
    banks_t = st.enter_context(nc.psum_tensor("banks", [128, 8, 512], F32))
    BK = [banks_t[:, i, :] for i in range(8)]
    rBK = [Res("bank%d" % i, psum=True) for i in range(8)]

    P.dma("sp", identf[:], dr["ident"], W=[rIdent])
    P.dma("sp", gvec[:], dr["gvec"], W=[rG])
    P.op("dve", "memset", epst[:], EPS, W=[rEps])
    P.op("dve", "memset", onesf[:], -1.0, W=[rOnes])
    P.op("pool", "memset", cm[:, 0, :], 1.0 / 1024, W=[rCM])
    P.op("pool", "memset", cm[:, 1, :], 0.0, W=[rCM])
    P.op("pool", "memset", cm[0:64, 1, 0:64], 1.0 / 64, W=[rCM])
    P.op("pool", "memset", cm[64:128, 1, 64:128], 1.0 / 64, W=[rCM])
    P.op("pool", "memset", cm[:, 2, :], 1.0 / 256, W=[rCM])
    P.op("pool", "memset", cm[:, 3, :], 1.0 / 128, W=[rCM])

    blk = lambda t: slice(t * 512, (t + 1) * 512)
    evac_rr = [0]

    def evac(out, in_, R, W):
        evac_rr[0] ^= 1
        if evac_rr[0]:
            P.op("act", "activation", out=out, in_=in_, func=AF.Copy, R=R, W=W)
        else:
            P.op("dve", "tensor_copy", out=out, in_=in_, R=R, W=W)

    def phase_load():
        m = AR.mark()
        xs = [AR.alloc(4096, F32) for _ in range(3)]
        rX = [Res("xs%d" % i) for i in range(3)]
        for i in range(16):
            b = i % 3
            P.dma("sp", xs[b], dr["x"][i * 128:(i + 1) * 128, :], W=[rX[b]])
            t = i // 4
            for half in range(2):
                bk = (2 * i + half) % 8
                for cc in range(4):
                    c = half * 4 + cc
                    P.op("pe", "transpose", BK[bk][:, cc * 128:(cc + 1) * 128], xs[b][:, c * 128:(c + 1) * 128],
                         identf[:], R=[rX[b], rIdent], W=[rBK[bk]])
                evac(hT[:, half * 4:half * 4 + 4, i * 128:(i + 1) * 128],
                     BK[bk].rearrange("p (a b) -> p a b", a=4),
                     R=[rBK[bk]], W=[rH[half * 4 + cc][t] for cc in range(4)])
        P.barrier()
        AR.release(m)

    def rms_block(t, srcs, rsrcs, gcol, cmi, dsts, rdsts, scratch):
        sq, lnv, rstd, rSq, rLn, rRs = scratch
        n = len(srcs)
        bk = 4 + (t % 2)
        for c in range(n):
            eng = "pool" if c % 2 == 0 else "dve"
            P.op(eng, "tensor_tensor", out=sq[:, c, :], in0=srcs[c], in1=srcs[c], op=ALU.mult, R=[rsrcs[c]], W=[rSq[c]])
        for c in range(n):
            P.op("pe", "matmul", BK[bk], cm[:, cmi, :], sq[:, c, :], start=(c == 0), stop=(c == n - 1),
                 R=[rCM, rSq[c]], W=[rBK[bk]])
        P.op("act", "activation", out=lnv, in_=BK[bk], func=AF.Ln, bias=epst[:], scale=1.0,
             R=[rBK[bk], rEps], W=[rLn])
        P.op("act", "activation", out=rstd, in_=lnv, func=AF.Exp, scale=-0.5, R=[rLn], W=[rRs])
        for c in range(n):
            eng = "dve"
            P.op(eng, "scalar_tensor_tensor", out=dsts[c], in0=srcs[c],
                 scalar=gvec[:, gcol + c:gcol + c + 1], in1=rstd, op0=ALU.mult, op1=ALU.mult,
                 R=[rsrcs[c], rG, rRs], W=[rdsts[c]])

    def rmsnorm_fm(src, rSrc, nchunks, gcol, cmi, dst, rDst, scratch):
        for t in range(NB):
            rms_block(t, [src[:, c, blk(t)] for c in range(nchunks)], [rSrc[c][t] for c in range(nchunks)], gcol, cmi,
                      [dst[:, c, blk(t)] for c in range(nchunks)], [rDst[c][t] for c in range(nchunks)], scratch)

    nsc = [0]

    def norm_scratch():
        nsc[0] += 1
        sq = AR.alloc(8 * 512 * 2, BF16).rearrange("p (c n) -> p c n", c=8)
        lnv = AR.alloc(2048, F32)
        rstd = AR.alloc(2048, F32)
        k = nsc[0]
        return sq, lnv, rstd, [Res("n%d_sq%d" % (k, c)) for c in range(8)], Res("n%d_ln" % k), Res("n%d_rs" % k)

    def run_attention(steps, scale, PT2, rPT2, spairs, bg=None, bg_every=4):
        n = len(steps)
        npair = len(spairs)
        la = npair - 1

        def qk(s_):
            sa = spairs[s_ % npair]
            for l, ln in enumerate(steps[s_][0]):
                P.op("pe", "matmul", BK[sa + l], ln[0], ln[1], start=True, stop=True,
                     R=[ln[2], ln[3]], W=[rBK[sa + l]])

        def ex(s_):
            sa = spairs[s_ % npair]
            k = s_ % len(PT2)
            P.op("act", "activation", out=PT2[k].rearrange("p (a n) -> p a n", a=2), in_=banks_t[:, sa:sa + 2, :],
                 func=AF.Exp, scale=scale, R=[rBK[sa], rBK[sa + 1]], W=[rPT2[k]])

        def pv(s_):
            k = s_ % len(PT2)
            for l, ln in enumerate(steps[s_][0]):
                P.op("pe", "matmul", BK[ln[6]], ln[4], PT2[k][:, l * 512:(l + 1) * 512], start=ln[7], stop=ln[8],
                     R=[ln[5], rPT2[k]], W=[rBK[ln[6]]])
            if steps[s_][1] is not None:
                steps[s_][1]()

        for s_ in range(min(la, n)):
            qk(s_)
        for s_ in range(n):
            ex(s_)
            if s_ + la < n:
                qk(s_ + la)
            pv(s_)
            if bg is not None and s_ % bg_every == bg_every - 1:
                next(bg, None)
        if bg is not None:
            for _ in bg:
                pass

    def normalize(ob, osel, dst, rDst, rec, rRec):
        orow = slice(0, 64) if osel == 0 else slice(64, 128)
        lrow = slice(64, 128) if osel == 0 else slice(0, 64)
        lc, rc = rec
        P.op("dve", "reciprocal", out=rc[lrow, :], in_=BK[ob][lrow, :], R=[rBK[ob]], W=[rRec[1]])
        P.op("dve", "tensor_tensor", out=dst[orow, :], in0=BK[ob][orow, :], in1=rc[lrow, :],
             op=ALU.mult, R=[rBK[ob], rRec[1]], W=[rDst])

    def out_proj(wo_name, OT, rOT):
        m = AR.mark()
        wo = AR.alloc(8 * 1024 * 2, BF16).rearrange("p (c n) -> p c n", c=8)
        rWo = Res("wo_" + wo_name)
        for q in range(4):
            P.dma("pool", wo[:, 2 * q:2 * q + 2, :], dr[wo_name][:, 2 * q:2 * q + 2, :], W=[rWo])
        k = 0
        for t in range(NB):
            for dc in range(8):
                bk = k % 4
                k += 1
                for fc in range(8):
                    P.op("pe", "matmul", BK[bk], wo[:, fc, dc * 128:(dc + 1) * 128], OT[:, fc, blk(t)],
                         start=(fc == 0), stop=(fc == 7), R=[rWo, rOT[fc][t]], W=[rBK[bk]])
                P.op("dve", "tensor_tensor", out=hT[:, dc, blk(t)], in0=BK[bk], in1=hT[:, dc, blk(t)], op=ALU.add,
                     R=[rBK[bk], rH[dc][t]], W=[rH[dc][t]])
        P.barrier()
        AR.release(m)

    def mlp(layer, gcol):
        m = AR.mark()
        hn = AR.alloc(8 * S * 2, BF16).rearrange("p (c n) -> p c n", c=8)
        rHn = [[Res("mhn%d_%d" % (c, t)) for t in range(NB)] for c in range(8)]
        w1b = [AR.alloc(8 * 1024 * 2, BF16).rearrange("p (c n) -> p c n", c=8) for _ in range(2)]
        w2b = [AR.alloc(8 * 1024 * 2, BF16).rearrange("p (c n) -> p c n", c=8) for _ in range(2)]
        rW1 = [Res("w1b%d" % i) for i in range(2)]
        rW2 = [Res("w2b%d" % i) for i in range(2)]
        rbuf = [AR.alloc(2048, F32) for _ in range(3)]
        rR = [Res("mr%d" % i) for i in range(3)]
        uset = [AR.alloc(8 * 1024, BF16).rearrange("p (c n) -> p c n", c=8) for _ in range(2)]
        rU = [[Res("mu%d_%d" % (i, c)) for c in range(8)] for i in range(2)]

        def load_w(g):
            b = g % 2
            for q in range(2):
                P.dma("pool", w1b[b][:, 4 * q:4 * q + 4, :], dr["w1"][layer, g, :, 4 * q:4 * q + 4, :], W=[rW1[b]])
            for q in range(2):
                P.dma("pool", w2b[b][:, 4 * q:4 * q + 4, :], dr["w2"][layer, g, :, 4 * q:4 * q + 4, :], W=[rW2[b]])

        load_w(0)
        load_w(1)
        scratch = norm_scratch()
        rmsnorm_fm(hT, rH, 8, gcol, 0, hn, rHn, scratch)
        its = [(g, t) for g in range(4) for t in range(NB)]
        rcnt = [0]
        pcnt = [0]

        def up(k):
            g, t = its[k]
            b = g % 2
            us, rUs = uset[k % 2], rU[k % 2]
            for fcl in range(8):
                bk = 6 + (fcl % 2)
                for kc in range(8):
                    P.op("pe", "matmul", BK[bk], w1b[b][:, kc, fcl * 128:(fcl + 1) * 128], hn[:, kc, blk(t)],
                         start=(kc == 0), stop=(kc == 7), R=[rW1[b], rHn[kc][t]], W=[rBK[bk]])
                ri = rcnt[0] % 3
                rcnt[0] += 1
                P.op("act", "activation", out=rbuf[ri], in_=BK[bk], func=AF.Relu, R=[rBK[bk]], W=[rR[ri]])
                eng = "pool" if fcl % 2 == 0 else "dve"
                P.op(eng, "tensor_tensor", out=us[:, fcl, :], in0=rbuf[ri], in1=rbuf[ri], op=ALU.mult, R=[rR[ri]], W=[rUs[fcl]])

        def down(k):
            g, t = its[k]
            b = g % 2
            us, rUs = uset[k % 2], rU[k % 2]
            for dcl in ((0, 1, 2), (3, 4, 5), (6, 7)):
                banks = (0, 1, 2) if pcnt[0] % 2 == 0 else (3, 4, 5)
                pcnt[0] += 1
                for di, dc in enumerate(dcl):
                    bk = banks[di]
                    for fcl in range(8):
                        P.op("pe", "matmul", BK[bk], w2b[b][:, fcl, dc * 128:(dc + 1) * 128], us[:, fcl, :],
                             start=(fcl == 0), stop=(fcl == 7), R=[rW2[b], rUs[fcl]], W=[rBK[bk]])
                    P.op("dve", "tensor_tensor", out=hT[:, dc, blk(t)], in0=BK[bk], in1=hT[:, dc, blk(t)], op=ALU.add,
                         R=[rBK[bk], rH[dc][t]], W=[rH[dc][t]])

        up(0)
        for k in range(16):
            if k + 1 < 16:
                up(k + 1)
            down(k)
            g, t = its[k]
            if t == NB - 1 and g + 2 < 4:
                load_w(g + 2)
        P.barrier()
        AR.release(m)

    def layer0_mix():
        m0 = AR.mark()
        OT = AR.alloc(8 * S * 2, BF16).rearrange("p (c n) -> p c n", c=8)
        rOT = [[Res("ot%d_%d" % (c, t)) for t in range(NB)] for c in range(8)]
        hn = AR.alloc(8 * S * 2, BF16).rearrange("p (c n) -> p c n", c=8)
        rHn = [[Res("hn%d_%d" % (c, t)) for t in range(NB)] for c in range(8)]
        PT2 = [AR.alloc(2048, BF16) for _ in range(3)]
        rPT2 = [Res("pt2_%d" % i) for i in range(3)]
        PT = [PT2[i // 2][:, (i % 2) * 512:(i % 2 + 1) * 512] for i in range(4)]
        rPT = [rPT2[i // 2] for i in range(4)]
        rec = [(AR.alloc(2048, F32), AR.alloc(2048, F32)) for _ in range(2)]
        rRec = [(Res("recl%d" % i), Res("recr%d" % i)) for i in range(2)]
        m1 = AR.mark()
        scratch = norm_scratch()
        rmsnorm_fm(hT, rH, 8, 0, 0, hn, rHn, scratch)
        P.barrier()
        AR.release(m1)

        ropeA = AR.alloc(2 * S * 4, F32).rearrange("p (a n) -> p a n", a=2)
        rRope = Res("ropeA")
        P.dma("sp", ropeA[:, 0, :], dr["ropeA"][0], W=[rRope])
        P.dma("sp", ropeA[:, 1, :], dr["ropeA"][1], W=[rRope])
        wA1 = AR.alloc(8 * 832 * 2, BF16).rearrange("p (c n) -> p c n", c=8)
        rWA1 = Res("wA")
        wA = [wA1, wA1]
        rWA = [rWA1, rWA1]
        QA = AR.alloc(2 * S * 2, BF16).rearrange("p (c n) -> p c n", c=2)
        KA = AR.alloc(S * 2, BF16)
        VA = AR.alloc(16 * 192 * 2, BF16).rearrange("p (i n) -> p i n", i=16)
        qf = [AR.alloc(2048, F32) for _ in range(2)]
        sqb = AR.alloc(1024, BF16)
        lnv = AR.alloc(2048, F32)
        rstd = AR.alloc(2048, F32)
        t1 = AR.alloc(2048, F32)
        t2 = AR.alloc(2048, F32)
        rQf = [Res("qf0"), Res("qf1")]
        rSq, rLn, rRs, rT1, rT2 = Res("a_sq"), Res("a_ln"), Res("a_rs"), Res("a_t1"), Res("a_t2")
        rQA = [Res("QA0"), Res("QA1")]
        rKA = Res("KA")
        rVA = Res("VA")
        P.op("pool", "memset", VA[:, :, 0:64], 1.0, W=[rVA])
        P.op("pool", "memset", VA[:, :, 128:192], 1.0, W=[rVA])

        def proj_norm_rope(w, rW, c0, cp0, gc, dstT, rDst):
            for t in range(NB):
                for kc in range(8):
                    P.op("pe", "matmul", BK[4], w[:, kc, c0:c0 + 128], hn[:, kc, blk(t)], start=(kc == 0), stop=(kc == 7),
                         R=[rW, rHn[kc][t]], W=[rBK[4]])
                for kc in range(8):
                    P.op("pe", "matmul", BK[5], w[:, kc, cp0:cp0 + 128], hn[:, kc, blk(t)], start=(kc == 0), stop=(kc == 7),
                         R=[rW, rHn[kc][t]], W=[rBK[5]])
                P.op("dve", "tensor_copy", out=qf[0], in_=BK[4], R=[rBK[4]], W=[rQf[0]])
                P.op("act", "activation", out=qf[1], in_=BK[5], func=AF.Copy, scale=gvec[:, gc + 1:gc + 2],
                     R=[rBK[5], rG], W=[rQf[1]])
                P.op("pool", "tensor_tensor", out=sqb, in0=qf[0], in1=qf[0], op=ALU.mult, R=[rQf[0]], W=[rSq])
                P.op("pe", "matmul", BK[4], cm[:, 1, :], sqb, start=True, stop=True, R=[rCM, rSq], W=[rBK[4]])
                P.op("act", "activation", out=lnv, in_=BK[4], func=AF.Ln, bias=epst[:], scale=1.0,
                     R=[rBK[4], rEps], W=[rLn])
                P.op("act", "activation", out=rstd, in_=lnv, func=AF.Exp, scale=-0.5, R=[rLn], W=[rRs])
                P.op("dve", "scalar_tensor_tensor", out=t1, in0=qf[0], scalar=gvec[:, gc:gc + 1], in1=ropeA[:, 0, blk(t)],
                     op0=ALU.mult, op1=ALU.mult, R=[rQf[0], rG, rRope], W=[rT1])
                P.op("dve", "tensor_tensor", out=t2, in0=qf[1], in1=ropeA[:, 1, blk(t)], op=ALU.mult, R=[rQf[1], rRope], W=[rT2])
                P.op("pool", "tensor_tensor", out=t1, in0=t1, in1=t2, op=ALU.add, R=[rT1, rT2], W=[rT1])
                P.op("dve", "tensor_tensor", out=dstT[:, blk(t)], in0=t1, in1=rstd, op=ALU.mult, R=[rT1, rRs], W=[rDst])

        for g in range(2):
            w = wA[g]
            for q in range(2):
                P.dma("pool", w[:, 4 * q:4 * q + 4, :], dr["wa"][g, :, 4 * q:4 * q + 4, :], W=[rWA[g]])
            for j in range(2):
                proj_norm_rope(w, rWA[g], j * 128, 256 + j * 128, 32, QA[:, j, :], rQA[j])
            proj_norm_rope(w, rWA[g], 512, 640, 34, KA, rKA)
            for half in range(2):
                bk = 4 + half
                for ii in range(8):
                    i = half * 8 + ii
                    for kc in range(8):
                        P.op("pe", "matmul", BK[bk][:, ii * 64:(ii + 1) * 64], hn[:, kc, i * 128:(i + 1) * 128],
                             w[:, kc, 768:832], start=(kc == 0), stop=(kc == 7), skip_group_check=True,
                             R=[rWA[g], rHn[kc][i // 4]], W=[rBK[bk]])
                evac(VA[:, half * 8:half * 8 + 8, 64:128], BK[bk].rearrange("p (i n) -> p i n", i=8),
                     R=[rBK[bk]], W=[rVA])
            steps = []
            for j in range(2):
                for t in range(NB):
                    obs = (4 + 2 * (t % 2), 5 + 2 * (t % 2))
                    for i in range(16):
                        lanes = []
                        for e in range(2):
                            rows = slice(e * 64, (e + 1) * 64)
                            v = VA[:, i, 64:192] if e == 0 else VA[:, i, 0:128]
                            lanes.append((KA[rows, i * 128:(i + 1) * 128], QA[rows, j, blk(t)], rKA, rQA[j], v, rVA,
                                          obs[e], i == 0, i == 15))
                        post = None
                        if i == 15:
                            def post(j=j, t=t, obs=obs):
                                for e in range(2):
                                    normalize(obs[e], e, OT[:, 2 * g + j, blk(t)], rOT[2 * g + j][t], rec[e], rRec[e])
                        steps.append((lanes, post))
            run_attention(steps, 0.125, PT2, rPT2, (0, 2))
        P.barrier()
        AR.release(m1)

        wB = AR.alloc(8 * 384 * 2, BF16).rearrange("p (c n) -> p c n", c=8)
        rWB = Res("wB")
        QB = AR.alloc(S * 2, BF16)
        KB = AR.alloc(S * 2, BF16)
        VB = AR.alloc(3 * 16 * 192 * 2, BF16).rearrange("p (b i n) -> p b i n", b=3, i=16)
        rQB, rKB, rVB = Res("QB"), Res("KB"), Res("VB")
        TBs = [AR.alloc(3 * 384 * 4, F32).rearrange("p (b n) -> p b n", b=3) for _ in range(2)]
        rTB = [Res("tb0"), Res("tb1")]
        tmp = [AR.alloc(2048, F32) for _ in range(3)]
        rTmp = [Res("btmp%d" % i) for i in range(3)]
        PTb = [AR.alloc(1024, BF16) for _ in range(4)]
        rPTb = [Res("bpt%d" % i) for i in range(4)]
        lnl = AR.alloc(2048, F32)
        rLnl = Res("b_lnl")
        P.op("pool", "memset", VB[:, :, :, 64:128], 1.0, W=[rVB])
        sb_rr = [0]
        tmp_rr = [0]
        pt_rr = [0]
        ob_rr = [0]
        for hp in range(4):
            for q in range(2):
                P.dma("pool", wB[:, 4 * q:4 * q + 4, :], dr["wb"][hp, :, 4 * q:4 * q + 4, :], W=[rWB])
            for e in range(2):
                P.dma("sp", TBs[e], dr["tb"][2 * hp + e], W=[rTB[e]])
            for (c0, dstT, rDst) in ((0, QB, rQB), (128, KB, rKB)):
                for t in range(NB):
                    bk = 5 + (t % 2)
                    for kc in range(8):
                        P.op("pe", "matmul", BK[bk], wB[:, kc, c0:c0 + 128], hn[:, kc, blk(t)], start=(kc == 0), stop=(kc == 7),
                             R=[rWB, rHn[kc][t]], W=[rBK[bk]])
                    evac(dstT[:, blk(t)], BK[bk], R=[rBK[bk]], W=[rDst])
            for bi, d in enumerate(B_DIL):
                nkt = (S // d) // 128
                for grp in range(4):
                    bk = 5 + (grp % 2)
                    for q in range(4):
                        ti = grp * 4 + q
                        r_, kt = ti // nkt, ti % nkt
                        s0 = kt * 128 * d + r_
                        tok = slice(s0, s0 + 127 * d + 1, d)
                        for kc in range(8):
                            P.op("pe", "matmul", BK[bk][:, q * 128:(q + 1) * 128], hn[:, kc, tok], wB[:, kc, 256:384],
                                 start=(kc == 0), stop=(kc == 7), skip_group_check=True,
                                 R=[rWB] + [rHn[kc][tt] for tt in range(NB)], W=[rBK[bk]])
                    dst = VB[:, bi, grp * 4:grp * 4 + 4, :].rearrange("p i (a n) -> p i a n", a=3)[:, :, 0:3:2, :]
                    evac(dst, BK[bk].rearrange("p (i a n) -> p i a n", i=4, a=2), R=[rBK[bk]], W=[rVB])
            flat = []
            for e in range(2):
                for c in range(NB):
                    ob = 6 + (ob_rr[0] % 2)
                    ob_rr[0] += 1
                    jobs = []
                    for di, delta in enumerate((128, 0, -128)):
                        regs = []
                        for qtl in range(4):
                            kt = 4 * c + qtl + delta // 128
                            if 0 <= kt < 16:
                                regs.append((qtl * 128, 128, kt, slice(kt * 128, kt * 128 + 128),
                                             slice(c * 512 + qtl * 128, c * 512 + qtl * 128 + 128),
                                             slice(qtl * 128, qtl * 128 + 128)))
                        if regs:
                            jobs.append((0, regs, slice(128 - delta, 256 - delta), 128))
                    for di, delta in enumerate((128, 0, -128)):
                        kt = c + delta // 128
                        if 0 <= kt < 4:
                            regs = []
                            for r_ in range(4):
                                ks = kt * 512 + r_
                                qs = c * 512 + r_
                                regs.append((r_ * 128, 128, r_ * 4 + kt, slice(ks, ks + 127 * 4 + 1, 4),
                                             slice(qs, qs + 127 * 4 + 1, 4), slice(r_, r_ + 127 * 4 + 1, 4)))
                            jobs.append((1, regs, slice(128 - delta, 256 - delta), 128))
                    regs = []
                    for r_ in range(16):
                        qs = c * 512 + r_
                        regs.append((r_ * 32, 32, r_, slice(r_, r_ + 127 * 16 + 1, 16),
                                     slice(qs, qs + 31 * 16 + 1, 16), slice(r_, r_ + 31 * 16 + 1, 16)))
                    jobs.append((2, regs, slice(128 + c * 32, 128 + c * 32 + 32), 32))
                    for ji, jb in enumerate(jobs):
                        flat.append((jb, ob, e, ji == 0, ji == len(jobs) - 1, c))
            nj = len(flat)
            LA = 3

            def b_qk(j):
                (bi, regs, mcols, w), ob, e, isfirst, islast, c = flat[j]
                rows = slice(e * 64, (e + 1) * 64)
                sb_ = j % 5
                for (col0, w_, ti, ksl, qsl, osl) in regs:
                    P.op("pe", "matmul", BK[sb_][:, col0:col0 + w_], KB[rows, ksl], QB[rows, qsl],
                         start=True, stop=True, skip_group_check=True, R=[rKB, rQB], W=[rBK[sb_]])

            def b_mid(j):
                (bi, regs, mcols, w), ob, e, isfirst, islast, c = flat[j]
                sb_ = j % 5
                c_lo = regs[0][0]
                c_hi = regs[-1][0] + w
                nreg = len(regs)
                tm, rTm = tmp[j % 3], rTmp[j % 3]
                P.op("dve", "scalar_tensor_tensor",
                     out=tm[:, c_lo:c_hi].rearrange("p (a n) -> p a n", a=nreg),
                     in0=BK[sb_][:, c_lo:c_hi].rearrange("p (a n) -> p a n", a=nreg),
                     scalar=0.125, in1=bcast_mid(TBs[e][:, bi, mcols], nreg),
                     op0=ALU.mult, op1=ALU.add, R=[rBK[sb_], rTB[e]], W=[rTm])
                P.op("act", "activation", out=PTb[j % 4][:, c_lo:c_hi], in_=tm[:, c_lo:c_hi], func=AF.Exp,
                     R=[rTm], W=[rPTb[j % 4]])

            def b_pv(j):
                (bi, regs, mcols, w), ob, e, isfirst, islast, c = flat[j]
                vsl = slice(0, 128) if e == 0 else slice(64, 192)
                orow = slice(0, 64) if e == 0 else slice(64, 128)
                lrow = slice(64, 128) if e == 0 else slice(0, 64)
                pt = PTb[j % 4]
                for ri, (col0, w_, ti, ksl, qsl, osl) in enumerate(regs):
                    P.op("pe", "matmul", BK[ob][:, osl], VB[:, bi, ti, vsl], pt[:, col0:col0 + w_],
                         start=(isfirst and ri == 0), stop=(islast and ri == len(regs) - 1), skip_group_check=True,
                         R=[rVB, rPTb[j % 4]], W=[rBK[ob]])
                if islast:
                    rr = c % 2
                    P.op("act", "activation", out=lnl[lrow, :], in_=BK[ob][lrow, :], func=AF.Ln, R=[rBK[ob]], W=[rLnl])
                    P.op("act", "activation", out=rec[rr][1][lrow, :], in_=lnl[lrow, :], func=AF.Exp, scale=-1.0,
                         R=[rLnl], W=[rRec[rr][1]])
                    P.op("dve", "tensor_tensor", out=OT[orow, 4 + hp, blk(c)], in0=BK[ob][orow, :], in1=rec[rr][1][lrow, :],
                         op=ALU.mult, R=[rBK[ob], rRec[rr][1]], W=[rOT[4 + hp][c]])

            for j in range(min(LA, nj)):
                b_qk(j)
            for j in range(nj):
                b_mid(j)
                if j + LA < nj:
                    b_qk(j + LA)
                b_pv(j)
        P.barrier()
        AR.release(m1)
        out_proj("wo0", OT, rOT)
        AR.release(m0)

    def layer1_mix():
        m0 = AR.mark()
        PT2 = [AR.alloc(2048, BF16) for _ in range(3)]
        rPT2 = [Res("c_pt2_%d" % i) for i in range(3)]
        rec = [(AR.alloc(2048, F32), AR.alloc(2048, F32)) for _ in range(2)]
        rRec = [(Res("c_recl%d" % i), Res("c_recr%d" % i)) for i in range(2)]
        cqn = AR.alloc(2 * S * 2, BF16).rearrange("p (c n) -> p c n", c=2)
        ckvn = AR.alloc(S * 2, BF16).rearrange("p (c n) -> p c n", c=1)
        krT = AR.alloc(S * 2, BF16)
        rCqn = [[Res("cqn%d_%d" % (c, t)) for t in range(NB)] for c in range(2)]
        rCkvn = [[Res("ckvn_%d" % t) for t in range(NB)]]
        rKr = Res("krT")
        ropeC = AR.alloc(2 * S * 4, F32).rearrange("p (a n) -> p a n", a=2)
        rRope = Res("ropeC")
        P.dma("sp", ropeC[:, 0, :], dr["ropeC"][0], W=[rRope])
        P.dma("sp", ropeC[:, 1, :], dr["ropeC"][1], W=[rRope])
        wuq = AR.alloc(2 * 16 * 128 * 2, BF16).rearrange("p (c h n) -> p c h n", c=2, h=16)
        wukv = AR.alloc(2048 * 2, BF16)
        rWuq, rWukv = Res("wuq"), Res("wukv")
        for c in range(2):
            P.dma("pool", wuq[:, c], dr["wuq"][:, c], W=[rWuq])
        P.dma("pool", wukv, dr["wukv"], W=[rWukv])
        t1 = AR.alloc(2048, F32)
        t2 = AR.alloc(2048, F32)
        rT1, rT2 = Res("c_t1"), Res("c_t2")
        m1 = AR.mark()
        hn = AR.alloc(8 * S * 2, BF16).rearrange("p (c n) -> p c n", c=8)
        rHn = [[Res("c_hn%d_%d" % (c, t)) for t in range(NB)] for c in range(8)]
        wdn = AR.alloc(8 * 448 * 2, BF16).rearrange("p (c n) -> p c n", c=8)
        rWdn = Res("wdn")
        for q in range(2):
            P.dma("pool", wdn[:, 4 * q:4 * q + 4, :], dr["wdn"][:, 4 * q:4 * q + 4, :], W=[rWdn])
        cblk = [AR.alloc(3 * 2048, F32).rearrange("p (c n) -> p c n", c=3) for _ in range(2)]
        rCb = [[Res("cb%d_%d" % (i, c)) for c in range(3)] for i in range(2)]
        scratch = norm_scratch()
        rmsnorm_fm(hT, rH, 8, 8, 0, hn, rHn, scratch)
        rr_ = slice(64, 96)
        for t in range(NB):
            cb = cblk[t % 2]
            rC = rCb[t % 2]
            for ci in range(3):
                c0 = ci * 128
                bk = 6 + (ci % 2)
                for kc in range(8):
                    P.op("pe", "matmul", BK[bk], wdn[:, kc, c0:c0 + 128], hn[:, kc, blk(t)], start=(kc == 0), stop=(kc == 7),
                         R=[rWdn, rHn[kc][t]], W=[rBK[bk]])
                evac(cb[:, ci, :], BK[bk], R=[rBK[bk]], W=[rC[ci]])
            for (c0, bk) in ((384, 2), (416, 3)):
                for kc in range(8):
                    P.op("pe", "matmul", BK[bk][64:96, :], wdn[:, kc, c0:c0 + 32], hn[:, kc, blk(t)],
                         start=(kc == 0), stop=(kc == 7), R=[rWdn, rHn[kc][t]], W=[rBK[bk]])
            P.op("dve", "tensor_tensor", out=t1[rr_, :], in0=BK[2][rr_, :], in1=ropeC[rr_, 0, blk(t)], op=ALU.mult,
                 R=[rBK[2], rRope], W=[rT1])
            P.op("dve", "tensor_tensor", out=t2[rr_, :], in0=BK[3][rr_, :], in1=ropeC[rr_, 1, blk(t)], op=ALU.mult,
                 R=[rBK[3], rRope], W=[rT2])
            P.op("pool", "tensor_tensor", out=krT[rr_, blk(t)], in0=t1[rr_, :], in1=t2[rr_, :], op=ALU.add,
                 R=[rT1, rT2], W=[rKr])
            rms_block(t, [cb[:, 0, :], cb[:, 1, :]], [rC[0], rC[1]], 36, 2,
                      [cqn[:, 0, blk(t)], cqn[:, 1, blk(t)]], [rCqn[0][t], rCqn[1][t]], scratch)
            rms_block(t, [cb[:, 2, :]], [rC[2]], 38, 3, [ckvn[:, 0, blk(t)]], [rCkvn[0][t]], scratch)
        P.barrier()
        AR.release(m1)
        OT = AR.alloc(8 * S * 2, BF16).rearrange("p (c n) -> p c n", c=8)
        rOT = [[Res("c_ot%d_%d" % (c, t)) for t in range(NB)] for c in range(8)]
        m2 = AR.mark()
        sets = []
        for si in range(2):
            QT = AR.alloc(2 * S * 2, BF16).rearrange("p (h n) -> p h n", h=2)
            KT = AR.alloc(2 * S * 2, BF16).rearrange("p (h n) -> p h n", h=2)
            VC = AR.alloc(16 * 192 * 2, BF16).rearrange("p (i n) -> p i n", i=16)
            rQT = [Res("QT%d_%d" % (si, i)) for i in range(2)]
            rKT = [Res("KT%d_%d" % (si, i)) for i in range(2)]
            rVC = Res("VC%d" % si)
            P.op("pool", "memset", VC[:, :, 64:128], 1.0, W=[rVC])
            sets.append((QT, KT, VC, rQT, rKT, rVC))
        t1s = [t1, t1]
        t2s = [t2, t2]
        rT1s = [rT1, rT1]
        rT2s = [rT2, rT2]
        cscale = 96.0 ** -0.5
        ucnt = [0]

        def proj_group(grp, si):
            QT, KT, VC, rQT, rKT, rVC = sets[si]
            for hl in range(2):
                h = 2 * grp + hl
                for t in range(NB):
                    u = ucnt[0] % 2
                    ucnt[0] += 1
                    for kc in range(2):
                        P.op("pe", "matmul", BK[4][0:96, :], wuq[:, kc, h, 0:96], cqn[:, kc, blk(t)], start=(kc == 0), stop=(kc == 1),
                             R=[rWuq, rCqn[kc][t]], W=[rBK[4]])
                    for kc in range(2):
                        P.op("pe", "matmul", BK[5][64:96, :], wuq[:, kc, h, 96:128], cqn[:, kc, blk(t)], start=(kc == 0), stop=(kc == 1),
                             R=[rWuq, rCqn[kc][t]], W=[rBK[5]])
                    P.op("pe", "matmul", BK[5][0:64, :], wukv[:, h * 64:(h + 1) * 64], ckvn[:, 0, blk(t)], start=True, stop=True,
                         R=[rWukv, rCkvn[0][t]], W=[rBK[5]])
                    P.op("dve", "tensor_copy", out=QT[0:64, hl, blk(t)], in_=BK[4][0:64, :], R=[rBK[4]], W=[rQT[hl]])
                    P.op("dve", "tensor_tensor", out=t1s[u][rr_, :], in0=BK[4][rr_, :], in1=ropeC[rr_, 0, blk(t)], op=ALU.mult,
                         R=[rBK[4], rRope], W=[rT1s[u]])
                    P.op("dve", "tensor_tensor", out=t2s[u][rr_, :], in0=BK[5][rr_, :], in1=ropeC[rr_, 1, blk(t)], op=ALU.mult,
                         R=[rBK[5], rRope], W=[rT2s[u]])
                    P.op("dve", "tensor_copy", out=KT[0:64, hl, blk(t)], in_=BK[5][0:64, :], R=[rBK[5]], W=[rKT[hl]])
                    P.op("pool", "tensor_tensor", out=QT[rr_, hl, blk(t)], in0=t1s[u][rr_, :], in1=t2s[u][rr_, :], op=ALU.add,
                         R=[rT1s[u], rT2s[u]], W=[rQT[hl]])
                    yield
                P.op("dve", "tensor_copy", out=KT[rr_, hl, :], in_=krT[rr_, :], R=[rKr], W=[rKT[hl]])
                yield
            for i4 in range(4):
                for q in range(4):
                    i = i4 * 4 + q
                    P.op("pe", "matmul", BK[4][:, q * 128:(q + 1) * 128], ckvn[:, 0, i * 128:(i + 1) * 128],
                         wukv[:, 1024 + grp * 128:1024 + (grp + 1) * 128], start=True, stop=True,
                         skip_group_check=True, R=[rWukv, rCkvn[0][i // 4]], W=[rBK[4]])
                dst = VC[:, i4 * 4:i4 * 4 + 4, :].rearrange("p i (b n) -> p i b n", b=3)[:, :, 0:3:2, :]
                P.op("dve", "tensor_copy", out=dst, in_=BK[4].rearrange("p (i b n) -> p i b n", i=4, b=2),
                     R=[rBK[4]], W=[rVC])
                yield

        def attn_steps(grp, si):
            QT, KT, VC, rQT, rKT, rVC = sets[si]
            steps = []
            for hl in range(2):
                h = 2 * grp + hl
                e = hl
                for t in range(NB):
                    ob = 6 + (t % 2)
                    for i2 in range(8):
                        lanes = []
                        for l in range(2):
                            i = 2 * i2 + l
                            v = VC[:, i, 0:128] if e == 0 else VC[:, i, 64:192]
                            lanes.append((KT[0:96, hl, i * 128:(i + 1) * 128], QT[0:96, hl, blk(t)], rKT[hl], rQT[hl], v, rVC,
                                          ob, i == 0, i == 15))
                        post = None
                        if i2 == 7:
                            def post(h=h, e=e, t=t, ob=ob):
                                normalize(ob, e, OT[:, h // 2, blk(t)], rOT[h // 2][t], rec[t % 2], rRec[t % 2])
                        steps.append((lanes, post))
            return steps

        import os
        MODE = os.environ.get("MLA_MODE", "bg")
        for _ in proj_group(0, 0):
            pass
        for grp in range(8):
            bg = proj_group(grp + 1, (grp + 1) % 2) if grp + 1 < 8 else None
            if MODE == "bg":
                run_attention(attn_steps(grp, grp % 2), cscale, PT2, rPT2, (0, 2), bg=bg, bg_every=4)
            elif MODE == "seq":
                run_attention(attn_steps(grp, grp % 2), cscale, PT2, rPT2, (0, 2))
                if bg is not None:
                    for _ in bg:
                        pass
            elif MODE == "projonly":
                if bg is not None:
                    for _ in bg:
                        pass
        P.barrier()
        AR.release(m2)
        out_proj("wo1", OT, rOT)
        AR.release(m0)

    def final_store():
        m = AR.mark()
        yT = AR.alloc(8 * S * 4, F32).rearrange("p (c n) -> p c n", c=8)
        rY = [[Res("y%d_%d" % (c, t)) for t in range(NB)] for c in range(8)]
        scratch = norm_scratch()
        rmsnorm_fm(hT, rH, 8, 39, 0, yT, rY, scratch)
        ys = [AR.alloc(4096, F32) for _ in range(3)]
        rYs = [Res("ys%d" % i) for i in range(3)]
        rOut = [Res("out%d" % i) for i in range(3)]
        for i in range(16):
            b = i % 3
            t = i // 4
            for half in range(2):
                bk = (2 * i + half) % 8
                for cc in range(4):
                    c = half * 4 + cc
                    P.op("pe", "transpose", BK[bk][:, cc * 128:(cc + 1) * 128], yT[:, c, i * 128:(i + 1) * 128],
                         identf[:], R=[rY[c][t], rIdent], W=[rBK[bk]])
                evac(ys[b][:, half * 512:(half + 1) * 512], BK[bk], R=[rBK[bk]], W=[rYs[b]])
            P.dma("sp", out_d[i * 128:(i + 1) * 128, :], ys[b], R=[rYs[b]], W=[rOut[b]])
        P.final_wait("sp", rOut)
        AR.release(m)

    def dump_h():
        rOut = Res("out")
        for c in range(8):
            P.dma("sp", out_d[:, c, :], hT[:, c, :], R=[rH[c][t] for t in range(NB)], W=[rOut])
        P.final_wait("sp", [rOut])

    phase_load()
    stages = ["x", "l0mix", "l0mlp", "l1mix", "l1mlp", "full"]
    si = stages.index(stage)
    if si >= 1:
        layer0_mix()
    if si >= 2:
        mlp(0, 16)
    if si >= 3:
        layer1_mix()
    if si >= 4:
        mlp(1, 24)
    if stage == "full":
        final_store()
    else:
        dump_h()
    P.emit(st)
    st.close()
    print('ops', {e: len(v) for e, v in P.ops.items()}, 'waits', P.nwaits)
    return nc


_CACHE = {}


def run(inputs, stage="full", ncores=8):
    if stage not in _CACHE:
        _CACHE[stage] = build(stage)
    nc = _CACHE[stage]
    x = np.asarray(inputs["x"], dtype=np.float32)
    sh = _pack_shared(inputs)
    in_maps = []
    for b in range(ncores):
        m = {"x": np.ascontiguousarray(x[b])}
        m.update(sh)
        in_maps.append(m)
    res = run_bass_kernel_spmd(nc, in_maps, core_ids=list(range(ncores)))
    return [r["out"] for r in res.results]


def kernel(**inputs):
    outs = run(inputs, "full")
    return np.stack(outs, axis=0).astype(np.float32)
```

```python
import math
import numpy as np
from contextlib import ExitStack
import concourse.bass as bass
import concourse.mybir as mybir
from concourse.bass_utils import run_bass_kernel_spmd

F32 = mybir.dt.float32
BF16 = mybir.dt.bfloat16
ALU = mybir.AluOpType
AF = mybir.ActivationFunctionType

S = 2048
D = 1024
NB = 4
NEG = -30000.0
EPS = 1e-6

ENGS = ("pe", "act", "dve", "pool", "sp")


class Res:
    __slots__ = ("name", "psum", "lw", "readers", "dma_tl")

    def __init__(self, name, psum=False):
        self.name = name
        self.psum = psum
        self.lw = None
        self.readers = []
        self.dma_tl = None


class _Op:
    __slots__ = ("eng", "meth", "args", "kw", "waits", "signal", "idx", "dma_tl")


class Prog:
    def __init__(self, nc):
        self.nc = nc
        self.ops = {e: [] for e in ENGS}
        self.seen = {e: {} for e in ENGS}
        self.snap = {}
        self.dma_count = {}
        self.dma_tls = []
        self.last_real = {}
        self.nwaits = 0

    def _deps(self, eng, own_tl, reads, writes):
        deps = []
        for r in reads:
            if r.lw is not None:
                if not (r.lw[0] == own_tl and eng == "pe"):
                    deps.append(r.lw)
            if r.psum:
                for ev in r.readers:
                    if ev[0] != own_tl:
                        deps.append(ev)
        same_ok = own_tl in ("act", "dve", "pool")
        for w in writes:
            if w.lw is not None and (w.lw[0] != own_tl or same_ok):
                deps.append(w.lw)
            for ev in w.readers:
                if ev[0] != own_tl or same_ok:
                    deps.append(ev)
        return deps

    def _resolve(self, eng, deps):
        seen = self.seen[eng]
        best = {}
        for tl, i in deps:
            if best.get(tl, 0) < i:
                best[tl] = i
        waits = [(tl, i) for tl, i in best.items() if seen.get(tl, 0) < i]
        for tl, i in waits:
            if seen.get(tl, 0) < i:
                seen[tl] = i
            sn = self.snap.get((tl, i))
            if sn:
                for t2, i2 in sn.items():
                    if seen.get(t2, 0) < i2:
                        seen[t2] = i2
        for tl, i in waits:
            if tl in self.ops:
                self.ops[tl][i - 1].signal = True
        return waits

    def op(self, eng, meth, *args, R=(), W=(), **kw):
        o = _Op()
        o.eng, o.meth, o.args, o.kw = eng, meth, args, kw
        o.signal = False
        o.dma_tl = None
        lst = self.ops[eng]
        o.idx = len(lst) + 1
        o.waits = self._resolve(eng, self._deps(eng, eng, R, W))
        lst.append(o)
        self.last_real[eng] = o.idx
        ev = (eng, o.idx)
        self.snap[ev] = dict(self.seen[eng])
        for r in R:
            r.readers.append(ev)
        for w in W:
            w.lw = ev
            w.readers = []
        return o

    def dma(self, queue, out, in_, R=(), W=(), **kw):
        o = _Op()
        o.eng, o.meth, o.args, o.kw = queue, "dma_start", (), dict(out=out, in_=in_, **kw)
        o.signal = False
        base = W[0] if W else R[0]
        if base.dma_tl is None:
            base.dma_tl = "dma:" + base.name
        tl = base.dma_tl
        if tl not in self.dma_count:
            self.dma_count[tl] = 0
            self.dma_tls.append(tl)
        o.dma_tl = tl
        lst = self.ops[queue]
        o.idx = len(lst) + 1
        o.waits = self._resolve(queue, self._deps(queue, tl, R, W))
        lst.append(o)
        self.dma_count[tl] += 1
        ev = (tl, self.dma_count[tl])
        self.snap[ev] = dict(self.seen[queue])
        for r in R:
            r.readers.append(ev)
        for w in W:
            w.lw = ev
            w.readers = []
        return o

    def _noop(self, eng, deps):
        o = _Op()
        o.eng, o.meth, o.args, o.kw = eng, None, (), {}
        o.signal = False
        o.dma_tl = None
        lst = self.ops[eng]
        o.idx = len(lst) + 1
        o.waits = self._resolve(eng, deps)
        lst.append(o)

    def final_wait(self, eng, resources):
        deps = []
        for r in resources:
            if r.lw is not None:
                deps.append(r.lw)
            deps.extend(r.readers)
        self._noop(eng, deps)

    def barrier(self):
        targets = [(e, i) for e, i in self.last_real.items() if i]
        targets += [(tl, c) for tl, c in self.dma_count.items() if c]
        for e in ENGS:
            deps = [t for t in targets if not (t[0] == e and e in ("pe", "sp"))]
            self._noop(e, deps)

    def emit(self, stack):
        nc = self.nc
        sems = {}
        for e in ENGS:
            if any(o.signal for o in self.ops[e]):
                sems[e] = stack.enter_context(nc.semaphore("s_" + e))
        for tl in self.dma_tls:
            sems[tl] = stack.enter_context(nc.semaphore("s_" + tl.replace(":", "_")))
        sigval = {}
        for e in ENGS:
            c = 0
            for o in self.ops[e]:
                if o.signal:
                    c += 1
                    sigval[(e, o.idx)] = c
        block = stack.enter_context(nc.Block())

        def run(ekey, eng):
            for o in self.ops[ekey]:
                for tl, i in o.waits:
                    if tl in self.ops:
                        eng.wait_ge(sems[tl], sigval[(tl, i)])
                    else:
                        eng.wait_ge(sems[tl], 16 * i)
                    self.nwaits += 1
                if o.meth is None:
                    continue
                ins = getattr(eng, o.meth)(*o.args, **o.kw)
                if o.dma_tl is not None:
                    ins.then_inc(sems[o.dma_tl], 16)
                elif o.signal:
                    ins.then_inc(sems[ekey], 1)

        @block.tensor
        def _(eng):
            run("pe", eng)

        @block.scalar
        def _(eng):
            run("act", eng)

        @block.vector
        def _(eng):
            run("dve", eng)

        @block.gpsimd
        def _(eng):
            run("pool", eng)

        @block.sync
        def _(eng):
            run("sp", eng)


class Arena:
    def __init__(self, tile, nbytes):
        self.tile = tile
        self.size = nbytes
        self.top = 0

    def mark(self):
        return self.top

    def release(self, m):
        self.top = m

    def alloc(self, nbytes, dt=BF16):
        nbytes = (nbytes + 63) // 64 * 64
        off = self.top
        assert off + nbytes <= self.size, ("arena overflow", off, nbytes, self.size)
        self.top += nbytes
        v = self.tile[:, off // 2:(off + nbytes) // 2]
        if dt == F32:
            v = v.bitcast(F32)
        return v


def bcast_mid(ap2, n):
    a = ap2.ap
    return bass.AP(ap2.tensor, ap2.offset, [list(a[0]), [0, n], list(a[1])])


def _t5_bucket(rel):
    nb = 16
    max_exact = 8
    base = np.where(rel > 0, nb, 0)
    n = np.abs(rel)
    nf = np.maximum(n, 1).astype(np.float32)
    large = max_exact + (np.log(nf / np.float32(max_exact)) / np.float32(math.log(1024 / max_exact))
                         * np.float32(nb - max_exact)).astype(np.int32)
    large = np.minimum(large, nb - 1)
    return base + np.where(n < max_exact, n, large)


def _rope_tables():
    def inv_freq(dim):
        return (10000.0 ** (-np.arange(0, dim, 2, dtype=np.float32) / np.float32(dim))).astype(np.float32)

    t = np.arange(S)
    row = (t // 64).astype(np.float32)
    col = (t % 64).astype(np.float32)
    f16 = inv_freq(32)
    ang_ax = np.concatenate([row[:, None] * f16[None, :], col[:, None] * f16[None, :]], axis=-1)
    cosA = np.cos(ang_ax).astype(np.float32)
    sinA = np.sin(ang_ax).astype(np.float32)
    ropeA = np.zeros((2, 128, S), np.float32)
    for p in range(128):
        dd = p % 64
        ropeA[0, p] = cosA[:, dd % 32]
        ropeA[1, p] = (-1.0 if dd < 32 else 1.0) * sinA[:, dd % 32]
    ang_c = t.astype(np.float32)[:, None] * inv_freq(32)[None, :]
    cosC = np.cos(ang_c).astype(np.float32)
    sinC = np.sin(ang_c).astype(np.float32)
    ropeC = np.zeros((2, 128, S), np.float32)
    for p in range(64, 96):
        dd = p - 64
        ropeC[0, p] = cosC[:, dd % 16]
        ropeC[1, p] = (-1.0 if dd < 16 else 1.0) * sinC[:, dd % 16]
    return ropeA, ropeC


B_DIL = (1, 4, 16)


def _pack_shared(inp):
    f = lambda a: np.ascontiguousarray(np.asarray(a, dtype=np.float32))
    sh = {}
    ropeA, ropeC = _rope_tables()
    sh["ropeA"] = ropeA
    sh["ropeC"] = ropeC
    sh["ident"] = np.eye(128, dtype=np.float32)
    nm = f(inp["norm_mix_g"])
    nf = f(inp["norm_mlp_g"])
    gq = f(inp["a_q_norm_g"])[0]
    gk = f(inp["a_k_norm_g"])[0]
    perm64 = np.concatenate([np.arange(32, 64), np.arange(0, 32)])
    cols = []
    for v in (nm[0], nm[1], nf[0], nf[1]):
        cols.append(v.reshape(8, 128).T)
    pidx = np.arange(128) % 64
    cols.append(np.stack([gq[pidx], gq[perm64][pidx], gk[pidx], gk[perm64][pidx]], axis=1))
    cols.append(f(inp["c_q_norm_g"])[0].reshape(2, 128).T)
    cols.append(f(inp["c_kv_norm_g"])[0].reshape(1, 128).T)
    cols.append(f(inp["final_norm_g"]).reshape(8, 128).T)
    sh["gvec"] = f(np.concatenate(cols, axis=1))
    Win = f(inp["ab_w_in"])[0]
    Wk = Win.reshape(8, 128, 2304).transpose(1, 0, 2)
    wa = np.zeros((2, 128, 8, 832), np.float32)
    for g in range(2):
        q = Wk[:, :, g * 256:(g + 1) * 256]
        qp = q.reshape(128, 8, 4, 64)[:, :, :, perm64].reshape(128, 8, 256)
        k = Wk[:, :, 512 + g * 64:512 + (g + 1) * 64]
        kp = k[:, :, perm64]
        v = Wk[:, :, 640 + g * 64:640 + (g + 1) * 64]
        wa[g] = np.concatenate([q, qp, k, k, kp, kp, v], axis=2)
    sh["wa"] = wa
    wb = np.zeros((4, 128, 8, 384), np.float32)
    for hp in range(4):
        wb[hp] = np.concatenate([Wk[:, :, 768 + hp * 128:768 + (hp + 1) * 128],
                                 Wk[:, :, 1280 + hp * 128:1280 + (hp + 1) * 128],
                                 Wk[:, :, 1792 + hp * 128:1792 + (hp + 1) * 128]], axis=2)
    sh["wb"] = wb
    sh["wo0"] = f(f(inp["ab_w_out"])[0].reshape(8, 128, 1024).transpose(1, 0, 2))
    sh["wo1"] = f(f(inp["c_w_out"])[0].reshape(8, 128, 1024).transpose(1, 0, 2))
    w1 = f(inp["mlp_w1"])
    w2 = f(inp["mlp_w2"])
    sh["w1"] = f(w1.reshape(2, 8, 128, 4, 1024).transpose(0, 3, 2, 1, 4))
    sh["w2"] = f(w2.reshape(2, 4, 8, 128, 1024).transpose(0, 1, 3, 2, 4))
    rb = f(inp["rel_bias"])
    k_i = np.arange(128)[:, None]
    v_i = np.arange(384)[None, :]
    xrel = k_i - v_i + 128
    valid = np.abs(xrel) <= 64
    tb = np.full((8, 128, 3, 384), NEG, np.float32)
    for bi, d in enumerate(B_DIL):
        bucket = _t5_bucket(np.clip(xrel, -64, 64) * d)
        for h in range(8):
            tb[h, :, bi, :] = np.where(valid, rb[bucket, h], np.float32(NEG))
    sh["tb"] = tb
    Wd = f(inp["c_w_down"])[0]
    perm32 = np.concatenate([np.arange(16, 32), np.arange(0, 16)])
    kr = Wd[:, 384:416]
    Wd2 = np.concatenate([Wd, kr[:, perm32]], axis=1)
    sh["wdn"] = f(Wd2.reshape(8, 128, 448).transpose(1, 0, 2))
    Wq = f(inp["c_w_uq"])[0].reshape(2, 128, 16, 96).transpose(1, 0, 2, 3)
    wuq = np.concatenate([Wq, Wq[:, :, :, 64:96][:, :, :, perm32]], axis=3)
    sh["wuq"] = f(wuq)
    Wkv = f(inp["c_w_ukv"])[0].reshape(128, 16, 128)
    sh["wukv"] = f(np.concatenate([Wkv[:, :, 0:64].reshape(128, 1024), Wkv[:, :, 64:128].reshape(128, 1024)], axis=1))
    return sh


SHARED_SHAPES = {
    "ropeA": [2, 128, S], "ropeC": [2, 128, S], "ident": [128, 128], "gvec": [128, 47],
    "wa": [2, 128, 8, 832], "wb": [4, 128, 8, 384], "wo0": [128, 8, 1024], "wo1": [128, 8, 1024],
    "w1": [2, 4, 128, 8, 1024], "w2": [2, 4, 128, 8, 1024], "tb": [8, 128, 3, 384],
    "wdn": [128, 8, 448], "wuq": [128, 2, 16, 128], "wukv": [128, 2048],
}


def build(stage="full"):
    nc = bass.Bass("TRN2", target_bir_lowering=False)
    dr = {}
    dr["x"] = nc.dram_tensor("x", [S, D], F32, kind="ExternalInput").ap()
    for k, shp in SHARED_SHAPES.items():
        dr[k] = nc.dram_tensor(k, shp, F32, kind="ExternalInput").ap()
    if stage == "full":
        out_d = nc.dram_tensor("out", [S, D], F32, kind="ExternalOutput").ap()
    else:
        out_d = nc.dram_tensor("out", [128, 8, S], F32, kind="ExternalOutput").ap()

    st = ExitStack()
    P = Prog(nc)

    def sbt(name, shape, dt=F32):
        return st.enter_context(nc.sbuf_tensor("sb_" + name, list(shape), dt))

    hT = sbt("hT", [128, 8, S], F32)
    rH = [[Res("h%d_%d" % (c, t)) for t in range(NB)] for c in range(8)]
    identf = sbt("identf", [128, 128], F32); rIdent = Res("identf")
    gvec = sbt("gvec", [128, 47], F32); rG = Res("gvec")
    cm = sbt("cm", [128, 4, 128], BF16); rCM = Res("cm")
    epst = sbt("epst", [128, 1], F32); rEps = Res("eps")
    identb = sbt("identb", [128, 128], BF16); rIdentb = Res("identb")
    arena_bytes = (nc.sbuf_bytes_remaining - 256) // 64 * 64
    print("arena bytes", arena_bytes)
    arena_t = sbt("arena", [128, arena_bytes // 2], BF16)
    AR = Arena(arena_t, arena_bytes)
    banks_t = st.enter_context(nc.psum_tensor("banks", [128, 8, 512], F32))
    BK = [banks_t[:, i, :] for i in range(8)]
    rBK = [Res("bank%d" % i, psum=True) for i in range(8)]

    P.dma("sp", identf[:], dr["ident"], W=[rIdent])
    P.dma("sp", gvec[:], dr["gvec"], W=[rG])
    P.op("dve", "memset", epst[:], EPS, W=[rEps])
    P.op("dve", "tensor_copy", out=identb[:], in_=identf[:], R=[rIdent], W=[rIdentb])
    P.op("pool", "memset", cm[:, 0, :], 1.0 / 1024, W=[rCM])
    P.op("pool", "memset", cm[:, 1, :], 0.0, W=[rCM])
    P.op("pool", "memset", cm[0:64, 1, 0:64], 1.0 / 64, W=[rCM])
    P.op("pool", "memset", cm[64:128, 1, 64:128], 1.0 / 64, W=[rCM])
    P.op("pool", "memset", cm[:, 2, :], 1.0 / 256, W=[rCM])
    P.op("pool", "memset", cm[:, 3, :], 1.0 / 128, W=[rCM])

    blk = lambda t: slice(t * 512, (t + 1) * 512)
    evac_rr = [0]

    def evac(out, in_, R, W):
        evac_rr[0] ^= 1
        if evac_rr[0]:
            P.op("act", "activation", out=out, in_=in_, func=AF.Copy, R=R, W=W)
        else:
            P.op("dve", "tensor_copy", out=out, in_=in_, R=R, W=W)

    def phase_load():
        m = AR.mark()
        xs = [AR.alloc(4096, F32) for _ in range(3)]
        rX = [Res("xs%d" % i) for i in range(3)]
        for i in range(16):
            b = i % 3
            P.dma("sp", xs[b], dr["x"][i * 128:(i + 1) * 128, :], W=[rX[b]])
            t = i // 4
            for half in range(2):
                bk = (2 * i + half) % 8
                for cc in range(4):
                    c = half * 4 + cc
                    P.op("pe", "transpose", BK[bk][:, cc * 128:(cc + 1) * 128], xs[b][:, c * 128:(c + 1) * 128],
                         identf[:], R=[rX[b], rIdent], W=[rBK[bk]])
                evac(hT[:, half * 4:half * 4 + 4, i * 128:(i + 1) * 128],
                     BK[bk].rearrange("p (a b) -> p a b", a=4),
                     R=[rBK[bk]], W=[rH[half * 4 + cc][t] for cc in range(4)])
        P.barrier()
        AR.release(m)

    def rms_block(t, srcs, rsrcs, gcol, cmi, dsts, rdsts, scratch):
        sq, lnv, rstd, rSq, rLn, rRs = scratch
        n = len(srcs)
        bk = 4 + (t % 2)
        for c in range(n):
            eng = "pool" if c % 2 == 0 else "dve"
            P.op(eng, "tensor_tensor", out=sq[:, c, :], in0=srcs[c], in1=srcs[c], op=ALU.mult, R=[rsrcs[c]], W=[rSq[c]])
        for c in range(n):
            P.op("pe", "matmul", BK[bk], cm[:, cmi, :], sq[:, c, :], start=(c == 0), stop=(c == n - 1),
                 R=[rCM, rSq[c]], W=[rBK[bk]])
        P.op("act", "activation", out=lnv, in_=BK[bk], func=AF.Ln, bias=epst[:], scale=1.0,
             R=[rBK[bk], rEps], W=[rLn])
        P.op("act", "activation", out=rstd, in_=lnv, func=AF.Exp, scale=-0.5, R=[rLn], W=[rRs])
        for c in range(n):
            eng = "dve"
            P.op(eng, "scalar_tensor_tensor", out=dsts[c], in0=srcs[c],
                 scalar=gvec[:, gcol + c:gcol + c + 1], in1=rstd, op0=ALU.mult, op1=ALU.mult,
                 R=[rsrcs[c], rG, rRs], W=[rdsts[c]])

    def rmsnorm_fm(src, rSrc, nchunks, gcol, cmi, dst, rDst, scratch):
        for t in range(NB):
            rms_block(t, [src[:, c, blk(t)] for c in range(nchunks)], [rSrc[c][t] for c in range(nchunks)], gcol, cmi,
                      [dst[:, c, blk(t)] for c in range(nchunks)], [rDst[c][t] for c in range(nchunks)], scratch)

    nsc = [0]

    def norm_scratch():
        nsc[0] += 1
        sq = AR.alloc(8 * 512 * 2, BF16).rearrange("p (c n) -> p c n", c=8)
        lnv = AR.alloc(2048, F32)
        rstd = AR.alloc(2048, F32)
        k = nsc[0]
        return sq, lnv, rstd, [Res("n%d_sq%d" % (k, c)) for c in range(8)], Res("n%d_ln" % k), Res("n%d_rs" % k)

    def run_attention(steps, scale, PT2, rPT2, spairs, bg=None, bg_every=4):
        n = len(steps)
        npair = len(spairs)
        la = npair - 1

        def qk(s_):
            sa = spairs[s_ % npair]
            for l, ln in enumerate(steps[s_][0]):
                P.op("pe", "matmul", BK[sa + l], ln[0], ln[1], start=True, stop=True,
                     R=[ln[2], ln[3]], W=[rBK[sa + l]])

        def ex(s_):
            sa = spairs[s_ % npair]
            k = s_ % len(PT2)
            P.op("act", "activation", out=PT2[k].rearrange("p (a n) -> p a n", a=2), in_=banks_t[:, sa:sa + 2, :],
                 func=AF.Exp, scale=scale, R=[rBK[sa], rBK[sa + 1]], W=[rPT2[k]])

        def pv(s_):
            k = s_ % len(PT2)
            for l, ln in enumerate(steps[s_][0]):
                P.op("pe", "matmul", BK[ln[6]], ln[4], PT2[k][:, l * 512:(l + 1) * 512], start=ln[7], stop=ln[8],
                     R=[ln[5], rPT2[k]], W=[rBK[ln[6]]])
            if steps[s_][1] is not None:
                steps[s_][1]()

        for s_ in range(min(la, n)):
            qk(s_)
        for s_ in range(n):
            ex(s_)
            if s_ + la < n:
                qk(s_ + la)
            pv(s_)
            if bg is not None and s_ % bg_every == bg_every - 1:
                next(bg, None)
        if bg is not None:
            for _ in bg:
                pass

    def normalize(ob, osel, dst, rDst, rec, rRec, on_act=False):
        orow = slice(0, 64) if osel == 0 else slice(64, 128)
        lrow = slice(64, 128) if osel == 0 else slice(0, 64)
        lc, rc = rec
        if on_act:
            P.op("act", "activation", out=lc[lrow, :], in_=BK[ob][lrow, :], func=AF.Ln, R=[rBK[ob]], W=[rRec[0]])
            P.op("act", "activation", out=rc[lrow, :], in_=lc[lrow, :], func=AF.Exp, scale=-1.0, R=[rRec[0]], W=[rRec[1]])
        else:
            P.op("dve", "reciprocal", out=rc[lrow, :], in_=BK[ob][lrow, :], R=[rBK[ob]], W=[rRec[1]])
        P.op("dve", "tensor_tensor", out=dst[orow, :], in0=BK[ob][orow, :], in1=rc[lrow, :],
             op=ALU.mult, R=[rBK[ob], rRec[1]], W=[rDst])

    def out_proj(wo_name, OT, rOT):
        m = AR.mark()
        wo = AR.alloc(8 * 1024 * 2, BF16).rearrange("p (c n) -> p c n", c=8)
        rWo = Res("wo_" + wo_name)
        for q in range(4):
            P.dma("pool", wo[:, 2 * q:2 * q + 2, :], dr[wo_name][:, 2 * q:2 * q + 2, :], W=[rWo])
        k = 0
        for t in range(NB):
            for dc in range(8):
                bk = k % 4
                k += 1
                for fc in range(8):
                    P.op("pe", "matmul", BK[bk], wo[:, fc, dc * 128:(dc + 1) * 128], OT[:, fc, blk(t)],
                         start=(fc == 0), stop=(fc == 7), R=[rWo, rOT[fc][t]], W=[rBK[bk]])
                P.op("dve", "tensor_tensor", out=hT[:, dc, blk(t)], in0=BK[bk], in1=hT[:, dc, blk(t)], op=ALU.add,
                     R=[rBK[bk], rH[dc][t]], W=[rH[dc][t]])
        P.barrier()
        AR.release(m)

    def mlp(layer, gcol):
        m = AR.mark()
        hn = AR.alloc(8 * S * 2, BF16).rearrange("p (c n) -> p c n", c=8)
        rHn = [[Res("mhn%d_%d" % (c, t)) for t in range(NB)] for c in range(8)]
        w1b = [AR.alloc(8 * 1024 * 2, BF16).rearrange("p (c n) -> p c n", c=8) for _ in range(2)]
        w2b = [AR.alloc(8 * 1024 * 2, BF16).rearrange("p (c n) -> p c n", c=8) for _ in range(2)]
        rW1 = [Res("w1b%d" % i) for i in range(2)]
        rW2 = [Res("w2b%d" % i) for i in range(2)]
        rbuf = [AR.alloc(2048, F32) for _ in range(3)]
        rR = [Res("mr%d" % i) for i in range(3)]
        uset = [AR.alloc(8 * 1024, BF16).rearrange("p (c n) -> p c n", c=8) for _ in range(2)]
        rU = [[Res("mu%d_%d" % (i, c)) for c in range(8)] for i in range(2)]

        def load_w(g):
            b = g % 2
            for q in range(2):
                P.dma("pool", w1b[b][:, 4 * q:4 * q + 4, :], dr["w1"][layer, g, :, 4 * q:4 * q + 4, :], W=[rW1[b]])
            for q in range(2):
                P.dma("pool", w2b[b][:, 4 * q:4 * q + 4, :], dr["w2"][layer, g, :, 4 * q:4 * q + 4, :], W=[rW2[b]])

        load_w(0)
        load_w(1)
        scratch = norm_scratch()
        rmsnorm_fm(hT, rH, 8, gcol, 0, hn, rHn, scratch)
        its = [(g, t) for g in range(4) for t in range(NB)]
        rcnt = [0]
        pcnt = [0]

        def up(k):
            g, t = its[k]
            b = g % 2
            us, rUs = uset[k % 2], rU[k % 2]
            for fcl in range(8):
                bk = 6 + (fcl % 2)
                for kc in range(8):
                    P.op("pe", "matmul", BK[bk], w1b[b][:, kc, fcl * 128:(fcl + 1) * 128], hn[:, kc, blk(t)],
                         start=(kc == 0), stop=(kc == 7), R=[rW1[b], rHn[kc][t]], W=[rBK[bk]])
                ri = rcnt[0] % 3
                rcnt[0] += 1
                P.op("act", "activation", out=rbuf[ri], in_=BK[bk], func=AF.Relu, R=[rBK[bk]], W=[rR[ri]])
                eng = "pool" if fcl % 2 == 0 else "dve"
                P.op(eng, "tensor_tensor", out=us[:, fcl, :], in0=rbuf[ri], in1=rbuf[ri], op=ALU.mult, R=[rR[ri]], W=[rUs[fcl]])

        def down(k):
            g, t = its[k]
            b = g % 2
            us, rUs = uset[k % 2], rU[k % 2]
            for dcl in ((0, 1, 2), (3, 4, 5), (6, 7)):
                banks = (0, 1, 2) if pcnt[0] % 2 == 0 else (3, 4, 5)
                pcnt[0] += 1
                for di, dc in enumerate(dcl):
                    bk = banks[di]
                    for fcl in range(8):
                        P.op("pe", "matmul", BK[bk], w2b[b][:, fcl, dc * 128:(dc + 1) * 128], us[:, fcl, :],
                             start=(fcl == 0), stop=(fcl == 7), R=[rW2[b], rUs[fcl]], W=[rBK[bk]])
                    P.op("dve", "tensor_tensor", out=hT[:, dc, blk(t)], in0=BK[bk], in1=hT[:, dc, blk(t)], op=ALU.add,
                         R=[rBK[bk], rH[dc][t]], W=[rH[dc][t]])

        up(0)
        for k in range(16):
            if k + 1 < 16:
                up(k + 1)
            down(k)
            g, t = its[k]
            if t == NB - 1 and g + 2 < 4:
                load_w(g + 2)
        P.barrier()
        AR.release(m)

    def layer0_mix():
        m0 = AR.mark()
        OT = AR.alloc(8 * S * 2, BF16).rearrange("p (c n) -> p c n", c=8)
        rOT = [[Res("ot%d_%d" % (c, t)) for t in range(NB)] for c in range(8)]
        hn = AR.alloc(8 * S * 2, BF16).rearrange("p (c n) -> p c n", c=8)
        rHn = [[Res("hn%d_%d" % (c, t)) for t in range(NB)] for c in range(8)]
        PT2 = [AR.alloc(2048, BF16) for _ in range(3)]
        rPT2 = [Res("pt2_%d" % i) for i in range(3)]
        PT = [PT2[i // 2][:, (i % 2) * 512:(i % 2 + 1) * 512] for i in range(4)]
        rPT = [rPT2[i // 2] for i in range(4)]
        rec = [(AR.alloc(2048, F32), AR.alloc(2048, F32)) for _ in range(2)]
        rRec = [(Res("recl%d" % i), Res("recr%d" % i)) for i in range(2)]
        m1 = AR.mark()
        scratch = norm_scratch()
        rmsnorm_fm(hT, rH, 8, 0, 0, hn, rHn, scratch)
        P.barrier()
        AR.release(m1)

        ropeA = AR.alloc(2 * S * 4, F32).rearrange("p (a n) -> p a n", a=2)
        rRope = Res("ropeA")
        P.dma("sp", ropeA[:, 0, :], dr["ropeA"][0], W=[rRope])
        P.dma("sp", ropeA[:, 1, :], dr["ropeA"][1], W=[rRope])
        wA1 = AR.alloc(8 * 832 * 2, BF16).rearrange("p (c n) -> p c n", c=8)
        rWA1 = Res("wA")
        wA = [wA1, wA1]
        rWA = [rWA1, rWA1]
        QA = AR.alloc(2 * S * 2, BF16).rearrange("p (c n) -> p c n", c=2)
        KA = AR.alloc(S * 2, BF16)
        VA = AR.alloc(16 * 192 * 2, BF16).rearrange("p (i n) -> p i n", i=16)
        qf = [AR.alloc(2048, F32) for _ in range(2)]
        sqb = AR.alloc(1024, BF16)
        lnv = AR.alloc(2048, F32)
        rstd = AR.alloc(2048, F32)
        t1 = AR.alloc(2048, F32)
        t2 = AR.alloc(2048, F32)
        rQf = [Res("qf0"), Res("qf1")]
        rSq, rLn, rRs, rT1, rT2 = Res("a_sq"), Res("a_ln"), Res("a_rs"), Res("a_t1"), Res("a_t2")
        rQA = [Res("QA0"), Res("QA1")]
        rKA = Res("KA")
        rVA = Res("VA")
        P.op("pool", "memset", VA[:, :, 0:64], 1.0, W=[rVA])
        P.op("pool", "memset", VA[:, :, 128:192], 1.0, W=[rVA])

        def proj_norm_rope(w, rW, c0, cp0, gc, dstT, rDst):
            for t in range(NB):
                for kc in range(8):
                    P.op("pe", "matmul", BK[4], w[:, kc, c0:c0 + 128], hn[:, kc, blk(t)], start=(kc == 0), stop=(kc == 7),
                         R=[rW, rHn[kc][t]], W=[rBK[4]])
                for kc in range(8):
                    P.op("pe", "matmul", BK[5], w[:, kc, cp0:cp0 + 128], hn[:, kc, blk(t)], start=(kc == 0), stop=(kc == 7),
                         R=[rW, rHn[kc][t]], W=[rBK[5]])
                P.op("dve", "tensor_copy", out=qf[0], in_=BK[4], R=[rBK[4]], W=[rQf[0]])
                P.op("act", "activation", out=qf[1], in_=BK[5], func=AF.Copy, scale=gvec[:, gc + 1:gc + 2],
                     R=[rBK[5], rG], W=[rQf[1]])
                P.op("pool", "tensor_tensor", out=sqb, in0=qf[0], in1=qf[0], op=ALU.mult, R=[rQf[0]], W=[rSq])
                P.op("pe", "matmul", BK[4], cm[:, 1, :], sqb, start=True, stop=True, R=[rCM, rSq], W=[rBK[4]])
                P.op("act", "activation", out=lnv, in_=BK[4], func=AF.Ln, bias=epst[:], scale=1.0,
                     R=[rBK[4], rEps], W=[rLn])
                P.op("act", "activation", out=rstd, in_=lnv, func=AF.Exp, scale=-0.5, R=[rLn], W=[rRs])
                P.op("dve", "scalar_tensor_tensor", out=t1, in0=qf[0], scalar=gvec[:, gc:gc + 1], in1=ropeA[:, 0, blk(t)],
                     op0=ALU.mult, op1=ALU.mult, R=[rQf[0], rG, rRope], W=[rT1])
                P.op("dve", "tensor_tensor", out=t2, in0=qf[1], in1=ropeA[:, 1, blk(t)], op=ALU.mult, R=[rQf[1], rRope], W=[rT2])
                P.op("pool", "tensor_tensor", out=t1, in0=t1, in1=t2, op=ALU.add, R=[rT1, rT2], W=[rT1])
                P.op("dve", "tensor_tensor", out=dstT[:, blk(t)], in0=t1, in1=rstd, op=ALU.mult, R=[rT1, rRs], W=[rDst])

        for g in range(2):
            w = wA[g]
            for q in range(2):
                P.dma("pool", w[:, 4 * q:4 * q + 4, :], dr["wa"][g, :, 4 * q:4 * q + 4, :], W=[rWA[g]])
            for j in range(2):
                proj_norm_rope(w, rWA[g], j * 128, 256 + j * 128, 32, QA[:, j, :], rQA[j])
            proj_norm_rope(w, rWA[g], 512, 640, 34, KA, rKA)
            for half in range(2):
                bk = 4 + half
                for ii in range(8):
                    i = half * 8 + ii
                    for kc in range(8):
                        P.op("pe", "matmul", BK[bk][:, ii * 64:(ii + 1) * 64], hn[:, kc, i * 128:(i + 1) * 128],
                             w[:, kc, 768:832], start=(kc == 0), stop=(kc == 7), skip_group_check=True,
                             R=[rWA[g], rHn[kc][i // 4]], W=[rBK[bk]])
                evac(VA[:, half * 8:half * 8 + 8, 64:128], BK[bk].rearrange("p (i n) -> p i n", i=8),
                     R=[rBK[bk]], W=[rVA])
            steps = []
            for j in range(2):
                for t in range(NB):
                    obs = (4 + 2 * (t % 2), 5 + 2 * (t % 2))
                    for i in range(16):
                        lanes = []
                        for e in range(2):
                            rows = slice(e * 64, (e + 1) * 64)
                            v = VA[:, i, 64:192] if e == 0 else VA[:, i, 0:128]
                            lanes.append((KA[rows, i * 128:(i + 1) * 128], QA[rows, j, blk(t)], rKA, rQA[j], v, rVA,
                                          obs[e], i == 0, i == 15))
                        post = None
                        if i == 15:
                            def post(j=j, t=t, obs=obs):
                                for e in range(2):
                                    normalize(obs[e], e, OT[:, 2 * g + j, blk(t)], rOT[2 * g + j][t], rec[e], rRec[e])
                        steps.append((lanes, post))
            run_attention(steps, 0.125, PT2, rPT2, (0, 2))
        P.barrier()
        AR.release(m1)

        wB = AR.alloc(8 * 384 * 2, BF16).rearrange("p (c n) -> p c n", c=8)
        rWB = Res("wB")
        QB = AR.alloc(S * 2, BF16)
        KB = AR.alloc(S * 2, BF16)
        VB = AR.alloc(3 * 16 * 192 * 2, BF16).rearrange("p (b i n) -> p b i n", b=3, i=16)
        rQB, rKB, rVB = Res("QB"), Res("KB"), Res("VB")
        VT = AR.alloc(S * 2, BF16)
        rVT = Res("VT")
        TBs = [AR.alloc(3 * 384 * 4, F32).rearrange("p (b n) -> p b n", b=3) for _ in range(2)]
        rTB = [Res("tb0"), Res("tb1")]
        tmp = [AR.alloc(2048, F32) for _ in range(3)]
        rTmp = [Res("btmp%d" % i) for i in range(3)]
        PTb = [AR.alloc(1024, BF16) for _ in range(4)]
        rPTb = [Res("bpt%d" % i) for i in range(4)]
        lnl = AR.alloc(2048, F32)
        rLnl = Res("b_lnl")
        P.op("pool", "memset", VB[:, :, :, 64:128], 1.0, W=[rVB])
        sb_rr = [0]
        tmp_rr = [0]
        pt_rr = [0]
        ob_rr = [0]
        for hp in range(4):
            for q in range(2):
                P.dma("pool", wB[:, 4 * q:4 * q + 4, :], dr["wb"][hp, :, 4 * q:4 * q + 4, :], W=[rWB])
            for e in range(2):
                P.dma("sp", TBs[e], dr["tb"][2 * hp + e], W=[rTB[e]])
            for (c0, dstT, rDst) in ((0, QB, rQB), (128, KB, rKB)):
                for t in range(NB):
                    bk = 5 + (t % 2)
                    for kc in range(8):
                        P.op("pe", "matmul", BK[bk], wB[:, kc, c0:c0 + 128], hn[:, kc, blk(t)], start=(kc == 0), stop=(kc == 7),
                             R=[rWB, rHn[kc][t]], W=[rBK[bk]])
                    evac(dstT[:, blk(t)], BK[bk], R=[rBK[bk]], W=[rDst])
            for t in range(NB):
                bk = 5 + (t % 2)
                for kc in range(8):
                    P.op("pe", "matmul", BK[bk], wB[:, kc, 256:384], hn[:, kc, blk(t)], start=(kc == 0), stop=(kc == 7),
                         R=[rWB, rHn[kc][t]], W=[rBK[bk]])
                evac(VT[:, blk(t)], BK[bk], R=[rBK[bk]], W=[rVT])
            for bi, d in enumerate(B_DIL):
                nkt = (S // d) // 128
                for grp in range(4):
                    bk = 5 + (grp % 2)
                    bkb = BK[bk].bitcast(BF16)
                    for q in range(4):
                        ti = grp * 4 + q
                        r_, kt = ti // nkt, ti % nkt
                        s0 = kt * 128 * d + r_
                        tok = slice(s0, s0 + 127 * d + 1, d)
                        P.op("pe", "transpose", bkb[:, q * 128:(q + 1) * 128], VT[:, tok], identb[:],
                             R=[rVT, rIdentb], W=[rBK[bk]])
                    dst = VB[:, bi, grp * 4:grp * 4 + 4, :].rearrange("p i (a n) -> p i a n", a=3)[:, :, 0:3:2, :]
                    evac(dst, bkb[:, 0:512].rearrange("p (i a n) -> p i a n", i=4, a=2), R=[rBK[bk]], W=[rVB])
            flat = []
            for e in range(2):
                for c in range(NB):
                    ob = 6 + (ob_rr[0] % 2)
                    ob_rr[0] += 1
                    jobs = []
                    for di, delta in enumerate((128, 0, -128)):
                        regs = []
                        for qtl in range(4):
                            kt = 4 * c + qtl + delta // 128
                            if 0 <= kt < 16:
                                regs.append((qtl * 128, 128, kt, slice(kt * 128, kt * 128 + 128),
                                             slice(c * 512 + qtl * 128, c * 512 + qtl * 128 + 128),
                                             slice(qtl * 128, qtl * 128 + 128)))
                        if regs:
                            jobs.append((0, regs, slice(128 - delta, 256 - delta), 128))
                    for di, delta in enumerate((128, 0, -128)):
                        kt = c + delta // 128
                        if 0 <= kt < 4:
                            regs = []
                            for r_ in range(4):
                                ks = kt * 512 + r_
                                qs = c * 512 + r_
                                regs.append((r_ * 128, 128, r_ * 4 + kt, slice(ks, ks + 127 * 4 + 1, 4),
                                             slice(qs, qs + 127 * 4 + 1, 4), slice(r_, r_ + 127 * 4 + 1, 4)))
                            jobs.append((1, regs, slice(128 - delta, 256 - delta), 128))
                    regs = []
                    for r_ in range(16):
                        qs = c * 512 + r_
                        regs.append((r_ * 32, 32, r_, slice(r_, r_ + 127 * 16 + 1, 16),
                                     slice(qs, qs + 31 * 16 + 1, 16), slice(r_, r_ + 31 * 16 + 1, 16)))
                    jobs.append((2, regs, slice(128 + c * 32, 128 + c * 32 + 32), 32))
                    for ji, jb in enumerate(jobs):
                        flat.append((jb, ob, e, ji == 0, ji == len(jobs) - 1, c))
            nj = len(flat)
            LA = 3

            def b_qk(j):
                (bi, regs, mcols, w), ob, e, isfirst, islast, c = flat[j]
                rows = slice(e * 64, (e + 1) * 64)
                sb_ = j % 5
                for (col0, w_, ti, ksl, qsl, osl) in regs:
                    P.op("pe", "matmul", BK[sb_][:, col0:col0 + w_], KB[rows, ksl], QB[rows, qsl],
                         start=True, stop=True, skip_group_check=True, R=[rKB, rQB], W=[rBK[sb_]])

            def b_mid(j):
                (bi, regs, mcols, w), ob, e, isfirst, islast, c = flat[j]
                sb_ = j % 5
                c_lo = regs[0][0]
                c_hi = regs[-1][0] + w
                nreg = len(regs)
                tm, rTm = tmp[j % 3], rTmp[j % 3]
                P.op("dve", "scalar_tensor_tensor",
                     out=tm[:, c_lo:c_hi].rearrange("p (a n) -> p a n", a=nreg),
                     in0=BK[sb_][:, c_lo:c_hi].rearrange("p (a n) -> p a n", a=nreg),
                     scalar=0.125, in1=bcast_mid(TBs[e][:, bi, mcols], nreg),
                     op0=ALU.mult, op1=ALU.add, R=[rBK[sb_], rTB[e]], W=[rTm])
                P.op("act", "activation", out=PTb[j % 4][:, c_lo:c_hi], in_=tm[:, c_lo:c_hi], func=AF.Exp,
                     R=[rTm], W=[rPTb[j % 4]])

            def b_pv(j):
                (bi, regs, mcols, w), ob, e, isfirst, islast, c = flat[j]
                vsl = slice(0, 128) if e == 0 else slice(64, 192)
                orow = slice(0, 64) if e == 0 else slice(64, 128)
                lrow = slice(64, 128) if e == 0 else slice(0, 64)
                pt = PTb[j % 4]
                for ri, (col0, w_, ti, ksl, qsl, osl) in enumerate(regs):
                    P.op("pe", "matmul", BK[ob][:, osl], VB[:, bi, ti, vsl], pt[:, col0:col0 + w_],
                         start=(isfirst and ri == 0), stop=(islast and ri == len(regs) - 1), skip_group_check=True,
                         R=[rVB, rPTb[j % 4]], W=[rBK[ob]])
                if islast:
                    rr = c % 2
                    P.op("act", "activation", out=lnl[lrow, :], in_=BK[ob][lrow, :], func=AF.Ln, R=[rBK[ob]], W=[rLnl])
                    P.op("act", "activation", out=rec[rr][1][lrow, :], in_=lnl[lrow, :], func=AF.Exp, scale=-1.0,
                         R=[rLnl], W=[rRec[rr][1]])
                    P.op("dve", "tensor_tensor", out=OT[orow, 4 + hp, blk(c)], in0=BK[ob][orow, :], in1=rec[rr][1][lrow, :],
                         op=ALU.mult, R=[rBK[ob], rRec[rr][1]], W=[rOT[4 + hp][c]])

            for j in range(min(LA, nj)):
                b_qk(j)
            for j in range(nj):
                b_mid(j)
                if j + LA < nj:
                    b_qk(j + LA)
                b_pv(j)
        P.barrier()
        AR.release(m1)
        out_proj("wo0", OT, rOT)
        AR.release(m0)

    def layer1_mix():
        m0 = AR.mark()
        PT2 = [AR.alloc(2048, BF16) for _ in range(3)]
        rPT2 = [Res("c_pt2_%d" % i) for i in range(3)]
        rec = [(AR.alloc(2048, F32), AR.alloc(2048, F32)) for _ in range(2)]
        rRec = [(Res("c_recl%d" % i), Res("c_recr%d" % i)) for i in range(2)]
        cqn = AR.alloc(2 * S * 2, BF16).rearrange("p (c n) -> p c n", c=2)
        ckvn = AR.alloc(S * 2, BF16).rearrange("p (c n) -> p c n", c=1)
        krT = AR.alloc(S * 2, BF16)
        rCqn = [[Res("cqn%d_%d" % (c, t)) for t in range(NB)] for c in range(2)]
        rCkvn = [[Res("ckvn_%d" % t) for t in range(NB)]]
        rKr = Res("krT")
        ropeC = AR.alloc(2 * S * 4, F32).rearrange("p (a n) -> p a n", a=2)
        rRope = Res("ropeC")
        P.dma("sp", ropeC[:, 0, :], dr["ropeC"][0], W=[rRope])
        P.dma("sp", ropeC[:, 1, :], dr["ropeC"][1], W=[rRope])
        wuq = AR.alloc(2 * 16 * 128 * 2, BF16).rearrange("p (c h n) -> p c h n", c=2, h=16)
        wukv = AR.alloc(2048 * 2, BF16)
        rWuq, rWukv = Res("wuq"), Res("wukv")
        for c in range(2):
            P.dma("pool", wuq[:, c], dr["wuq"][:, c], W=[rWuq])
        P.dma("pool", wukv, dr["wukv"], W=[rWukv])
        t1 = AR.alloc(2048, F32)
        t2 = AR.alloc(2048, F32)
        rT1, rT2 = Res("c_t1"), Res("c_t2")
        m1 = AR.mark()
        hn = AR.alloc(8 * S * 2, BF16).rearrange("p (c n) -> p c n", c=8)
        rHn = [[Res("c_hn%d_%d" % (c, t)) for t in range(NB)] for c in range(8)]
        wdn = AR.alloc(8 * 448 * 2, BF16).rearrange("p (c n) -> p c n", c=8)
        rWdn = Res("wdn")
        for q in range(2):
            P.dma("pool", wdn[:, 4 * q:4 * q + 4, :], dr["wdn"][:, 4 * q:4 * q + 4, :], W=[rWdn])
        cblk = [AR.alloc(3 * 2048, F32).rearrange("p (c n) -> p c n", c=3) for _ in range(2)]
        rCb = [[Res("cb%d_%d" % (i, c)) for c in range(3)] for i in range(2)]
        scratch = norm_scratch()
        rmsnorm_fm(hT, rH, 8, 8, 0, hn, rHn, scratch)
        rr_ = slice(64, 96)
        for t in range(NB):
            cb = cblk[t % 2]
            rC = rCb[t % 2]
            for ci in range(3):
                c0 = ci * 128
                bk = 6 + (ci % 2)
                for kc in range(8):
                    P.op("pe", "matmul", BK[bk], wdn[:, kc, c0:c0 + 128], hn[:, kc, blk(t)], start=(kc == 0), stop=(kc == 7),
                         R=[rWdn, rHn[kc][t]], W=[rBK[bk]])
                evac(cb[:, ci, :], BK[bk], R=[rBK[bk]], W=[rC[ci]])
            for (c0, bk) in ((384, 2), (416, 3)):
                for kc in range(8):
                    P.op("pe", "matmul", BK[bk][64:96, :], wdn[:, kc, c0:c0 + 32], hn[:, kc, blk(t)],
                         start=(kc == 0), stop=(kc == 7), R=[rWdn, rHn[kc][t]], W=[rBK[bk]])
            P.op("dve", "tensor_tensor", out=t1[rr_, :], in0=BK[2][rr_, :], in1=ropeC[rr_, 0, blk(t)], op=ALU.mult,
                 R=[rBK[2], rRope], W=[rT1])
            P.op("dve", "tensor_tensor", out=t2[rr_, :], in0=BK[3][rr_, :], in1=ropeC[rr_, 1, blk(t)], op=ALU.mult,
                 R=[rBK[3], rRope], W=[rT2])
            P.op("pool", "tensor_tensor", out=krT[rr_, blk(t)], in0=t1[rr_, :], in1=t2[rr_, :], op=ALU.add,
                 R=[rT1, rT2], W=[rKr])
            rms_block(t, [cb[:, 0, :], cb[:, 1, :]], [rC[0], rC[1]], 36, 2,
                      [cqn[:, 0, blk(t)], cqn[:, 1, blk(t)]], [rCqn[0][t], rCqn[1][t]], scratch)
            rms_block(t, [cb[:, 2, :]], [rC[2]], 38, 3, [ckvn[:, 0, blk(t)]], [rCkvn[0][t]], scratch)
        P.barrier()
        AR.release(m1)
        OT = AR.alloc(8 * S * 2, BF16).rearrange("p (c n) -> p c n", c=8)
        rOT = [[Res("c_ot%d_%d" % (c, t)) for t in range(NB)] for c in range(8)]
        m2 = AR.mark()
        sets = []
        for si in range(2):
            QT = AR.alloc(2 * S * 2, BF16).rearrange("p (h n) -> p h n", h=2)
            KT = AR.alloc(2 * S * 2, BF16).rearrange("p (h n) -> p h n", h=2)
            VC = AR.alloc(16 * 192 * 2, BF16).rearrange("p (i n) -> p i n", i=16)
            rQT = [Res("QT%d_%d" % (si, i)) for i in range(2)]
            rKT = [Res("KT%d_%d" % (si, i)) for i in range(2)]
            rVC = Res("VC%d" % si)
            P.op("pool", "memset", VC[:, :, 64:128], 1.0, W=[rVC])
            sets.append((QT, KT, VC, rQT, rKT, rVC))
        t1s = [t1, t1]
        t2s = [t2, t2]
        rT1s = [rT1, rT1]
        rT2s = [rT2, rT2]
        cscale = 96.0 ** -0.5
        ucnt = [0]

        def proj_group(grp, si):
            QT, KT, VC, rQT, rKT, rVC = sets[si]
            for hl in range(2):
                h = 2 * grp + hl
                for t in range(NB):
                    u = ucnt[0] % 2
                    ucnt[0] += 1
                    for kc in range(2):
                        P.op("pe", "matmul", BK[4][0:96, :], wuq[:, kc, h, 0:96], cqn[:, kc, blk(t)], start=(kc == 0), stop=(kc == 1),
                             R=[rWuq, rCqn[kc][t]], W=[rBK[4]])
                    for kc in range(2):
                        P.op("pe", "matmul", BK[5][64:96, :], wuq[:, kc, h, 96:128], cqn[:, kc, blk(t)], start=(kc == 0), stop=(kc == 1),
                             R=[rWuq, rCqn[kc][t]], W=[rBK[5]])
                    P.op("pe", "matmul", BK[5][0:64, :], wukv[:, h * 64:(h + 1) * 64], ckvn[:, 0, blk(t)], start=True, stop=True,
                         R=[rWukv, rCkvn[0][t]], W=[rBK[5]])
                    P.op("dve", "tensor_copy", out=QT[0:64, hl, blk(t)], in_=BK[4][0:64, :], R=[rBK[4]], W=[rQT[hl]])
                    P.op("dve", "tensor_tensor", out=t1s[u][rr_, :], in0=BK[4][rr_, :], in1=ropeC[rr_, 0, blk(t)], op=ALU.mult,
                         R=[rBK[4], rRope], W=[rT1s[u]])
                    P.op("dve", "tensor_tensor", out=t2s[u][rr_, :], in0=BK[5][rr_, :], in1=ropeC[rr_, 1, blk(t)], op=ALU.mult,
                         R=[rBK[5], rRope], W=[rT2s[u]])
                    P.op("dve", "tensor_copy", out=KT[0:64, hl, blk(t)], in_=BK[5][0:64, :], R=[rBK[5]], W=[rKT[hl]])
                    P.op("pool", "tensor_tensor", out=QT[rr_, hl, blk(t)], in0=t1s[u][rr_, :], in1=t2s[u][rr_, :], op=ALU.add,
                         R=[rT1s[u], rT2s[u]], W=[rQT[hl]])
                    yield
                P.op("dve", "tensor_copy", out=KT[rr_, hl, :], in_=krT[rr_, :], R=[rKr], W=[rKT[hl]])
                yield
            for i4 in range(4):
                for q in range(4):
                    i = i4 * 4 + q
                    P.op("pe", "matmul", BK[4][:, q * 128:(q + 1) * 128], ckvn[:, 0, i * 128:(i + 1) * 128],
                         wukv[:, 1024 + grp * 128:1024 + (grp + 1) * 128], start=True, stop=True,
                         skip_group_check=True, R=[rWukv, rCkvn[0][i // 4]], W=[rBK[4]])
                dst = VC[:, i4 * 4:i4 * 4 + 4, :].rearrange("p i (b n) -> p i b n", b=3)[:, :, 0:3:2, :]
                P.op("dve", "tensor_copy", out=dst, in_=BK[4].rearrange("p (i b n) -> p i b n", i=4, b=2),
                     R=[rBK[4]], W=[rVC])
                yield

        def attn_steps(grp, si):
            QT, KT, VC, rQT, rKT, rVC = sets[si]
            steps = []
            for hl in range(2):
                h = 2 * grp + hl
                e = hl
                for t in range(NB):
                    ob = 6 + (t % 2)
                    for i2 in range(8):
                        lanes = []
                        for l in range(2):
                            i = 2 * i2 + l
                            v = VC[:, i, 0:128] if e == 0 else VC[:, i, 64:192]
                            lanes.append((KT[0:96, hl, i * 128:(i + 1) * 128], QT[0:96, hl, blk(t)], rKT[hl], rQT[hl], v, rVC,
                                          ob, i == 0, i == 15))
                        post = None
                        if i2 == 7:
                            def post(h=h, e=e, t=t, ob=ob):
                                normalize(ob, e, OT[:, h // 2, blk(t)], rOT[h // 2][t], rec[t % 2], rRec[t % 2], on_act=False)
                        steps.append((lanes, post))
            return steps

        import os
        MODE = os.environ.get("MLA_MODE", "bg")
        for _ in proj_group(0, 0):
            pass
        for grp in range(8):
            bg = proj_group(grp + 1, (grp + 1) % 2) if grp + 1 < 8 else None
            if MODE == "bg":
                run_attention(attn_steps(grp, grp % 2), cscale, PT2, rPT2, (0, 2), bg=bg, bg_every=4)
            elif MODE == "seq":
                run_attention(attn_steps(grp, grp % 2), cscale, PT2, rPT2, (0, 2))
                if bg is not None:
                    for _ in bg:
                        pass
            elif MODE == "projonly":
                if bg is not None:
                    for _ in bg:
                        pass
        P.barrier()
        AR.release(m2)
        out_proj("wo1", OT, rOT)
        AR.release(m0)

    def final_store():
        m = AR.mark()
        yT = AR.alloc(8 * S * 4, F32).rearrange("p (c n) -> p c n", c=8)
        rY = [[Res("y%d_%d" % (c, t)) for t in range(NB)] for c in range(8)]
        scratch = norm_scratch()
        rmsnorm_fm(hT, rH, 8, 39, 0, yT, rY, scratch)
        ys = [AR.alloc(4096, F32) for _ in range(3)]
        rYs = [Res("ys%d" % i) for i in range(3)]
        rOut = [Res("out%d" % i) for i in range(3)]
        for i in range(16):
            b = i % 3
            t = i // 4
            for half in range(2):
                bk = (2 * i + half) % 8
                for cc in range(4):
                    c = half * 4 + cc
                    P.op("pe", "transpose", BK[bk][:, cc * 128:(cc + 1) * 128], yT[:, c, i * 128:(i + 1) * 128],
                         identf[:], R=[rY[c][t], rIdent], W=[rBK[bk]])
                evac(ys[b][:, half * 512:(half + 1) * 512], BK[bk], R=[rBK[bk]], W=[rYs[b]])
            P.dma("sp", out_d[i * 128:(i + 1) * 128, :], ys[b], R=[rYs[b]], W=[rOut[b]])
        P.final_wait("sp", rOut)
        AR.release(m)

    def dump_h():
        rOut = Res("out")
        for c in range(8):
            P.dma("sp", out_d[:, c, :], hT[:, c, :], R=[rH[c][t] for t in range(NB)], W=[rOut])
        P.final_wait("sp", [rOut])

    phase_load()
    stages = ["x", "l0mix", "l0mlp", "l1mix", "l1mlp", "full"]
    si = stages.index(stage)
    if si >= 1:
        layer0_mix()
    if si >= 2:
        mlp(0, 16)
    if si >= 3:
        layer1_mix()
    if si >= 4:
        mlp(1, 24)
    if stage == "full":
        final_store()
    else:
        dump_h()
    P.emit(st)
    st.close()
    print('ops', {e: len(v) for e, v in P.ops.items()}, 'waits', P.nwaits)
    return nc


_CACHE = {}


def run(inputs, stage="full", ncores=8):
    if stage not in _CACHE:
        _CACHE[stage] = build(stage)
    nc = _CACHE[stage]
    x = np.asarray(inputs["x"], dtype=np.float32)
    sh = _pack_shared(inputs)
    in_maps = []
    for b in range(ncores):
        m = {"x": np.ascontiguousarray(x[b])}
        m.update(sh)
        in_maps.append(m)
    res = run_bass_kernel_spmd(nc, in_maps, core_ids=list(range(ncores)))
    return [r["out"] for r in res.results]


def kernel(**inputs):
    outs = run(inputs, "full")
    return np.stack(outs, axis=0).astype(np.float32)
```

```python
import math
import numpy as np
from contextlib import ExitStack
import concourse.bass as bass
import concourse.mybir as mybir
from concourse.bass_utils import run_bass_kernel_spmd

F32 = mybir.dt.float32
BF16 = mybir.dt.bfloat16
ALU = mybir.AluOpType
AF = mybir.ActivationFunctionType

S = 2048
D = 1024
NB = 4
NEG = -30000.0
EPS = 1e-6

ENGS = ("pe", "act", "dve", "pool", "sp")


class Res:
    __slots__ = ("name", "psum", "lw", "readers", "dma_tl")

    def __init__(self, name, psum=False):
        self.name = name
        self.psum = psum
        self.lw = None
        self.readers = []
        self.dma_tl = None


class _Op:
    __slots__ = ("eng", "meth", "args", "kw", "waits", "signal", "idx", "dma_tl")


class Prog:
    def __init__(self, nc):
        self.nc = nc
        self.ops = {e: [] for e in ENGS}
        self.seen = {e: {} for e in ENGS}
        self.snap = {}
        self.dma_count = {}
        self.dma_tls = []
        self.last_real = {}
        self.nwaits = 0

    def _deps(self, eng, own_tl, reads, writes):
        deps = []
        for r in reads:
            if r.lw is not None:
                if not (r.lw[0] == own_tl and eng == "pe"):
                    deps.append(r.lw)
            if r.psum:
                for ev in r.readers:
                    if ev[0] != own_tl:
                        deps.append(ev)
        same_ok = own_tl in ("act", "dve", "pool")
        for w in writes:
            if w.lw is not None and (w.lw[0] != own_tl or same_ok):
                deps.append(w.lw)
            for ev in w.readers:
                if ev[0] != own_tl or same_ok:
                    deps.append(ev)
        return deps

    def _resolve(self, eng, deps):
        seen = self.seen[eng]
        best = {}
        for tl, i in deps:
            if best.get(tl, 0) < i:
                best[tl] = i
        waits = [(tl, i) for tl, i in best.items() if seen.get(tl, 0) < i]
        for tl, i in waits:
            if seen.get(tl, 0) < i:
                seen[tl] = i
            sn = self.snap.get((tl, i))
            if sn:
                for t2, i2 in sn.items():
                    if seen.get(t2, 0) < i2:
                        seen[t2] = i2
        for tl, i in waits:
            if tl in self.ops:
                self.ops[tl][i - 1].signal = True
        return waits

    def op(self, eng, meth, *args, R=(), W=(), **kw):
        o = _Op()
        o.eng, o.meth, o.args, o.kw = eng, meth, args, kw
        o.signal = False
        o.dma_tl = None
        lst = self.ops[eng]
        o.idx = len(lst) + 1
        o.waits = self._resolve(eng, self._deps(eng, eng, R, W))
        lst.append(o)
        self.last_real[eng] = o.idx
        ev = (eng, o.idx)
        self.snap[ev] = dict(self.seen[eng])
        for r in R:
            r.readers.append(ev)
        for w in W:
            w.lw = ev
            w.readers = []
        return o

    def dma(self, queue, out, in_, R=(), W=(), **kw):
        o = _Op()
        o.eng, o.meth, o.args, o.kw = queue, "dma_start", (), dict(out=out, in_=in_, **kw)
        o.signal = False
        base = W[0] if W else R[0]
        if base.dma_tl is None:
            base.dma_tl = "dma:" + base.name
        tl = base.dma_tl
        if tl not in self.dma_count:
            self.dma_count[tl] = 0
            self.dma_tls.append(tl)
        o.dma_tl = tl
        lst = self.ops[queue]
        o.idx = len(lst) + 1
        o.waits = self._resolve(queue, self._deps(queue, tl, R, W))
        lst.append(o)
        self.dma_count[tl] += 1
        ev = (tl, self.dma_count[tl])
        self.snap[ev] = dict(self.seen[queue])
        for r in R:
            r.readers.append(ev)
        for w in W:
            w.lw = ev
            w.readers = []
        return o

    def _noop(self, eng, deps):
        o = _Op()
        o.eng, o.meth, o.args, o.kw = eng, None, (), {}
        o.signal = False
        o.dma_tl = None
        lst = self.ops[eng]
        o.idx = len(lst) + 1
        o.waits = self._resolve(eng, deps)
        lst.append(o)

    def final_wait(self, eng, resources):
        deps = []
        for r in resources:
            if r.lw is not None:
                deps.append(r.lw)
            deps.extend(r.readers)
        self._noop(eng, deps)

    def barrier(self):
        targets = [(e, i) for e, i in self.last_real.items() if i]
        targets += [(tl, c) for tl, c in self.dma_count.items() if c]
        for e in ENGS:
            deps = [t for t in targets if not (t[0] == e and e in ("pe", "sp"))]
            self._noop(e, deps)

    def emit(self, stack):
        nc = self.nc
        sems = {}
        for e in ENGS:
            if any(o.signal for o in self.ops[e]):
                sems[e] = stack.enter_context(nc.semaphore("s_" + e))
        for tl in self.dma_tls:
            sems[tl] = stack.enter_context(nc.semaphore("s_" + tl.replace(":", "_")))
        sigval = {}
        for e in ENGS:
            c = 0
            for o in self.ops[e]:
                if o.signal:
                    c += 1
                    sigval[(e, o.idx)] = c
        block = stack.enter_context(nc.Block())

        def run(ekey, eng):
            for o in self.ops[ekey]:
                for tl, i in o.waits:
                    if tl in self.ops:
                        eng.wait_ge(sems[tl], sigval[(tl, i)])
                    else:
                        eng.wait_ge(sems[tl], 16 * i)
                    self.nwaits += 1
                if o.meth is None:
                    continue
                ins = getattr(eng, o.meth)(*o.args, **o.kw)
                if o.dma_tl is not None:
                    ins.then_inc(sems[o.dma_tl], 16)
                elif o.signal:
                    ins.then_inc(sems[ekey], 1)

        @block.tensor
        def _(eng):
            run("pe", eng)

        @block.scalar
        def _(eng):
            run("act", eng)

        @block.vector
        def _(eng):
            run("dve", eng)

        @block.gpsimd
        def _(eng):
            run("pool", eng)

        @block.sync
        def _(eng):
            run("sp", eng)


class Arena:
    def __init__(self, tile, nbytes):
        self.tile = tile
        self.size = nbytes
        self.top = 0

    def mark(self):
        return self.top

    def release(self, m):
        self.top = m

    def alloc(self, nbytes, dt=BF16):
        nbytes = (nbytes + 63) // 64 * 64
        off = self.top
        assert off + nbytes <= self.size, ("arena overflow", off, nbytes, self.size)
        self.top += nbytes
        v = self.tile[:, off // 2:(off + nbytes) // 2]
        if dt == F32:
            v = v.bitcast(F32)
        return v


def bcast_mid(ap2, n):
    a = ap2.ap
    return bass.AP(ap2.tensor, ap2.offset, [list(a[0]), [0, n], list(a[1])])


def _t5_bucket(rel):
    nb = 16
    max_exact = 8
    base = np.where(rel > 0, nb, 0)
    n = np.abs(rel)
    nf = np.maximum(n, 1).astype(np.float32)
    large = max_exact + (np.log(nf / np.float32(max_exact)) / np.float32(math.log(1024 / max_exact))
                         * np.float32(nb - max_exact)).astype(np.int32)
    large = np.minimum(large, nb - 1)
    return base + np.where(n < max_exact, n, large)


def _rope_tables():
    def inv_freq(dim):
        return (10000.0 ** (-np.arange(0, dim, 2, dtype=np.float32) / np.float32(dim))).astype(np.float32)

    t = np.arange(S)
    row = (t // 64).astype(np.float32)
    col = (t % 64).astype(np.float32)
    f16 = inv_freq(32)
    ang_ax = np.concatenate([row[:, None] * f16[None, :], col[:, None] * f16[None, :]], axis=-1)
    cosA = np.cos(ang_ax).astype(np.float32)
    sinA = np.sin(ang_ax).astype(np.float32)
    ropeA = np.zeros((2, 128, S), np.float32)
    for p in range(128):
        dd = p % 64
        ropeA[0, p] = cosA[:, dd % 32]
        ropeA[1, p] = (-1.0 if dd < 32 else 1.0) * sinA[:, dd % 32]
    ang_c = t.astype(np.float32)[:, None] * inv_freq(32)[None, :]
    cosC = np.cos(ang_c).astype(np.float32)
    sinC = np.sin(ang_c).astype(np.float32)
    ropeC = np.zeros((2, 128, S), np.float32)
    for p in range(64, 96):
        dd = p - 64
        ropeC[0, p] = cosC[:, dd % 16]
        ropeC[1, p] = (-1.0 if dd < 16 else 1.0) * sinC[:, dd % 16]
    return ropeA, ropeC


B_DIL = (1, 4, 16)


def _pack_shared(inp):
    f = lambda a: np.ascontiguousarray(np.asarray(a, dtype=np.float32))
    sh = {}
    ropeA, ropeC = _rope_tables()
    sh["ropeA"] = ropeA
    sh["ropeC"] = ropeC
    sh["ident"] = np.eye(128, dtype=np.float32)
    nm = f(inp["norm_mix_g"])
    nf = f(inp["norm_mlp_g"])
    gq = f(inp["a_q_norm_g"])[0]
    gk = f(inp["a_k_norm_g"])[0]
    perm64 = np.concatenate([np.arange(32, 64), np.arange(0, 32)])
    cols = []
    for v in (nm[0], nm[1], nf[0], nf[1]):
        cols.append(v.reshape(8, 128).T)
    pidx = np.arange(128) % 64
    cols.append(np.stack([gq[pidx], gq[perm64][pidx], gk[pidx], gk[perm64][pidx]], axis=1))
    cols.append(f(inp["c_q_norm_g"])[0].reshape(2, 128).T)
    cols.append(f(inp["c_kv_norm_g"])[0].reshape(1, 128).T)
    cols.append(f(inp["final_norm_g"]).reshape(8, 128).T)
    sh["gvec"] = f(np.concatenate(cols, axis=1))
    Win = f(inp["ab_w_in"])[0]
    Wk = Win.reshape(8, 128, 2304).transpose(1, 0, 2)
    wa = np.zeros((2, 128, 8, 832), np.float32)
    for g in range(2):
        q = Wk[:, :, g * 256:(g + 1) * 256]
        qp = q.reshape(128, 8, 4, 64)[:, :, :, perm64].reshape(128, 8, 256)
        k = Wk[:, :, 512 + g * 64:512 + (g + 1) * 64]
        kp = k[:, :, perm64]
        v = Wk[:, :, 640 + g * 64:640 + (g + 1) * 64]
        wa[g] = np.concatenate([q, qp, k, k, kp, kp, v], axis=2)
    sh["wa"] = wa
    wb = np.zeros((4, 128, 8, 384), np.float32)
    for hp in range(4):
        wb[hp] = np.concatenate([Wk[:, :, 768 + hp * 128:768 + (hp + 1) * 128],
                                 Wk[:, :, 1280 + hp * 128:1280 + (hp + 1) * 128],
                                 Wk[:, :, 1792 + hp * 128:1792 + (hp + 1) * 128]], axis=2)
    sh["wb"] = wb
    sh["wo0"] = f(f(inp["ab_w_out"])[0].reshape(8, 128, 1024).transpose(1, 0, 2))
    sh["wo1"] = f(f(inp["c_w_out"])[0].reshape(8, 128, 1024).transpose(1, 0, 2))
    w1 = f(inp["mlp_w1"])
    w2 = f(inp["mlp_w2"])
    sh["w1"] = f(w1.reshape(2, 8, 128, 4, 1024).transpose(0, 3, 2, 1, 4))
    sh["w2"] = f(w2.reshape(2, 4, 8, 128, 1024).transpose(0, 1, 3, 2, 4))
    rb = f(inp["rel_bias"])
    k_i = np.arange(128)[:, None]
    v_i = np.arange(384)[None, :]
    xrel = k_i - v_i + 128
    valid = np.abs(xrel) <= 64
    tb = np.full((8, 128, 3, 384), NEG, np.float32)
    for bi, d in enumerate(B_DIL):
        bucket = _t5_bucket(np.clip(xrel, -64, 64) * d)
        for h in range(8):
            tb[h, :, bi, :] = np.where(valid, rb[bucket, h], np.float32(NEG))
    sh["tb"] = tb
    Wd = f(inp["c_w_down"])[0]
    perm32 = np.concatenate([np.arange(16, 32), np.arange(0, 16)])
    kr = Wd[:, 384:416]
    Wd2 = np.concatenate([Wd, kr[:, perm32]], axis=1)
    sh["wdn"] = f(Wd2.reshape(8, 128, 448).transpose(1, 0, 2))
    Wq = f(inp["c_w_uq"])[0].reshape(2, 128, 16, 96).transpose(1, 0, 2, 3)
    wuq = np.concatenate([Wq, Wq[:, :, :, 64:96][:, :, :, perm32]], axis=3)
    sh["wuq"] = f(wuq)
    Wkv = f(inp["c_w_ukv"])[0].reshape(128, 16, 128)
    sh["wukv"] = f(np.concatenate([Wkv[:, :, 0:64].reshape(128, 1024), Wkv[:, :, 64:128].reshape(128, 1024)], axis=1))
    return sh


SHARED_SHAPES = {
    "ropeA": [2, 128, S], "ropeC": [2, 128, S], "ident": [128, 128], "gvec": [128, 47],
    "wa": [2, 128, 8, 832], "wb": [4, 128, 8, 384], "wo0": [128, 8, 1024], "wo1": [128, 8, 1024],
    "w1": [2, 4, 128, 8, 1024], "w2": [2, 4, 128, 8, 1024], "tb": [8, 128, 3, 384],
    "wdn": [128, 8, 448], "wuq": [128, 2, 16, 128], "wukv": [128, 2048],
}


def build(stage="full"):
    nc = bass.Bass("TRN2", target_bir_lowering=False)
    dr = {}
    dr["x"] = nc.dram_tensor("x", [S, D], F32, kind="ExternalInput").ap()
    for k, shp in SHARED_SHAPES.items():
        dr[k] = nc.dram_tensor(k, shp, F32, kind="ExternalInput").ap()
    if stage == "full":
        out_d = nc.dram_tensor("out", [S, D], F32, kind="ExternalOutput").ap()
    else:
        out_d = nc.dram_tensor("out", [128, 8, S], F32, kind="ExternalOutput").ap()

    st = ExitStack()
    P = Prog(nc)

    def sbt(name, shape, dt=F32):
        return st.enter_context(nc.sbuf_tensor("sb_" + name, list(shape), dt))

    hT = sbt("hT", [128, 8, S], F32)
    rH = [[Res("h%d_%d" % (c, t)) for t in range(NB)] for c in range(8)]
    identf = sbt("identf", [128, 128], F32); rIdent = Res("identf")
    gvec = sbt("gvec", [128, 47], F32); rG = Res("gvec")
    cm = sbt("cm", [128, 4, 128], BF16); rCM = Res("cm")
    epst = sbt("epst", [128, 1], F32); rEps = Res("eps")
    identb = sbt("identb", [128, 128], BF16); rIdentb = Res("identb")
    arena_bytes = (nc.sbuf_bytes_remaining - 256) // 64 * 64
    print("arena bytes", arena_bytes)
    arena_t = sbt("arena", [128, arena_bytes // 2], BF16)
    AR = Arena(arena_t, arena_bytes)
    banks_t = st.enter_context(nc.psum_tensor("banks", [128, 8, 512], F32))
    BK = [banks_t[:, i, :] for i in range(8)]
    rBK = [Res("bank%d" % i, psum=True) for i in range(8)]

    P.dma("sp", identf[:], dr["ident"], W=[rIdent])
    P.dma("sp", gvec[:], dr["gvec"], W=[rG])
    P.op("dve", "memset", epst[:], EPS, W=[rEps])
    P.op("dve", "tensor_copy", out=identb[:], in_=identf[:], R=[rIdent], W=[rIdentb])
    P.op("pool", "memset", cm[:, 0, :], 1.0 / 1024, W=[rCM])
    P.op("pool", "memset", cm[:, 1, :], 0.0, W=[rCM])
    P.op("pool", "memset", cm[0:64, 1, 0:64], 1.0 / 64, W=[rCM])
    P.op("pool", "memset", cm[64:128, 1, 64:128], 1.0 / 64, W=[rCM])
    P.op("pool", "memset", cm[:, 2, :], 1.0 / 256, W=[rCM])
    P.op("pool", "memset", cm[:, 3, :], 1.0 / 128, W=[rCM])

    blk = lambda t: slice(t * 512, (t + 1) * 512)
    evac_rr = [0]

    def evac(out, in_, R, W):
        evac_rr[0] ^= 1
        if evac_rr[0]:
            P.op("act", "activation", out=out, in_=in_, func=AF.Copy, R=R, W=W)
        else:
            P.op("dve", "tensor_copy", out=out, in_=in_, R=R, W=W)

    def phase_load():
        m = AR.mark()
        xs = [AR.alloc(4096, F32) for _ in range(3)]
        rX = [Res("xs%d" % i) for i in range(3)]
        for i in range(16):
            b = i % 3
            P.dma("sp", xs[b], dr["x"][i * 128:(i + 1) * 128, :], W=[rX[b]])
            t = i // 4
            for half in range(2):
                bk = (2 * i + half) % 8
                for cc in range(4):
                    c = half * 4 + cc
                    P.op("pe", "transpose", BK[bk][:, cc * 128:(cc + 1) * 128], xs[b][:, c * 128:(c + 1) * 128],
                         identf[:], R=[rX[b], rIdent], W=[rBK[bk]])
                evac(hT[:, half * 4:half * 4 + 4, i * 128:(i + 1) * 128],
                     BK[bk].rearrange("p (a b) -> p a b", a=4),
                     R=[rBK[bk]], W=[rH[half * 4 + cc][t] for cc in range(4)])
        AR.release(m)

    def rms_block(t, srcs, rsrcs, gcol, cmi, dsts, rdsts, scratch):
        sq, lnv, rstd, rSq, rLn, rRs = scratch
        n = len(srcs)
        bk = 4 + (t % 2)
        for c in range(n):
            eng = "pool" if c % 2 == 0 else "dve"
            P.op(eng, "tensor_tensor", out=sq[:, c, :], in0=srcs[c], in1=srcs[c], op=ALU.mult, R=[rsrcs[c]], W=[rSq[c]])
        for c in range(n):
            P.op("pe", "matmul", BK[bk], cm[:, cmi, :], sq[:, c, :], start=(c == 0), stop=(c == n - 1),
                 R=[rCM, rSq[c]], W=[rBK[bk]])
        P.op("act", "activation", out=lnv, in_=BK[bk], func=AF.Ln, bias=epst[:], scale=1.0,
             R=[rBK[bk], rEps], W=[rLn])
        P.op("act", "activation", out=rstd, in_=lnv, func=AF.Exp, scale=-0.5, R=[rLn], W=[rRs])
        for c in range(n):
            eng = "dve"
            P.op(eng, "scalar_tensor_tensor", out=dsts[c], in0=srcs[c],
                 scalar=gvec[:, gcol + c:gcol + c + 1], in1=rstd, op0=ALU.mult, op1=ALU.mult,
                 R=[rsrcs[c], rG, rRs], W=[rdsts[c]])

    def rmsnorm_fm(src, rSrc, nchunks, gcol, cmi, dst, rDst, scratch):
        for t in range(NB):
            rms_block(t, [src[:, c, blk(t)] for c in range(nchunks)], [rSrc[c][t] for c in range(nchunks)], gcol, cmi,
                      [dst[:, c, blk(t)] for c in range(nchunks)], [rDst[c][t] for c in range(nchunks)], scratch)

    nsc = [0]

    def norm_scratch():
        nsc[0] += 1
        sq = AR.alloc(8 * 512 * 2, BF16).rearrange("p (c n) -> p c n", c=8)
        lnv = AR.alloc(2048, F32)
        rstd = AR.alloc(2048, F32)
        k = nsc[0]
        return sq, lnv, rstd, [Res("n%d_sq%d" % (k, c)) for c in range(8)], Res("n%d_ln" % k), Res("n%d_rs" % k)

    def run_attention(steps, scale, PT2, rPT2, spairs, bg=None, bg_every=4):
        n = len(steps)
        npair = len(spairs)
        la = npair - 1

        def qk(s_):
            sa = spairs[s_ % npair]
            for l, ln in enumerate(steps[s_][0]):
                P.op("pe", "matmul", BK[sa + l], ln[0], ln[1], start=True, stop=True,
                     R=[ln[2], ln[3]], W=[rBK[sa + l]])

        def ex(s_):
            sa = spairs[s_ % npair]
            k = s_ % len(PT2)
            P.op("act", "activation", out=PT2[k].rearrange("p (a n) -> p a n", a=2), in_=banks_t[:, sa:sa + 2, :],
                 func=AF.Exp, scale=scale, R=[rBK[sa], rBK[sa + 1]], W=[rPT2[k]])

        def pv(s_):
            k = s_ % len(PT2)
            for l, ln in enumerate(steps[s_][0]):
                P.op("pe", "matmul", BK[ln[6]], ln[4], PT2[k][:, l * 512:(l + 1) * 512], start=ln[7], stop=ln[8],
                     R=[ln[5], rPT2[k]], W=[rBK[ln[6]]])
            if steps[s_][1] is not None:
                steps[s_][1]()

        for s_ in range(min(la, n)):
            qk(s_)
        for s_ in range(n):
            ex(s_)
            if s_ + la < n:
                qk(s_ + la)
            pv(s_)
            if bg is not None and s_ % bg_every == bg_every - 1:
                next(bg, None)
        if bg is not None:
            for _ in bg:
                pass

    def normalize(ob, osel, dst, rDst, rec, rRec, on_act=False):
        orow = slice(0, 64) if osel == 0 else slice(64, 128)
        lrow = slice(64, 128) if osel == 0 else slice(0, 64)
        lc, rc = rec
        if on_act:
            P.op("act", "activation", out=lc[lrow, :], in_=BK[ob][lrow, :], func=AF.Ln, R=[rBK[ob]], W=[rRec[0]])
            P.op("act", "activation", out=rc[lrow, :], in_=lc[lrow, :], func=AF.Exp, scale=-1.0, R=[rRec[0]], W=[rRec[1]])
        else:
            P.op("dve", "reciprocal", out=rc[lrow, :], in_=BK[ob][lrow, :], R=[rBK[ob]], W=[rRec[1]])
        P.op("dve", "tensor_tensor", out=dst[orow, :], in0=BK[ob][orow, :], in1=rc[lrow, :],
             op=ALU.mult, R=[rBK[ob], rRec[1]], W=[rDst])

    def out_proj(wo_name, OT, rOT):
        m = AR.mark()
        wo = AR.alloc(8 * 1024 * 2, BF16).rearrange("p (c n) -> p c n", c=8)
        rWo = Res("wo_" + wo_name)
        for q in range(4):
            P.dma("pool", wo[:, 2 * q:2 * q + 2, :], dr[wo_name][:, 2 * q:2 * q + 2, :], W=[rWo])
        k = 0
        for t in range(NB):
            for dc in range(8):
                bk = k % 4
                k += 1
                for fc in range(8):
                    P.op("pe", "matmul", BK[bk], wo[:, fc, dc * 128:(dc + 1) * 128], OT[:, fc, blk(t)],
                         start=(fc == 0), stop=(fc == 7), R=[rWo, rOT[fc][t]], W=[rBK[bk]])
                P.op("dve", "tensor_tensor", out=hT[:, dc, blk(t)], in0=BK[bk], in1=hT[:, dc, blk(t)], op=ALU.add,
                     R=[rBK[bk], rH[dc][t]], W=[rH[dc][t]])
        P.barrier()
        AR.release(m)

    def mlp(layer, gcol):
        m = AR.mark()
        hn = AR.alloc(8 * S * 2, BF16).rearrange("p (c n) -> p c n", c=8)
        rHn = [[Res("mhn%d_%d" % (c, t)) for t in range(NB)] for c in range(8)]
        w1b = [AR.alloc(8 * 1024 * 2, BF16).rearrange("p (c n) -> p c n", c=8) for _ in range(2)]
        w2b = [AR.alloc(8 * 1024 * 2, BF16).rearrange("p (c n) -> p c n", c=8) for _ in range(2)]
        rW1 = [Res("w1b%d" % i) for i in range(2)]
        rW2 = [Res("w2b%d" % i) for i in range(2)]
        rbuf = [AR.alloc(2048, F32) for _ in range(3)]
        rR = [Res("mr%d" % i) for i in range(3)]
        uset = [AR.alloc(8 * 1024, BF16).rearrange("p (c n) -> p c n", c=8) for _ in range(2)]
        rU = [[Res("mu%d_%d" % (i, c)) for c in range(8)] for i in range(2)]

        def load_w(g):
            b = g % 2
            for q in range(2):
                P.dma("pool", w1b[b][:, 4 * q:4 * q + 4, :], dr["w1"][layer, g, :, 4 * q:4 * q + 4, :], W=[rW1[b]])
            for q in range(2):
                P.dma("pool", w2b[b][:, 4 * q:4 * q + 4, :], dr["w2"][layer, g, :, 4 * q:4 * q + 4, :], W=[rW2[b]])

        load_w(0)
        load_w(1)
        scratch = norm_scratch()
        rmsnorm_fm(hT, rH, 8, gcol, 0, hn, rHn, scratch)
        its = [(g, t) for g in range(4) for t in range(NB)]
        rcnt = [0]
        pcnt = [0]

        def up(k):
            g, t = its[k]
            b = g % 2
            us, rUs = uset[k % 2], rU[k % 2]
            for fcl in range(8):
                bk = 6 + (fcl % 2)
                for kc in range(8):
                    P.op("pe", "matmul", BK[bk], w1b[b][:, kc, fcl * 128:(fcl + 1) * 128], hn[:, kc, blk(t)],
                         start=(kc == 0), stop=(kc == 7), R=[rW1[b], rHn[kc][t]], W=[rBK[bk]])
                ri = rcnt[0] % 3
                rcnt[0] += 1
                P.op("act", "activation", out=rbuf[ri], in_=BK[bk], func=AF.Relu, R=[rBK[bk]], W=[rR[ri]])
                eng = "pool" if fcl % 2 == 0 else "dve"
                P.op(eng, "tensor_tensor", out=us[:, fcl, :], in0=rbuf[ri], in1=rbuf[ri], op=ALU.mult, R=[rR[ri]], W=[rUs[fcl]])

        def down(k):
            g, t = its[k]
            b = g % 2
            us, rUs = uset[k % 2], rU[k % 2]
            for dcl in ((0, 1, 2), (3, 4, 5), (6, 7)):
                banks = (0, 1, 2) if pcnt[0] % 2 == 0 else (3, 4, 5)
                pcnt[0] += 1
                for di, dc in enumerate(dcl):
                    bk = banks[di]
                    for fcl in range(8):
                        P.op("pe", "matmul", BK[bk], w2b[b][:, fcl, dc * 128:(dc + 1) * 128], us[:, fcl, :],
                             start=(fcl == 0), stop=(fcl == 7), R=[rW2[b], rUs[fcl]], W=[rBK[bk]])
                    P.op("dve", "tensor_tensor", out=hT[:, dc, blk(t)], in0=BK[bk], in1=hT[:, dc, blk(t)], op=ALU.add,
                         R=[rBK[bk], rH[dc][t]], W=[rH[dc][t]])

        up(0)
        for k in range(16):
            if k + 1 < 16:
                up(k + 1)
            down(k)
            g, t = its[k]
            if t == NB - 1 and g + 2 < 4:
                load_w(g + 2)
        P.barrier()
        AR.release(m)

    def layer0_mix():
        m0 = AR.mark()
        OT = AR.alloc(8 * S * 2, BF16).rearrange("p (c n) -> p c n", c=8)
        rOT = [[Res("ot%d_%d" % (c, t)) for t in range(NB)] for c in range(8)]
        hn = AR.alloc(8 * S * 2, BF16).rearrange("p (c n) -> p c n", c=8)
        rHn = [[Res("hn%d_%d" % (c, t)) for t in range(NB)] for c in range(8)]
        PT2 = [AR.alloc(2048, BF16) for _ in range(3)]
        rPT2 = [Res("pt2_%d" % i) for i in range(3)]
        PT = [PT2[i // 2][:, (i % 2) * 512:(i % 2 + 1) * 512] for i in range(4)]
        rPT = [rPT2[i // 2] for i in range(4)]
        rec = [(AR.alloc(2048, F32), AR.alloc(2048, F32)) for _ in range(2)]
        rRec = [(Res("recl%d" % i), Res("recr%d" % i)) for i in range(2)]
        m1 = AR.mark()
        scratch = norm_scratch()
        rmsnorm_fm(hT, rH, 8, 0, 0, hn, rHn, scratch)
        P.barrier()
        AR.release(m1)

        ropeA = AR.alloc(2 * S * 4, F32).rearrange("p (a n) -> p a n", a=2)
        rRope = Res("ropeA")
        P.dma("sp", ropeA[:, 0, :], dr["ropeA"][0], W=[rRope])
        P.dma("sp", ropeA[:, 1, :], dr["ropeA"][1], W=[rRope])
        wA1 = AR.alloc(8 * 832 * 2, BF16).rearrange("p (c n) -> p c n", c=8)
        rWA1 = Res("wA")
        wA = [wA1, wA1]
        rWA = [rWA1, rWA1]
        QA = AR.alloc(2 * S * 2, BF16).rearrange("p (c n) -> p c n", c=2)
        KA = AR.alloc(S * 2, BF16)
        VA = AR.alloc(16 * 192 * 2, BF16).rearrange("p (i n) -> p i n", i=16)
        qf = [AR.alloc(2048, F32) for _ in range(2)]
        sqb = AR.alloc(1024, BF16)
        lnv = AR.alloc(2048, F32)
        rstd = AR.alloc(2048, F32)
        t1 = AR.alloc(2048, F32)
        t2 = AR.alloc(2048, F32)
        rQf = [Res("qf0"), Res("qf1")]
        rSq, rLn, rRs, rT1, rT2 = Res("a_sq"), Res("a_ln"), Res("a_rs"), Res("a_t1"), Res("a_t2")
        rQA = [Res("QA0"), Res("QA1")]
        rKA = Res("KA")
        rVA = Res("VA")
        P.op("pool", "memset", VA[:, :, 0:64], 1.0, W=[rVA])
        P.op("pool", "memset", VA[:, :, 128:192], 1.0, W=[rVA])

        def proj_norm_rope(w, rW, c0, cp0, gc, dstT, rDst):
            for t in range(NB):
                for kc in range(8):
                    P.op("pe", "matmul", BK[4], w[:, kc, c0:c0 + 128], hn[:, kc, blk(t)], start=(kc == 0), stop=(kc == 7),
                         R=[rW, rHn[kc][t]], W=[rBK[4]])
                for kc in range(8):
                    P.op("pe", "matmul", BK[5], w[:, kc, cp0:cp0 + 128], hn[:, kc, blk(t)], start=(kc == 0), stop=(kc == 7),
                         R=[rW, rHn[kc][t]], W=[rBK[5]])
                P.op("dve", "tensor_copy", out=qf[0], in_=BK[4], R=[rBK[4]], W=[rQf[0]])
                P.op("act", "activation", out=qf[1], in_=BK[5], func=AF.Copy, scale=gvec[:, gc + 1:gc + 2],
                     R=[rBK[5], rG], W=[rQf[1]])
                P.op("pool", "tensor_tensor", out=sqb, in0=qf[0], in1=qf[0], op=ALU.mult, R=[rQf[0]], W=[rSq])
                P.op("pe", "matmul", BK[4], cm[:, 1, :], sqb, start=True, stop=True, R=[rCM, rSq], W=[rBK[4]])
                P.op("act", "activation", out=lnv, in_=BK[4], func=AF.Ln, bias=epst[:], scale=1.0,
                     R=[rBK[4], rEps], W=[rLn])
                P.op("act", "activation", out=rstd, in_=lnv, func=AF.Exp, scale=-0.5, R=[rLn], W=[rRs])
                P.op("dve", "scalar_tensor_tensor", out=t1, in0=qf[0], scalar=gvec[:, gc:gc + 1], in1=ropeA[:, 0, blk(t)],
                     op0=ALU.mult, op1=ALU.mult, R=[rQf[0], rG, rRope], W=[rT1])
                P.op("dve", "tensor_tensor", out=t2, in0=qf[1], in1=ropeA[:, 1, blk(t)], op=ALU.mult, R=[rQf[1], rRope], W=[rT2])
                P.op("pool", "tensor_tensor", out=t1, in0=t1, in1=t2, op=ALU.add, R=[rT1, rT2], W=[rT1])
                P.op("dve", "tensor_tensor", out=dstT[:, blk(t)], in0=t1, in1=rstd, op=ALU.mult, R=[rT1, rRs], W=[rDst])

        for g in range(2):
            w = wA[g]
            for q in range(2):
                P.dma("pool", w[:, 4 * q:4 * q + 4, :], dr["wa"][g, :, 4 * q:4 * q + 4, :], W=[rWA[g]])
            for j in range(2):
                proj_norm_rope(w, rWA[g], j * 128, 256 + j * 128, 32, QA[:, j, :], rQA[j])
            proj_norm_rope(w, rWA[g], 512, 640, 34, KA, rKA)
            for half in range(2):
                bk = 4 + half
                for ii in range(8):
                    i = half * 8 + ii
                    for kc in range(8):
                        P.op("pe", "matmul", BK[bk][:, ii * 64:(ii + 1) * 64], hn[:, kc, i * 128:(i + 1) * 128],
                             w[:, kc, 768:832], start=(kc == 0), stop=(kc == 7), skip_group_check=True,
                             R=[rWA[g], rHn[kc][i // 4]], W=[rBK[bk]])
                evac(VA[:, half * 8:half * 8 + 8, 64:128], BK[bk].rearrange("p (i n) -> p i n", i=8),
                     R=[rBK[bk]], W=[rVA])
            steps = []
            for j in range(2):
                for t in range(NB):
                    obs = (4 + 2 * (t % 2), 5 + 2 * (t % 2))
                    for i in range(16):
                        lanes = []
                        for e in range(2):
                            rows = slice(e * 64, (e + 1) * 64)
                            v = VA[:, i, 64:192] if e == 0 else VA[:, i, 0:128]
                            lanes.append((KA[rows, i * 128:(i + 1) * 128], QA[rows, j, blk(t)], rKA, rQA[j], v, rVA,
                                          obs[e], i == 0, i == 15))
                        post = None
                        if i == 15:
                            def post(j=j, t=t, obs=obs):
                                for e in range(2):
                                    normalize(obs[e], e, OT[:, 2 * g + j, blk(t)], rOT[2 * g + j][t], rec[e], rRec[e])
                        steps.append((lanes, post))
            run_attention(steps, 0.125, PT2, rPT2, (0, 2))
        P.barrier()
        AR.release(m1)

        wB = AR.alloc(8 * 384 * 2, BF16).rearrange("p (c n) -> p c n", c=8)
        rWB = Res("wB")
        QB = AR.alloc(S * 2, BF16)
        KB = AR.alloc(S * 2, BF16)
        VB = AR.alloc(3 * 16 * 192 * 2, BF16).rearrange("p (b i n) -> p b i n", b=3, i=16)
        rQB, rKB, rVB = Res("QB"), Res("KB"), Res("VB")
        VT = AR.alloc(S * 2, BF16)
        rVT = Res("VT")
        TBs = [AR.alloc(3 * 384 * 4, F32).rearrange("p (b n) -> p b n", b=3) for _ in range(2)]
        rTB = [Res("tb0"), Res("tb1")]
        tmp = [AR.alloc(2048, F32) for _ in range(3)]
        rTmp = [Res("btmp%d" % i) for i in range(3)]
        PTb = [AR.alloc(1024, BF16) for _ in range(4)]
        rPTb = [Res("bpt%d" % i) for i in range(4)]
        lnl = AR.alloc(2048, F32)
        rLnl = Res("b_lnl")
        P.op("pool", "memset", VB[:, :, :, 64:128], 1.0, W=[rVB])
        sb_rr = [0]
        tmp_rr = [0]
        pt_rr = [0]
        ob_rr = [0]
        for hp in range(4):
            for q in range(2):
                P.dma("pool", wB[:, 4 * q:4 * q + 4, :], dr["wb"][hp, :, 4 * q:4 * q + 4, :], W=[rWB])
            for e in range(2):
                P.dma("sp", TBs[e], dr["tb"][2 * hp + e], W=[rTB[e]])
            for (c0, dstT, rDst) in ((0, QB, rQB), (128, KB, rKB)):
                for t in range(NB):
                    bk = 5 + (t % 2)
                    for kc in range(8):
                        P.op("pe", "matmul", BK[bk], wB[:, kc, c0:c0 + 128], hn[:, kc, blk(t)], start=(kc == 0), stop=(kc == 7),
                             R=[rWB, rHn[kc][t]], W=[rBK[bk]])
                    evac(dstT[:, blk(t)], BK[bk], R=[rBK[bk]], W=[rDst])
            for t in range(NB):
                bk = 5 + (t % 2)
                for kc in range(8):
                    P.op("pe", "matmul", BK[bk], wB[:, kc, 256:384], hn[:, kc, blk(t)], start=(kc == 0), stop=(kc == 7),
                         R=[rWB, rHn[kc][t]], W=[rBK[bk]])
                evac(VT[:, blk(t)], BK[bk], R=[rBK[bk]], W=[rVT])
            for bi, d in enumerate(B_DIL):
                nkt = (S // d) // 128
                for grp in range(4):
                    bk = 5 + (grp % 2)
                    bkb = BK[bk].bitcast(BF16)
                    for q in range(4):
                        ti = grp * 4 + q
                        r_, kt = ti // nkt, ti % nkt
                        s0 = kt * 128 * d + r_
                        tok = slice(s0, s0 + 127 * d + 1, d)
                        P.op("pe", "transpose", bkb[:, q * 128:(q + 1) * 128], VT[:, tok], identb[:],
                             R=[rVT, rIdentb], W=[rBK[bk]])
                    dst = VB[:, bi, grp * 4:grp * 4 + 4, :].rearrange("p i (a n) -> p i a n", a=3)[:, :, 0:3:2, :]
                    evac(dst, bkb[:, 0:512].rearrange("p (i a n) -> p i a n", i=4, a=2), R=[rBK[bk]], W=[rVB])
            flat = []
            for e in range(2):
                for c in range(NB):
                    ob = 6 + (ob_rr[0] % 2)
                    ob_rr[0] += 1
                    jobs = []
                    for di, delta in enumerate((128, 0, -128)):
                        regs = []
                        for qtl in range(4):
                            kt = 4 * c + qtl + delta // 128
                            if 0 <= kt < 16:
                                regs.append((qtl * 128, 128, kt, slice(kt * 128, kt * 128 + 128),
                                             slice(c * 512 + qtl * 128, c * 512 + qtl * 128 + 128),
                                             slice(qtl * 128, qtl * 128 + 128)))
                        if regs:
                            jobs.append((0, regs, slice(128 - delta, 256 - delta), 128))
                    for di, delta in enumerate((128, 0, -128)):
                        kt = c + delta // 128
                        if 0 <= kt < 4:
                            regs = []
                            for r_ in range(4):
                                ks = kt * 512 + r_
                                qs = c * 512 + r_
                                regs.append((r_ * 128, 128, r_ * 4 + kt, slice(ks, ks + 127 * 4 + 1, 4),
                                             slice(qs, qs + 127 * 4 + 1, 4), slice(r_, r_ + 127 * 4 + 1, 4)))
                            jobs.append((1, regs, slice(128 - delta, 256 - delta), 128))
                    regs = []
                    for r_ in range(16):
                        qs = c * 512 + r_
                        regs.append((r_ * 32, 32, r_, slice(r_, r_ + 127 * 16 + 1, 16),
                                     slice(qs, qs + 31 * 16 + 1, 16), slice(r_, r_ + 31 * 16 + 1, 16)))
                    jobs.append((2, regs, slice(128 + c * 32, 128 + c * 32 + 32), 32))
                    for ji, jb in enumerate(jobs):
                        flat.append((jb, ob, e, ji == 0, ji == len(jobs) - 1, c))
            nj = len(flat)
            LA = 3

            def b_qk(j):
                (bi, regs, mcols, w), ob, e, isfirst, islast, c = flat[j]
                rows = slice(e * 64, (e + 1) * 64)
                sb_ = j % 5
                for (col0, w_, ti, ksl, qsl, osl) in regs:
                    P.op("pe", "matmul", BK[sb_][:, col0:col0 + w_], KB[rows, ksl], QB[rows, qsl],
                         start=True, stop=True, skip_group_check=True, R=[rKB, rQB], W=[rBK[sb_]])

            def b_mid(j):
                (bi, regs, mcols, w), ob, e, isfirst, islast, c = flat[j]
                sb_ = j % 5
                c_lo = regs[0][0]
                c_hi = regs[-1][0] + w
                nreg = len(regs)
                tm, rTm = tmp[j % 3], rTmp[j % 3]
                P.op("dve", "scalar_tensor_tensor",
                     out=tm[:, c_lo:c_hi].rearrange("p (a n) -> p a n", a=nreg),
                     in0=BK[sb_][:, c_lo:c_hi].rearrange("p (a n) -> p a n", a=nreg),
                     scalar=0.125, in1=bcast_mid(TBs[e][:, bi, mcols], nreg),
                     op0=ALU.mult, op1=ALU.add, R=[rBK[sb_], rTB[e]], W=[rTm])
                P.op("act", "activation", out=PTb[j % 4][:, c_lo:c_hi], in_=tm[:, c_lo:c_hi], func=AF.Exp,
                     R=[rTm], W=[rPTb[j % 4]])

            def b_pv(j):
                (bi, regs, mcols, w), ob, e, isfirst, islast, c = flat[j]
                vsl = slice(0, 128) if e == 0 else slice(64, 192)
                orow = slice(0, 64) if e == 0 else slice(64, 128)
                lrow = slice(64, 128) if e == 0 else slice(0, 64)
                pt = PTb[j % 4]
                for ri, (col0, w_, ti, ksl, qsl, osl) in enumerate(regs):
                    P.op("pe", "matmul", BK[ob][:, osl], VB[:, bi, ti, vsl], pt[:, col0:col0 + w_],
                         start=(isfirst and ri == 0), stop=(islast and ri == len(regs) - 1), skip_group_check=True,
                         R=[rVB, rPTb[j % 4]], W=[rBK[ob]])
                if islast:
                    rr = c % 2
                    P.op("act", "activation", out=lnl[lrow, :], in_=BK[ob][lrow, :], func=AF.Ln, R=[rBK[ob]], W=[rLnl])
                    P.op("act", "activation", out=rec[rr][1][lrow, :], in_=lnl[lrow, :], func=AF.Exp, scale=-1.0,
                         R=[rLnl], W=[rRec[rr][1]])
                    P.op("dve", "tensor_tensor", out=OT[orow, 4 + hp, blk(c)], in0=BK[ob][orow, :], in1=rec[rr][1][lrow, :],
                         op=ALU.mult, R=[rBK[ob], rRec[rr][1]], W=[rOT[4 + hp][c]])

            for j in range(min(LA, nj)):
                b_qk(j)
            for j in range(nj):
                b_mid(j)
                if j + LA < nj:
                    b_qk(j + LA)
                b_pv(j)
        P.barrier()
        AR.release(m1)
        out_proj("wo0", OT, rOT)
        AR.release(m0)

    def layer1_mix():
        m0 = AR.mark()
        PT2 = [AR.alloc(2048, BF16) for _ in range(3)]
        rPT2 = [Res("c_pt2_%d" % i) for i in range(3)]
        rec = [(AR.alloc(2048, F32), AR.alloc(2048, F32)) for _ in range(2)]
        rRec = [(Res("c_recl%d" % i), Res("c_recr%d" % i)) for i in range(2)]
        cqn = AR.alloc(2 * S * 2, BF16).rearrange("p (c n) -> p c n", c=2)
        ckvn = AR.alloc(S * 2, BF16).rearrange("p (c n) -> p c n", c=1)
        krT = AR.alloc(S * 2, BF16)
        rCqn = [[Res("cqn%d_%d" % (c, t)) for t in range(NB)] for c in range(2)]
        rCkvn = [[Res("ckvn_%d" % t) for t in range(NB)]]
        rKr = Res("krT")
        ropeC = AR.alloc(2 * S * 4, F32).rearrange("p (a n) -> p a n", a=2)
        rRope = Res("ropeC")
        P.dma("sp", ropeC[:, 0, :], dr["ropeC"][0], W=[rRope])
        P.dma("sp", ropeC[:, 1, :], dr["ropeC"][1], W=[rRope])
        wuq = AR.alloc(2 * 16 * 128 * 2, BF16).rearrange("p (c h n) -> p c h n", c=2, h=16)
        wukv = AR.alloc(2048 * 2, BF16)
        rWuq, rWukv = Res("wuq"), Res("wukv")
        for c in range(2):
            P.dma("pool", wuq[:, c], dr["wuq"][:, c], W=[rWuq])
        P.dma("pool", wukv, dr["wukv"], W=[rWukv])
        t1 = AR.alloc(2048, F32)
        t2 = AR.alloc(2048, F32)
        rT1, rT2 = Res("c_t1"), Res("c_t2")
        m1 = AR.mark()
        hn = AR.alloc(8 * S * 2, BF16).rearrange("p (c n) -> p c n", c=8)
        rHn = [[Res("c_hn%d_%d" % (c, t)) for t in range(NB)] for c in range(8)]
        wdn = AR.alloc(8 * 448 * 2, BF16).rearrange("p (c n) -> p c n", c=8)
        rWdn = Res("wdn")
        for q in range(2):
            P.dma("pool", wdn[:, 4 * q:4 * q + 4, :], dr["wdn"][:, 4 * q:4 * q + 4, :], W=[rWdn])
        cblk = [AR.alloc(3 * 2048, F32).rearrange("p (c n) -> p c n", c=3) for _ in range(2)]
        rCb = [[Res("cb%d_%d" % (i, c)) for c in range(3)] for i in range(2)]
        scratch = norm_scratch()
        rmsnorm_fm(hT, rH, 8, 8, 0, hn, rHn, scratch)
        rr_ = slice(64, 96)
        for t in range(NB):
            cb = cblk[t % 2]
            rC = rCb[t % 2]
            for ci in range(3):
                c0 = ci * 128
                bk = 6 + (ci % 2)
                for kc in range(8):
                    P.op("pe", "matmul", BK[bk], wdn[:, kc, c0:c0 + 128], hn[:, kc, blk(t)], start=(kc == 0), stop=(kc == 7),
                         R=[rWdn, rHn[kc][t]], W=[rBK[bk]])
                evac(cb[:, ci, :], BK[bk], R=[rBK[bk]], W=[rC[ci]])
            for (c0, bk) in ((384, 2), (416, 3)):
                for kc in range(8):
                    P.op("pe", "matmul", BK[bk][64:96, :], wdn[:, kc, c0:c0 + 32], hn[:, kc, blk(t)],
                         start=(kc == 0), stop=(kc == 7), R=[rWdn, rHn[kc][t]], W=[rBK[bk]])
            P.op("dve", "tensor_tensor", out=t1[rr_, :], in0=BK[2][rr_, :], in1=ropeC[rr_, 0, blk(t)], op=ALU.mult,
                 R=[rBK[2], rRope], W=[rT1])
            P.op("dve", "tensor_tensor", out=t2[rr_, :], in0=BK[3][rr_, :], in1=ropeC[rr_, 1, blk(t)], op=ALU.mult,
                 R=[rBK[3], rRope], W=[rT2])
            P.op("pool", "tensor_tensor", out=krT[rr_, blk(t)], in0=t1[rr_, :], in1=t2[rr_, :], op=ALU.add,
                 R=[rT1, rT2], W=[rKr])
            rms_block(t, [cb[:, 0, :], cb[:, 1, :]], [rC[0], rC[1]], 36, 2,
                      [cqn[:, 0, blk(t)], cqn[:, 1, blk(t)]], [rCqn[0][t], rCqn[1][t]], scratch)
            rms_block(t, [cb[:, 2, :]], [rC[2]], 38, 3, [ckvn[:, 0, blk(t)]], [rCkvn[0][t]], scratch)
        P.barrier()
        AR.release(m1)
        OT = AR.alloc(8 * S * 2, BF16).rearrange("p (c n) -> p c n", c=8)
        rOT = [[Res("c_ot%d_%d" % (c, t)) for t in range(NB)] for c in range(8)]
        m2 = AR.mark()
        sets = []
        for si in range(2):
            QT = AR.alloc(2 * S * 2, BF16).rearrange("p (h n) -> p h n", h=2)
            KT = AR.alloc(2 * S * 2, BF16).rearrange("p (h n) -> p h n", h=2)
            VC = AR.alloc(16 * 192 * 2, BF16).rearrange("p (i n) -> p i n", i=16)
            rQT = [Res("QT%d_%d" % (si, i)) for i in range(2)]
            rKT = [Res("KT%d_%d" % (si, i)) for i in range(2)]
            rVC = Res("VC%d" % si)
            P.op("pool", "memset", VC[:, :, 64:128], 1.0, W=[rVC])
            sets.append((QT, KT, VC, rQT, rKT, rVC))
        t1s = [t1, t1]
        t2s = [t2, t2]
        rT1s = [rT1, rT1]
        rT2s = [rT2, rT2]
        cscale = 96.0 ** -0.5
        ucnt = [0]

        def proj_group(grp, si):
            QT, KT, VC, rQT, rKT, rVC = sets[si]
            for hl in range(2):
                h = 2 * grp + hl
                for t in range(NB):
                    u = ucnt[0] % 2
                    ucnt[0] += 1
                    for kc in range(2):
                        P.op("pe", "matmul", BK[4][0:96, :], wuq[:, kc, h, 0:96], cqn[:, kc, blk(t)], start=(kc == 0), stop=(kc == 1),
                             R=[rWuq, rCqn[kc][t]], W=[rBK[4]])
                    for kc in range(2):
                        P.op("pe", "matmul", BK[5][64:96, :], wuq[:, kc, h, 96:128], cqn[:, kc, blk(t)], start=(kc == 0), stop=(kc == 1),
                             R=[rWuq, rCqn[kc][t]], W=[rBK[5]])
                    P.op("pe", "matmul", BK[5][0:64, :], wukv[:, h * 64:(h + 1) * 64], ckvn[:, 0, blk(t)], start=True, stop=True,
                         R=[rWukv, rCkvn[0][t]], W=[rBK[5]])
                    P.op("dve", "tensor_copy", out=QT[0:64, hl, blk(t)], in_=BK[4][0:64, :], R=[rBK[4]], W=[rQT[hl]])
                    P.op("dve", "tensor_tensor", out=t1s[u][rr_, :], in0=BK[4][rr_, :], in1=ropeC[rr_, 0, blk(t)], op=ALU.mult,
                         R=[rBK[4], rRope], W=[rT1s[u]])
                    P.op("dve", "tensor_tensor", out=t2s[u][rr_, :], in0=BK[5][rr_, :], in1=ropeC[rr_, 1, blk(t)], op=ALU.mult,
                         R=[rBK[5], rRope], W=[rT2s[u]])
                    P.op("dve", "tensor_copy", out=KT[0:64, hl, blk(t)], in_=BK[5][0:64, :], R=[rBK[5]], W=[rKT[hl]])
                    P.op("pool", "tensor_tensor", out=QT[rr_, hl, blk(t)], in0=t1s[u][rr_, :], in1=t2s[u][rr_, :], op=ALU.add,
                         R=[rT1s[u], rT2s[u]], W=[rQT[hl]])
                    yield
                P.op("dve", "tensor_copy", out=KT[rr_, hl, :], in_=krT[rr_, :], R=[rKr], W=[rKT[hl]])
                yield
            for i4 in range(4):
                for q in range(4):
                    i = i4 * 4 + q
                    P.op("pe", "matmul", BK[4][:, q * 128:(q + 1) * 128], ckvn[:, 0, i * 128:(i + 1) * 128],
                         wukv[:, 1024 + grp * 128:1024 + (grp + 1) * 128], start=True, stop=True,
                         skip_group_check=True, R=[rWukv, rCkvn[0][i // 4]], W=[rBK[4]])
                dst = VC[:, i4 * 4:i4 * 4 + 4, :].rearrange("p i (b n) -> p i b n", b=3)[:, :, 0:3:2, :]
                P.op("dve", "tensor_copy", out=dst, in_=BK[4].rearrange("p (i b n) -> p i b n", i=4, b=2),
                     R=[rBK[4]], W=[rVC])
                yield

        def attn_steps(grp, si):
            QT, KT, VC, rQT, rKT, rVC = sets[si]
            steps = []
            for hl in range(2):
                h = 2 * grp + hl
                e = hl
                for t in range(NB):
                    ob = 6 + (t % 2)
                    for i2 in range(8):
                        lanes = []
                        for l in range(2):
                            i = 2 * i2 + l
                            v = VC[:, i, 0:128] if e == 0 else VC[:, i, 64:192]
                            lanes.append((KT[0:96, hl, i * 128:(i + 1) * 128], QT[0:96, hl, blk(t)], rKT[hl], rQT[hl], v, rVC,
                                          ob, i == 0, i == 15))
                        post = None
                        if i2 == 7:
                            def post(h=h, e=e, t=t, ob=ob):
                                normalize(ob, e, OT[:, h // 2, blk(t)], rOT[h // 2][t], rec[t % 2], rRec[t % 2], on_act=False)
                        steps.append((lanes, post))
            return steps

        import os
        MODE = os.environ.get("MLA_MODE", "bg")
        for _ in proj_group(0, 0):
            pass
        for grp in range(8):
            bg = proj_group(grp + 1, (grp + 1) % 2) if grp + 1 < 8 else None
            if MODE == "bg":
                run_attention(attn_steps(grp, grp % 2), cscale, PT2, rPT2, (0, 2), bg=bg, bg_every=4)
            elif MODE == "seq":
                run_attention(attn_steps(grp, grp % 2), cscale, PT2, rPT2, (0, 2))
                if bg is not None:
                    for _ in bg:
                        pass
            elif MODE == "projonly":
                if bg is not None:
                    for _ in bg:
                        pass
        P.barrier()
        AR.release(m2)
        out_proj("wo1", OT, rOT)
        AR.release(m0)

    def final_store():
        m = AR.mark()
        yT = AR.alloc(8 * S * 4, F32).rearrange("p (c n) -> p c n", c=8)
        rY = [[Res("y%d_%d" % (c, t)) for t in range(NB)] for c in range(8)]
        scratch = norm_scratch()
        rmsnorm_fm(hT, rH, 8, 39, 0, yT, rY, scratch)
        ys = [AR.alloc(4096, F32) for _ in range(3)]
        rYs = [Res("ys%d" % i) for i in range(3)]
        rOut = [Res("out%d" % i) for i in range(3)]
        for i in range(16):
            b = i % 3
            t = i // 4
            for half in range(2):
                bk = (2 * i + half) % 8
                for cc in range(4):
                    c = half * 4 + cc
                    P.op("pe", "transpose", BK[bk][:, cc * 128:(cc + 1) * 128], yT[:, c, i * 128:(i + 1) * 128],
                         identf[:], R=[rY[c][t], rIdent], W=[rBK[bk]])
                evac(ys[b][:, half * 512:(half + 1) * 512], BK[bk], R=[rBK[bk]], W=[rYs[b]])
            P.dma("sp", out_d[i * 128:(i + 1) * 128, :], ys[b], R=[rYs[b]], W=[rOut[b]])
        P.final_wait("sp", rOut)
        AR.release(m)

    def dump_h():
        rOut = Res("out")
        for c in range(8):
            P.dma("sp", out_d[:, c, :], hT[:, c, :], R=[rH[c][t] for t in range(NB)], W=[rOut])
        P.final_wait("sp", [rOut])

    phase_load()
    stages = ["x", "l0mix", "l0mlp", "l1mix", "l1mlp", "full"]
    si = stages.index(stage)
    if si >= 1:
        layer0_mix()
    if si >= 2:
        mlp(0, 16)
    if si >= 3:
        layer1_mix()
    if si >= 4:
        mlp(1, 24)
    if stage == "full":
        final_store()
    else:
        dump_h()
    P.emit(st)
    st.close()
    print('ops', {e: len(v) for e, v in P.ops.items()}, 'waits', P.nwaits)
    return nc


_CACHE = {}


def run(inputs, stage="full", ncores=8):
    if stage not in _CACHE:
        _CACHE[stage] = build(stage)
    nc = _CACHE[stage]
    x = np.asarray(inputs["x"], dtype=np.float32)
    sh = _pack_shared(inputs)
    in_maps = []
    for b in range(ncores):
        m = {"x": np.ascontiguousarray(x[b])}
        m.update(sh)
        in_maps.append(m)
    res = run_bass_kernel_spmd(nc, in_maps, core_ids=list(range(ncores)))
    return [r["out"] for r in res.results]


def kernel(**inputs):
    outs = run(inputs, "full")
    return np.stack(outs, axis=0).astype(np.float32)
```
